# Optimizing a Trainium2 kernel written in Bass

```python
import jax, jax.numpy as jnp
from jax import lax
import numpy as np

D_MODEL = 2048
BATCH = 4
SEQ = 8192
DEPTH = 1

CHUNK = 64
D_FF = 5632
FFN_RES = 0.5
RMS_EPS = 1e-6

RWKV_WIDTH = D_MODEL // 2
RWKV_HEAD = 64
RWKV_HEADS = RWKV_WIDTH // RWKV_HEAD
DECAY_LORA = 64
AAA_LORA = 64
GATE_LORA = 160
GN_EPS = 64e-5

HGRN_WIDTH = D_MODEL - RWKV_WIDTH
HGRN_EXPAND = 128
HGRN_HEADS = HGRN_WIDTH // HGRN_EXPAND
HGRN_HEAD = HGRN_WIDTH // HGRN_HEADS

RWKV_COLS = 3 * RWKV_WIDTH + DECAY_LORA + AAA_LORA + GATE_LORA
HGRN_COLS = 4 * HGRN_WIDTH
IN_COLS = RWKV_COLS + HGRN_COLS

kernel_name = 'hybrid_rwkv7_hgrn2_macaron_layer'


def rms_norm(x, gain, eps=RMS_EPS):
    x32 = x.astype(jnp.float32)
    inv = lax.rsqrt(jnp.mean(x32 * x32, axis=-1, keepdims=True) + eps)
    return (x32 * inv).astype(x.dtype) * gain


def swiglu(h, w_gate, w_up, w_down):
    return (jax.nn.silu(h @ w_gate) * (h @ w_up)) @ w_down


def rwkv7_mix(p, mu, w0, w2, a0, a2, g2, k_k, k_a, r_k, ln_w, ln_b):
    B, T, _ = p.shape
    C, H, N = RWKV_WIDTH, RWKV_HEADS, RWKV_HEAD
    f32 = jnp.float32
    p_prev = jnp.pad(p, ((0, 0), (1, 0), (0, 0)))[:, :T]
    p = p + (p_prev - p) * mu
    r, k, v, xw, xa, xg = jnp.split(
        p, [C, 2 * C, 3 * C, 3 * C + DECAY_LORA, 3 * C + DECAY_LORA + AAA_LORA], axis=-1)
    w = -jax.nn.softplus(-(w0 + jnp.tanh(xw.astype(f32)) @ w2.astype(f32))) - 0.5
    decay = jnp.exp(-jnp.exp(w))
    a = jax.nn.sigmoid((a0 + xa @ a2).astype(f32))
    g = jax.nn.sigmoid(xg) @ g2
    heads = lambda t: t.astype(f32).reshape(B, T, H, N)
    r, k, v, a, decay = heads(r), heads(k), heads(v), heads(a), heads(decay)
    kk = k * k_k.astype(f32).reshape(H, N)
    kk = kk / jnp.maximum(jnp.sqrt(jnp.sum(kk * kk, axis=-1, keepdims=True)), 1e-12)
    k = k * (1.0 + (a - 1.0) * k_a.astype(f32).reshape(H, N))

    def step(S, inp):
        r_t, w_t, k_t, v_t, kk_t, a_t = inp
        sa = jnp.einsum('bhvk,bhk->bhv', S, -kk_t)
        S = (S * w_t[:, :, None, :] + sa[..., None] * (kk_t * a_t)[:, :, None, :]
             + v_t[..., None] * k_t[:, :, None, :])
        return S, jnp.einsum('bhvk,bhk->bhv', S, r_t)

    seq_first = lambda t: jnp.moveaxis(t, 1, 0)
    S0 = jnp.zeros((B, H, N, N), f32)
    _, y = lax.scan(step, S0, (seq_first(r), seq_first(decay), seq_first(k),
                               seq_first(v), seq_first(kk), seq_first(a)))
    y = jnp.moveaxis(y, 0, 1)
    mean = jnp.mean(y, axis=-1, keepdims=True)
    var = jnp.var(y, axis=-1, keepdims=True)
    y = ((y - mean) * lax.rsqrt(var + GN_EPS)).reshape(B, T, C) * ln_w + ln_b
    bonus = jnp.sum(r * k * r_k.astype(f32), axis=-1, keepdims=True) * v
    y = (y + bonus.reshape(B, T, C)) * g
    return y.astype(p.dtype)


def hgrn2_mix(p, lb, norm_w):
    B, T, _ = p.shape
    H, Dh = HGRN_HEADS, HGRN_HEAD
    f32 = jnp.float32
    q, f, i, gate = jnp.split(p, 4, axis=-1)
    q = jax.nn.silu(q.astype(f32))
    forget = lb + (1.0 - lb) * jax.nn.sigmoid(f.astype(f32))
    log_f = jnp.log(forget)
    k = 1.0 - forget
    n_chunks = T // CHUNK
    chunks = lambda t: t.astype(f32).reshape(B, n_chunks, CHUNK, H, Dh).transpose(1, 0, 3, 2, 4)
    causal = jnp.tril(jnp.ones((CHUNK, CHUNK), dtype=bool))

    def step(S, inp):
        q_c, k_c, v_c, lf_c = inp
        G = jnp.cumsum(lf_c, axis=2)
        o_inter = jnp.einsum('bhik,bhkv->bhiv', q_c * jnp.exp(G), S)
        diff = G[:, :, :, None, :] - G[:, :, None, :, :]
        dec = jnp.exp(jnp.where(causal[:, :, None], diff, -jnp.inf))
        A = jnp.einsum('bhik,bhijk->bhij', q_c, dec * k_c[:, :, None, :, :])
        o = o_inter + jnp.einsum('bhij,bhjv->bhiv', A, v_c)
        G_last = G[:, :, -1:, :]
        S = (S * jnp.exp(G_last)[:, :, 0, :, None]
             + jnp.einsum('bhjk,bhjv->bhkv', k_c * jnp.exp(G_last - G), v_c))
        return S, o

    S0 = jnp.zeros((B, H, Dh, Dh), f32)
    _, o = lax.scan(step, S0, (chunks(q), chunks(k), chunks(i), chunks(log_f)))
    o = o.transpose(1, 0, 3, 2, 4).reshape(B, T, H, Dh)
    o = o * lax.rsqrt(jnp.mean(o * o, axis=-1, keepdims=True) + RMS_EPS)
    o = o.reshape(B, T, HGRN_WIDTH) * norm_w * jax.nn.silu(gate.astype(f32))
    return o.astype(p.dtype)


def setup_inputs(seed: int = 0) -> dict:
    key = jax.random.key(seed)
    ks = jax.random.split(key, 32)
    f32 = jnp.float32
    L, D, F = DEPTH, D_MODEL, D_FF
    C, Ch = RWKV_WIDTH, HGRN_WIDTH
    nrm = lambda k, shape, scale: scale * jax.random.normal(k, shape, f32)
    gain = lambda k, shape: 1.0 + 0.02 * jax.random.normal(k, shape, f32)
    return {
        'x': nrm(ks[0], (BATCH, SEQ, D), 1.0),
        'ffn1_norm': gain(ks[1], (L, D)),
        'ffn1_w_gate': nrm(ks[2], (L, D, F), D ** -0.5),
        'ffn1_w_up': nrm(ks[3], (L, D, F), D ** -0.5),
        'ffn1_w_down': nrm(ks[4], (L, F, D), F ** -0.5),
        'mix_norm': gain(ks[5], (L, D)),
        'w_in': nrm(ks[6], (L, D, IN_COLS), D ** -0.5),
        'rwkv_mu': jax.random.uniform(ks[7], (L, RWKV_COLS), f32),
        'rwkv_w0': jax.random.uniform(ks[8], (L, C), f32, -6.0, -1.0),
        'rwkv_w2': nrm(ks[9], (L, DECAY_LORA, C), 0.1),
        'rwkv_a0': nrm(ks[10], (L, C), 0.1),
        'rwkv_a2': nrm(ks[11], (L, AAA_LORA, C), 0.1),
        'rwkv_g2': nrm(ks[12], (L, GATE_LORA, C), GATE_LORA ** -0.5),
        'rwkv_k_k': 0.85 + nrm(ks[13], (L, C), 0.05),
        'rwkv_k_a': 1.0 + nrm(ks[14], (L, C), 0.05),
        'rwkv_r_k': nrm(ks[15], (L, RWKV_HEADS, RWKV_HEAD), 0.1),
        'rwkv_ln_w': gain(ks[16], (L, C)),
        'rwkv_ln_b': nrm(ks[17], (L, C), 0.02),
        'hgrn_lb_logits': nrm(ks[18], (L + 1, Ch), 0.5),
        'hgrn_norm': gain(ks[19], (L, Ch)),
        'w_out': nrm(ks[20], (L, D, D), D ** -0.5),
        'ffn2_norm': gain(ks[21], (L, D)),
        'ffn2_w_gate': nrm(ks[22], (L, D, F), D ** -0.5),
        'ffn2_w_up': nrm(ks[23], (L, D, F), D ** -0.5),
        'ffn2_w_down': nrm(ks[24], (L, F, D), F ** -0.5),
        'final_norm': gain(ks[25], (D,)),
    }


def reference(x, ffn1_norm, ffn1_w_gate, ffn1_w_up, ffn1_w_down, mix_norm, w_in,
              rwkv_mu, rwkv_w0, rwkv_w2, rwkv_a0, rwkv_a2, rwkv_g2, rwkv_k_k, rwkv_k_a,
              rwkv_r_k, rwkv_ln_w, rwkv_ln_b, hgrn_lb_logits, hgrn_norm, w_out,
              ffn2_norm, ffn2_w_gate, ffn2_w_up, ffn2_w_down, final_norm):
    lb_all = jnp.cumsum(jax.nn.softmax(hgrn_lb_logits.astype(jnp.float32), axis=0), axis=0)
    for l in range(DEPTH):
        h = rms_norm(x, ffn1_norm[l])
        x = x + FFN_RES * swiglu(h, ffn1_w_gate[l], ffn1_w_up[l], ffn1_w_down[l])
        h = rms_norm(x, mix_norm[l])
        proj = h @ w_in[l]
        y_rwkv = rwkv7_mix(proj[..., :RWKV_COLS], rwkv_mu[l], rwkv_w0[l], rwkv_w2[l],
                           rwkv_a0[l], rwkv_a2[l], rwkv_g2[l], rwkv_k_k[l], rwkv_k_a[l],
                           rwkv_r_k[l], rwkv_ln_w[l], rwkv_ln_b[l])
        y_hgrn = hgrn2_mix(proj[..., RWKV_COLS:], lb_all[l], hgrn_norm[l])
        x = x + jnp.concatenate([y_rwkv, y_hgrn], axis=-1) @ w_out[l]
        h = rms_norm(x, ffn2_norm[l])
        x = x + FFN_RES * swiglu(h, ffn2_w_gate[l], ffn2_w_up[l], ffn2_w_down[l])
    return rms_norm(x, final_norm)
```

```python
from contextlib import ExitStack
import numpy as np
import concourse.bass as bass
import concourse.mybir as mybir
from concourse.bass_utils import run_bass_kernel_spmd

F32 = mybir.dt.float32
BF16 = mybir.dt.bfloat16
AF = mybir.ActivationFunctionType
ALU = mybir.AluOpType

NT = 512
CH = 64
NCH = NT // CH
NSLOT = 4
SLOT = 5632
RMS_EPS = 1e-6
GN_EPS = 64e-5
WSCALE = -0.6065306597126334


class Cfg:
    def __init__(self, D=2048, F=5632, ntp=8, ntm=8, stage=99):
        self.D, self.F, self.ntp, self.ntm, self.stage = D, F, ntp, ntm, stage
        self.KD = D // 128
        self.KF = F // 128
        self.C = D // 2
        self.KR = self.C // 128
        self.NSF = F // 256
        self.NSO = D // 256


class B:
    __slots__ = ("ap", "key")

    def __init__(self, ap, key):
        self.ap, self.key = ap, key

    def __getitem__(self, idx):
        return B(self.ap[idx], self.key)

    def v(self, ap):
        return B(ap, self.key)


class Prog:
    def __init__(self, nc, st, dry):
        self.nc, self.dry = nc, dry
        self.eng = {"pe": nc.tensor, "act": nc.scalar, "dve": nc.vector, "pool": nc.gpsimd, "sp": nc.sync}
        self.semobj = {}
        self.cnt = {k: 0 for k in self.eng}
        self.seen = {k: {} for k in self.eng}
        self.nd = 12
        self.dcnt = [0] * self.nd
        self.dnext = 0
        if not dry:
            for k in self.eng:
                self.semobj["s_" + k] = st.enter_context(nc.semaphore("s_" + k))
            for i in range(self.nd):
                self.semobj["d%d" % i] = st.enter_context(nc.semaphore("d%d" % i))
            self.semobj["cc"] = st.enter_context(nc.semaphore("cc"))
        self.ncc = 0
        self.bufs = {}
        self.ninstr = 0
        self.pe_open = False

    def _wait(self, e, sname, val):
        if e == "pe" and sname == "s_pe":
            return
        if self.seen[e].get(sname, 0) >= val:
            return
        self.eng[e].wait_ge(self.semobj[sname], val)
        self.seen[e][sname] = val

    def _deps(self, e, reads, writes):
        for k in reads:
            b = self.bufs.get(k)
            if b and b[0]:
                self._wait(e, b[0][0], b[0][1])
        for k in writes:
            b = self.bufs.get(k)
            if b:
                if b[0]:
                    self._wait(e, b[0][0], b[0][1])
                for s, v in b[1].items():
                    self._wait(e, s, v)

    def _mark(self, tok, reads, writes):
        for k in reads:
            b = self.bufs.get(k)
            if b is None:
                b = self.bufs[k] = [None, {}]
            if b[1].get(tok[0], 0) < tok[1]:
                b[1][tok[0]] = tok[1]
        for k in writes:
            self.bufs[k] = [tok, {}]

    def op(self, e, fn, reads=(), writes=(), inc=True):
        self.ninstr += 1
        if self.dry:
            return None
        if e != "pe":
            assert not self.pe_open, "other-engine op emitted inside an open PE group"
            extra = [k for k in reads if k.startswith("ps")]
            if extra:
                writes = list(writes) + extra
        self._deps(e, reads, writes)
        ins = fn(self.eng[e])
        if inc:
            self.cnt[e] += 1
            ins.then_inc(self.semobj["s_" + e], 1)
            tok = ("s_" + e, self.cnt[e])
            if e == "pe":
                self.pe_open = False
        else:
            assert e == "pe"
            tok = ("s_" + e, self.cnt[e] + 1)
            self.pe_open = True
        self._mark(tok, reads, writes)
        return tok

    def dma(self, q, out, in_, reads=(), writes=()):
        self.ninstr += 1
        if self.dry:
            return None
        assert not self.pe_open
        i = self.dnext
        self.dnext = (self.dnext + 1) % self.nd
        if self.dcnt[i] > 0:
            self._wait(q, "d%d" % i, 16 * self.dcnt[i])
        self._deps(q, reads, writes)
        ins = self.eng[q].dma_start(out=out, in_=in_)
        self.dcnt[i] += 1
        ins.then_inc(self.semobj["d%d" % i], 16)
        tok = ("d%d" % i, 16 * self.dcnt[i])
        self._mark(tok, reads, writes)
        return tok

    def collective(self, in_t, out_t, groups, reads, writes):
        self.ninstr += 1
        if self.dry:
            return None
        assert not self.pe_open
        self._deps("pool", reads, writes)
        ins = self.eng["pool"].collective_compute("AllGather", ALU.bypass, replica_groups=groups,
                                                  ins=[in_t.ap().opt()], outs=[out_t.ap().opt()])
        self.ncc += 1
        ins.then_inc(self.semobj["cc"], 1)
        tok = ("cc", self.ncc)
        self._mark(tok, reads, writes)
        return tok

    def transfer(self, from_keys, to_keys):
        if self.dry:
            return
        acc = {}
        for k in from_keys:
            b = self.bufs.get(k)
            if not b:
                continue
            if b[0]:
                acc[b[0][0]] = max(acc.get(b[0][0], 0), b[0][1])
            for s, v in b[1].items():
                acc[s] = max(acc.get(s, 0), v)
        for k in to_keys:
            b = self.bufs.get(k)
            if b is None:
                b = self.bufs[k] = [None, {}]
            for s, v in acc.items():
                if b[1].get(s, 0) < v:
                    b[1][s] = v

    def wait_all(self, e, toks):
        if self.dry:
            return
        for t in toks:
            if t:
                self._wait(e, t[0], t[1])


class WStream:
    def __init__(self, P, slots):
        self.P, self.slots = P, slots
        self.reqs = []
        self.pos = 0
        self.loaded = 0

    def get(self, dram_ap, nelem):
        if self.P.dry:
            self.reqs.append((dram_ap, nelem))
            return self.slots[0]
        idx = self.pos
        self.pos += 1
        lim = min(len(self.reqs), idx + NSLOT - 1)
        while self.loaded < lim:
            j = self.loaded
            ap, n = self.reqs[j]
            s = self.slots[j % NSLOT]
            self.P.dma("pool", s.ap[:, 0:n], ap, writes=[s.key])
            self.loaded += 1
        return self.slots[idx % NSLOT]


def build_program(cfg, debug=False):
    nc = bass.Bass("TRN2", target_bir_lowering=False)
    D, F, KD, KF = cfg.D, cfg.F, cfg.KD, cfg.KF
    C, KR = cfg.C // 2, cfg.KR // 2
    ntm = cfg.ntm
    NTB = 2 * ntm
    XW = KD * NT
    YW = 2 * KR * NT

    def din(name, shape):
        return nc.dram_tensor(name, shape, F32, kind="ExternalInput").ap()

    xmain = din("xmain", [ntm, 128, XW])
    outT = nc.dram_tensor("outT", [ntm, 128, XW], F32, kind="ExternalOutput").ap()
    wg = [din("wg%d" % i, [cfg.NSF, 128, KD * 256]) for i in (1, 2)]
    wu = [din("wu%d" % i, [cfg.NSF, 128, KD * 256]) for i in (1, 2)]
    wd = [din("wd%d" % i, [KD, 128, KF * 128]) for i in (1, 2)]
    winx = din("winx", [1, 128, KD * 288])
    winr = din("winr", [KR, 128, KD * 128])
    winkv = din("winkv", [KR, 128, KD * 256])
    winfi = din("winfi", [KR, 128, KD * 256])
    winqg = din("winqg", [KR, 128, KD * 256])
    wout = din("wout", [cfg.NSO, 128, KD * 256])
    X1 = nc.dram_tensor("X1s", [ntm * 128, XW], F32)
    HBin = [nc.dram_tensor("HBin%d" % i, [128, XW], BF16) for i in range(ntm)]
    HBall = [nc.dram_tensor("HBall%d" % i, [256, XW], BF16) for i in range(ntm)]
    YBin = [nc.dram_tensor("YBin%d" % i, [128, YW], BF16) for i in range(NTB)]
    YBall = [nc.dram_tensor("YBall%d" % i, [256, YW], BF16) for i in range(NTB)]
    NSM = 4 * KD + 13 * KR + 6
    smalls_d = din("smalls", [128, NSM])
    w2_d = din("w2", [64, C])
    a2_d = din("a2", [64, C])
    g2_d = din("g2", [160, C])
    dbg = None
    if debug:
        dbg = nc.dram_tensor("dbg", [16, 128, NT], F32, kind="ExternalOutput").ap()

    for dry in (True, False):
        with ExitStack() as st:
            P = Prog(nc, st, dry)
            if dry:
                ws_prev = None

            pfx = "dry_" if dry else ""

            def sb(name, shape, dt=F32):
                return st.enter_context(nc.sbuf_tensor(pfx + name, shape, dt))

            XTt = sb("XT", [128, XW])
            HTt = sb("HT", [128, XW], BF16)
            YTt = sb("YT", [128, YW], BF16)
            ARW = max(KF * NT // 2, 16 * NT + 2 * (8 * NT // 2 + NT) + 16 * NT // 2)
            ARt = sb("ARENA", [128, ARW])
            WRt = [sb("WR%d" % i, [128, SLOT], BF16) for i in range(NSLOT)]
            XT = [B(XTt[:, k * NT:(k + 1) * NT], "XT%d" % k) for k in range(KD)]
            HT = [B(HTt[:, k * NT:(k + 1) * NT], "HT%d" % k) for k in range(KD)]
            YT = [B(YTt[:, k * NT:(k + 1) * NT], "YT%d" % k) for k in range(2 * KR)]
            HX = [B(XTt[:, 0:XW // 2].bitcast(BF16)[:, k * NT:(k + 1) * NT], "HX%d" % k) for k in range(KD)]
            HTB = [HT, HX]
            HTcur = [HT]
            ATv = ARt[:].bitcast(BF16)
            AT = [B(ATv[:, f * NT:(f + 1) * NT], "AT%d" % f) for f in range(KF)]
            WR = [B(WRt[i][:], "WR%d" % i) for i in range(NSLOT)]
            ws = WStream(P, WR)
            if not dry:
                ws.reqs = saved_reqs
            arena_off = [0]
            mix_keys = []

            def carve(name, words, dt=F32):
                o = arena_off[0]
                arena_off[0] += words
                assert arena_off[0] <= ARW, "arena overflow"
                ap = ARt[:, o:o + words]
                if dt == BF16:
                    ap = ap.bitcast(BF16)
                mix_keys.append(name)
                return B(ap, name)

            def f32t(name):
                return carve(name, NT)

            def b16t(name):
                return carve(name, NT // 2, BF16)

            SM = B(sb("SM", [128, NSM])[:], "SM")
            ONESB = B(sb("ONESB", [128, 128], BF16)[:], "ONESB")
            ONES32 = B(sb("ONES32", [128, 128])[:], "ONES32")
            BONES32 = B(sb("BONES32", [128, 128])[:], "BONES32")
            BONESB = B(sb("BONESB", [128, 128], BF16)[:], "BONESB")
            BONV = B(sb("BONV", [128, 128])[:], "BONV")
            CEN32 = B(sb("CEN32", [128, 128])[:], "CEN32")
            IDB = B(sb("IDB", [128, 128], BF16)[:], "IDB")
            IDF = B(sb("IDF", [128, 128])[:], "IDF")
            MU_S = B(sb("MU_S", [128, NT], BF16)[:], "MU_S")
            MU_I = B(sb("MU_I", [128, NT], BF16)[:], "MU_I")
            ML_S = B(sb("ML_S", [128, NT], BF16)[:], "ML_S")
            IDS = B(sb("IDS", [128, NT])[:], "IDS")
            IDSB = B(sb("IDSB", [128, NT], BF16)[:], "IDSB")
            RMASK = B(sb("RMASK", [128, NT])[:], "RMASK")
            TMPC = B(sb("TMPC", [128, NT])[:], "TMPC")
            W2B = B(sb("W2B", [64, C], BF16)[:], "W2B")
            A2B = B(sb("A2B", [64, C], BF16)[:], "A2B")
            G2A = B(sb("G2A", [128, C], BF16)[:], "G2A")
            G2B = B(sb("G2B", [32, C], BF16)[:], "G2B")
            NCC = 3 * KR + 4
            CARRY = B(sb("CARRY", [128, NCC])[:], "CARRY")
            DER = B(sb("DER", [128, 4 * KR])[:], "DER")
            OM = B(sb("OM", [128, NCC])[:], "OM")
            STR = [B(sb("STR%d" % h, [128, 64], BF16)[:], "STR%d" % h) for h in range(KR)]
            SG32 = [B(sb("SG32_%d" % h, [128, 128])[:], "SG32_%d" % h) for h in range(KR)]
            SGB = [B(sb("SGB_%d" % h, [128, 128], BF16)[:], "SGB_%d" % h) for h in range(KR)]
            TW = B(sb("TW", [64, NT], BF16)[:], "TW")
            XA = B(sb("XA", [64, NT], BF16)[:], "XA")
            SGA = B(sb("SGA", [128, NT], BF16)[:], "SGA")
            SGBb = B(sb("SGBb", [32, NT], BF16)[:], "SGBb")
            PF = [B(sb("PF%d" % i, [128, NT + 1])[:], "PF%d" % i) for i in range(2)]
            SQ = [B(sb("SQ%d" % i, [128, NT], BF16)[:], "SQ%d" % i) for i in range(2)]
            RS = B(sb("RS", [128, NT])[:], "RS")
            PSF = [B(st.enter_context(nc.psum_tensor(pfx + "psf%d" % i, [128, NT], F32))[:], "psf%d" % i) for i in range(6)]
            PSB = [B(st.enter_context(nc.psum_tensor(pfx + "psb%d" % i, [128, 2 * NT], BF16))[:], "psb%d" % i) for i in range(2)]
            rr = {"f": 0, "b": 0, "pf": 0, "sq": 0}

            def psf():
                rr["f"] = (rr["f"] + 1) % len(PSF)
                return PSF[rr["f"]]

            def psb():
                rr["b"] = (rr["b"] + 1) % len(PSB)
                return PSB[rr["b"]]

            def mm(out, lhsT, rhs, start=True, stop=True, inc=None):
                if inc is None:
                    inc = stop
                P.op("pe", lambda e: e.matmul(out.ap, lhsT.ap, rhs.ap, start=start, stop=stop),
                     reads=[lhsT.key, rhs.key], writes=[out.key], inc=inc)

            def tr(out, in_, ident, inc=True):
                P.op("pe", lambda e: e.transpose(out.ap, in_.ap, ident.ap), reads=[in_.key, ident.key], writes=[out.key], inc=inc)

            def act(out, in_, func, scale=None, bias=None):
                kw = {}
                rd = [in_.key]
                if scale is not None:
                    if isinstance(scale, B):
                        kw["scale"] = scale.ap
                        rd.append(scale.key)
                    else:
                        kw["scale"] = float(scale)
                if bias is not None:
                    if isinstance(bias, B):
                        kw["bias"] = bias.ap
                        rd.append(bias.key)
                    else:
                        kw["bias"] = float(bias)
                P.op("act", lambda e: e.activation(out=out.ap, in_=in_.ap, func=func, **kw), reads=rd, writes=[out.key])

            def tt(out, in0, in1, op, eng="dve"):
                P.op(eng, lambda e: e.tensor_tensor(out=out.ap, in0=in0.ap, in1=in1.ap, op=op),
                     reads=[in0.key, in1.key], writes=[out.key])

            def tsc(out, in0, s1, op0, s2=None, op1=None, eng="dve"):
                rd = [in0.key]
                a1 = s1.ap if isinstance(s1, B) else float(s1)
                if isinstance(s1, B):
                    rd.append(s1.key)
                if s2 is None:
                    P.op(eng, lambda e: e.tensor_scalar(out=out.ap, in0=in0.ap, scalar1=a1, scalar2=None, op0=op0),
                         reads=rd, writes=[out.key])
                else:
                    a2 = s2.ap if isinstance(s2, B) else float(s2)
                    if isinstance(s2, B):
                        rd.append(s2.key)
                    P.op(eng, lambda e: e.tensor_scalar(out=out.ap, in0=in0.ap, scalar1=a1, scalar2=a2, op0=op0, op1=op1),
                         reads=rd, writes=[out.key])

            def stt(out, in0, s, in1, op0, op1):
                rd = [in0.key, in1.key]
                a = s.ap if isinstance(s, B) else float(s)
                if isinstance(s, B):
                    rd.append(s.key)
                P.op("dve", lambda e: e.scalar_tensor_tensor(out=out.ap, in0=in0.ap, scalar=a, in1=in1.ap, op0=op0, op1=op1),
                     reads=rd, writes=[out.key])

            def cp(out, in_, eng="dve"):
                if eng == "act":
                    act(out, in_, AF.Identity)
                    return
                P.op(eng, lambda e: e.tensor_copy(out=out.ap, in_=in_.ap), reads=[in_.key], writes=[out.key])

            def recip(out, in_):
                P.op("dve", lambda e: e.reciprocal(out=out.ap, in_=in_.ap), reads=[in_.key], writes=[out.key])

            def memset(buf, val, eng="dve"):
                P.op(eng, lambda e: e.memset(buf.ap, val), writes=[buf.key])

            def scan(out, d0, d1):
                P.op("dve", lambda e: e.tensor_tensor_scan(out=out.ap, data0=d0.ap, data1=d1.ap, initial=0.0,
                                                           op0=ALU.mult, op1=ALU.add),
                     reads=[d0.key, d1.key], writes=[out.key])

            def asel(buf, rows, pattern, cmp, fill, base, cm):
                P.op("pool", lambda e: e.affine_select(out=buf.ap[rows], in_=buf.ap[rows], pattern=pattern, compare_op=cmp,
                                                       fill=fill, base=base, channel_multiplier=cm),
                     reads=[buf.key], writes=[buf.key])

            dbg_n = [0]

            def dump(buf):
                if debug and dbg_n[0] < 16:
                    if buf.ap.dtype != F32:
                        cp(TMPC.v(TMPC.ap[0:buf.ap.shape[0], 0:buf.ap.shape[1]]), buf)
                        src = TMPC.v(TMPC.ap[0:buf.ap.shape[0], 0:buf.ap.shape[1]])
                    else:
                        src = buf
                    t = P.dma("sp", dbg[dbg_n[0], 0:src.ap.shape[0], 0:src.ap.shape[1]], src.ap, reads=[src.key])
                    out_toks.append(t)
                    dbg_n[0] += 1

            out_toks = []

            P.dma("sp", SM.ap, smalls_d[:, :], writes=["SM"])
            P.dma("pool", W2B.ap, w2_d[:, :], writes=["W2B"])
            P.dma("pool", A2B.ap, a2_d[:, :], writes=["A2B"])
            P.dma("pool", G2A.ap, g2_d[0:128, :], writes=["G2A"])
            P.dma("pool", G2B.ap, g2_d[128:160, :], writes=["G2B"])
            memset(ONESB, 1.0)
            memset(ONES32, 1.0)
            memset(BONES32, 0.0)
            memset(BONES32.v(BONES32.ap[0:64, 0:64]), 1.0)
            memset(BONES32.v(BONES32.ap[64:128, 64:128]), 1.0)
            cp(BONESB, BONES32)
            memset(IDF, 0.0, "pool")
            asel(IDF, slice(0, 128), [[-1, 128]], ALU.not_equal, 1.0, 0, 1)
            cp(IDB, IDF)
            tsc(BONV, BONES32, 1.0 / 64, ALU.mult)
            tt(CEN32, IDF, BONV, ALU.subtract)
            memset(TMPC, 1.0, "pool")
            TMPM = RS
            for buf, pat, base, cm, cmpop in (
                (MU_S, [[0, NCH], [1, CH]], -1, -1, ALU.is_ge),
                (MU_I, [[0, NCH], [1, CH]], 0, -1, ALU.is_ge),
                (ML_S, [[0, NCH], [-1, CH]], -1, 1, ALU.is_ge),
                (IDS, [[0, NCH], [-1, CH]], 0, 1, ALU.is_equal),
            ):
                tmpm = TMPM
                memset(tmpm, 1.0, "pool")
                for h2 in range(2):
                    asel(tmpm, slice(64 * h2, 64 * h2 + 64), pat, cmpop, 0.0, base, cm)
                cp(buf, tmpm)
            cp(IDSB, IDS)
            memset(RMASK, 1.0)
            memset(RMASK.v(RMASK.ap.rearrange("p (c t) -> p c t", t=CH)[:, :, 0:1]), 0.0)
            memset(CARRY, 0.0)
            for h in range(KR):
                memset(STR[h], 0.0)
                memset(SG32[h], 0.0)
                memset(SGB[h], 0.0)
            o = [0]

            def col(n):
                s = o[0]
                o[0] += n
                return s

            c_g1, c_gm, c_g2, c_gf = col(KD), col(KD), col(KD), col(KD)
            c_mur, c_muk, c_muv = col(KR), col(KR), col(KR)
            c_w0, c_a0, c_kk, c_ka, c_rk, c_lnw, c_lnb = (col(KR) for _ in range(7))
            c_l0, c_l1, c_hn = col(KR), col(KR), col(KR)
            c_mxw, c_mxa, c_mga, c_mgb = col(1), col(1), col(1), col(1)
            c_s, c_1ms = col(1), col(1)
            assert o[0] == NSM

            def smc(c, rows=128):
                return SM.v(SM.ap[0:rows, c:c + 1])

            def derc(c, rows=128):
                return DER.v(DER.ap[0:rows, c:c + 1])

            tsc(OM.v(OM.ap[:, 0:3 * KR]), SM.v(SM.ap[:, c_mur:c_mur + 3 * KR]), -1.0, ALU.mult, 1.0, ALU.add)
            tsc(OM.v(OM.ap[:, 3 * KR:3 * KR + 4]), SM.v(SM.ap[:, c_mxw:c_mxw + 4]), -1.0, ALU.mult, 1.0, ALU.add)
            tsc(DER.v(DER.ap[:, 0:KR]), SM.v(SM.ap[:, c_ka:c_ka + KR]), -1.0, ALU.mult, 1.0, ALU.add)
            tt(DER.v(DER.ap[:, 3 * KR:4 * KR]), SM.v(SM.ap[:, c_l0:c_l0 + KR]), SM.v(SM.ap[:, c_l1:c_l1 + KR]), ALU.subtract)
            act(DER.v(DER.ap[:, KR:2 * KR]), DER.v(DER.ap[:, 3 * KR:4 * KR]), AF.Sigmoid)
            tsc(DER.v(DER.ap[:, 2 * KR:3 * KR]), DER.v(DER.ap[:, KR:2 * KR]), -1.0, ALU.mult, 1.0, ALU.add)

            PSS = B(PSB[0].ap.bitcast(F32)[:, 0:NT], PSB[0].key)

            XSRC = [XT]

            def sumsq_chunk(k):
                s_ = SQ[rr["sq"] % 2]
                rr["sq"] += 1
                act(s_, XSRC[0][k], AF.Square)
                mm(PSS, ONESB, s_, start=(k == 0), stop=(k == KD - 1), inc=True)

            def norm_apply(cg, out_list=None, inplace=False):
                act(RS, PSS, AF.Ln, scale=1.0 / D, bias=smc_eps)
                act(RS, RS, AF.Exp, scale=-0.5)
                for k in range(KD):
                    dst = XSRC[0][k] if inplace else out_list[k]
                    stt(dst, XSRC[0][k], smc(cg + k), RS, ALU.mult, ALU.mult)

            def rmsnorm(cg, out_list=None, inplace=False):
                for k in range(KD):
                    sumsq_chunk(k)
                norm_apply(cg, out_list, inplace)

            def ffn(i, on_chunk=None, dst=None):
                for s in range(cfg.NSF):
                    sg = ws.get(wg[i][s], KD * 256)
                    su = ws.get(wu[i][s], KD * 256)
                    for j in range(2):
                        f = s * 2 + j
                        pg, pu = psf(), psf()
                        for k in range(KD):
                            mm(pg, sg.v(sg.ap[:, k * 256 + j * 128:k * 256 + (j + 1) * 128]), HT[k], start=(k == 0), stop=(k == KD - 1))
                        for k in range(KD):
                            mm(pu, su.v(su.ap[:, k * 256 + j * 128:k * 256 + (j + 1) * 128]), HT[k], start=(k == 0), stop=(k == KD - 1))
                        act(TMPC, pg, AF.Silu)
                        tt(AT[f], TMPC, pu, ALU.mult)
                for dch in range(KD):
                    sd = ws.get(wd[i][dch], KF * 128)
                    po = psf()
                    for f in range(KF):
                        mm(po, sd.v(sd.ap[:, f * 128:(f + 1) * 128]), AT[f], start=(f == 0), stop=(f == KF - 1))
                    if on_chunk and dch > 0:
                        on_chunk(dch - 1)
                    stt((dst or XT)[dch], po, 0.5, XT[dch], ALU.mult, ALU.add)
                if on_chunk:
                    on_chunk(KD - 1)

            def shift_lerp(ps, rows, mu, cidx, out):
                pf = PF[rr["pf"] % 2]
                rr["pf"] += 1
                pfr = pf.v(pf.ap[0:rows, :])
                cp(pfr.v(pf.ap[0:rows, 0:1]), CARRY.v(CARRY.ap[0:rows, cidx:cidx + 1]))
                act(pfr.v(pf.ap[0:rows, 1:NT + 1]), ps.v(ps.ap[0:rows, :]), AF.Identity)
                tmp = TMPC.v(TMPC.ap[0:rows, :])
                act(tmp, ps.v(ps.ap[0:rows, :]), AF.Identity, scale=OM.v(OM.ap[0:rows, cidx:cidx + 1]))
                cp(CARRY.v(CARRY.ap[0:rows, cidx:cidx + 1]), pfr.v(pf.ap[0:rows, NT:NT + 1]), "act")
                stt(out, pfr.v(pf.ap[0:rows, 0:NT]), mu, tmp, ALU.mult, ALU.add)

            def proj(slab, coff, ncols, width):
                ps = psf()
                for k in range(KD):
                    mm(ps.v(ps.ap[0:ncols, :]), slab.v(slab.ap[:, k * width + coff:k * width + coff + ncols]), HTcur[0][k],
                       start=(k == 0), stop=(k == KD - 1))
                return ps

            def c3(b):
                return b.ap.rearrange("p (c t) -> p c t", t=CH)

            def blk(buf, h2, c, w=CH):
                return buf.v(buf.ap[64 * h2:64 * h2 + 64, c * w:(c + 1) * w])

            LASTF = lambda h2, c: (h2 == 1 and c == NCH - 1)

            def rwkv_prep(hp, need_out, T, need_r, O):
                Rf, Kf, Vf = T["Rf"], T["Kf"], T["Vf"]
                if need_r:
                    slab = ws.get(winr[hp], KD * 128)
                    ps = proj(slab, 0, 128, 128)
                    shift_lerp(ps, 128, smc(c_mur + hp), hp, Rf)
                    yield
                slab = ws.get(winkv[hp], KD * 256)
                ps = proj(slab, 0, 128, 256)
                shift_lerp(ps, 128, smc(c_muk + hp), KR + hp, Kf)
                yield
                ps = proj(slab, 128, 128, 256)
                shift_lerp(ps, 128, smc(c_muv + hp), 2 * KR + hp, Vf)
                yield
                chs = slice(hp * 128, (hp + 1) * 128)
                LW, Af, GG = T["LW"], T["Af"], O["GG"]
                ps = psf()
                mm(ps, W2B.v(W2B.ap[:, chs]), TW)
                act(LW, ps, AF.Sigmoid, bias=smc(c_w0 + hp))
                tsc(LW, LW, WSCALE, ALU.mult)
                yield
                ps = psf()
                mm(ps, A2B.v(A2B.ap[:, chs]), XA)
                act(Af, ps, AF.Sigmoid, bias=smc(c_a0 + hp))
                yield
                if need_out:
                    ps = psf()
                    mm(ps, G2A.v(G2A.ap[:, chs]), SGA, start=True, stop=False)
                    mm(ps, G2B.v(G2B.ap[:, chs]), SGBb, start=False, stop=True)
                    act(GG, ps, AF.Identity)
                    yield
                KK, T1, T2 = T["KK"], T["T1"], T["T2"]
                tsc(KK, Kf, smc(c_kk + hp), ALU.mult)
                sq = SQ[rr["sq"] % 2]
                rr["sq"] += 1
                act(sq, KK, AF.Square)
                ps = psf()
                mm(ps, BONESB, sq)
                yield
                act(T1, ps, AF.Sqrt)
                tsc(T1, T1, 1e-12, ALU.max)
                yield
                recip(T1, T1)
                tt(KK, KK, T1, ALU.mult)
                yield
                KM, BE = T["KM"], T["BE"]
                tsc(T1, Af, smc(c_ka + hp), ALU.mult, derc(hp), ALU.add)
                tt(KM, Kf, T1, ALU.mult)
                yield
                tt(BE, KK, Af, ALU.mult)
                G, EG, EGM, EGX, EP = T["G"], T["EG"], T["EGM"], T["EGX"], T["EP"]
                scan(G, RMASK, LW)
                yield
                act(EG, G, AF.Exp)
                act(EGM, G, AF.Exp, scale=-1.0)
                tt(T2, G, LW, ALU.subtract)
                yield
                act(EGX, T2, AF.Exp)
                cp(O["EGC"], EG.v(c3(EG)[:, :, CH - 1:CH]))
                egc = O["EGC"].v(O["EGC"].ap.to_broadcast([128, NCH, CH]))
                tt(EP.v(c3(EP)), EGM.v(c3(EGM)), egc, ALU.mult)
                yield
                stt(O["AL"], KK, -1.0, EGX, ALU.mult, ALU.mult)
                tt(O["BM"], BE, EGM, ALU.mult)
                yield
                tt(O["KMm"], KM, EGM, ALU.mult)
                tt(O["BP"], BE, EP, ALU.mult)
                yield
                tt(O["KP"], KM, EP, ALU.mult)
                cp(O["VB"], Vf, "act")
                yield
                if need_out:
                    tt(O["RT"], Rf, EG, ALU.mult)
                    stt(T2, Rf, smc(c_rk + hp), KM, ALU.mult, ALU.mult)
                    pb_ = psf()
                    mm(pb_, BONES32, T2)
                    yield
                    tt(O["BON"], pb_, Vf, ALU.mult)
                    yield

            def rwkv_mm(hp, need_out, T, O):
                AL, BM, KMm, BP, KP, VB, RT = (O[n] for n in ("AL", "BM", "KMm", "BP", "KP", "VB", "RT"))

                def transposed(src, dst):
                    pb = psb()
                    for c in range(NCH):
                        for h2 in range(2):
                            tr(blk(pb, h2, c), blk(src, h2, c), IDB.v(IDB.ap[64 * h2:64 * h2 + 64, 64 * h2:64 * h2 + 64]), inc=LASTF(h2, c))
                    cp(dst, pb.v(pb.ap[:, 0:NT]), "act")

                Zb, Z32 = T["Zb"], T["Z32"]
                Zb3 = Zb.ap.rearrange("p (c j) -> p c j", j=2 * CH)
                Z323 = Z32.ap.rearrange("p (c j) -> p c j", j=2 * CH)

                def zw(h2, c):
                    return Zb.v(Zb.ap[64 * h2:64 * h2 + 64, c * 2 * CH:c * 2 * CH + CH])

                def zu(h2, c):
                    return Zb.v(Zb.ap[64 * h2:64 * h2 + 64, c * 2 * CH + CH:(c + 1) * 2 * CH])

                ALt, BPt, KPt, Vt = T["ALt"], T["BPt"], T["KPt"], T["Vt"]
                transposed(AL, ALt)
                yield
                transposed(VB, Vt)
                yield

                def prod(lhs, rhs, mask, dst, eng="dve"):
                    ps = psf()
                    for c in range(NCH):
                        for h2 in range(2):
                            mm(blk(ps, h2, c), blk(lhs, h2, c), blk(rhs, h2, c), inc=LASTF(h2, c))
                    if mask is None:
                        if eng == "act":
                            act(dst, ps, AF.Identity)
                        else:
                            cp(dst, ps)
                    else:
                        tt(dst, ps, mask, ALU.mult)

                PT, PN = [T["PTa"], T["PTb"]], [T["PNa"], T["PNb"]]
                AKT, ARBT, ARKT = T["AKT"], T["ARBT"], T["ARKT"]
                prod(KMm, AL, MU_S, AKT)
                yield
                prod(BM, AL, MU_S, PT[0])
                yield
                prod(AL, BM, ML_S, PN[0])
                yield
                ps = psf()
                for c in range(NCH):
                    for h2 in range(2):
                        mm(blk(ps, h2, c), blk(AKT, h2, c), blk(Vt, h2, c), inc=LASTF(h2, c))
                cp(Zb.v(Zb3[:, :, 0:CH]), ALt.v(c3(ALt)), "act")
                act(Zb.v(Zb3[:, :, CH:2 * CH]), ps.v(c3(ps)), AF.Identity)
                PTI = T["AKT"]
                tt(PTI, PT[0], IDSB, ALU.add)
                yield
                transposed(BP, BPt)
                yield
                transposed(KP, KPt)
                yield
                if need_out:
                    prod(BM, RT, MU_I, ARBT)
                    yield
                    prod(KMm, RT, MU_I, ARKT)
                    yield
                cur = 0
                for lvl in range(6):
                    psa, psb_ = psf(), psf()
                    for c in range(NCH):
                        pz = psa if c < NCH // 2 else psb_
                        cc_ = c % (NCH // 2)
                        for h2 in range(2):
                            mm(pz.v(pz.ap[64 * h2:64 * h2 + 64, cc_ * 2 * CH:(cc_ + 1) * 2 * CH]), blk(PTI, h2, c),
                               Zb.v(Zb.ap[64 * h2:64 * h2 + 64, c * 2 * CH:(c + 1) * 2 * CH]),
                               inc=(h2 == 1 and (c == NCH // 2 - 1 or c == NCH - 1)))
                    nxt = 1 - cur
                    if lvl < 5:
                        psq = psf()
                        for c in range(NCH):
                            for h2 in range(2):
                                mm(blk(psq, h2, c), blk(PN[cur], h2, c), blk(PT[cur], h2, c), inc=LASTF(h2, c))
                        if lvl < 4:
                            psn = psf()
                            for c in range(NCH):
                                for h2 in range(2):
                                    mm(blk(psn, h2, c), blk(PT[cur], h2, c), blk(PN[cur], h2, c), inc=LASTF(h2, c))
                    act(Zb.v(Zb.ap[:, 0:NT]), psa, AF.Identity)
                    if lvl < 5:
                        tt(PTI, psq, IDSB, ALU.add)
                    act(Zb.v(Zb.ap[:, NT:2 * NT]), psb_, AF.Identity)
                    yield
                    if lvl < 4:
                        act(PT[nxt], psq, AF.Identity)
                        act(PN[nxt], psn, AF.Identity)
                        cur = nxt
                        yield
                PMT, QT, RH, DG = T["PMT"], T["QT"], T["RH"], T["DG"]
                egc = O["EGC"].v(O["EGC"].ap.to_broadcast([128, NCH, CH]))
                tt(DG.v(c3(DG)), IDS.v(c3(IDS)), egc, ALU.mult)
                ps = psf()
                for c in range(NCH):
                    for h2 in range(2):
                        mm(blk(ps, h2, c), zw(h2, c), blk(BPt, h2, c), inc=LASTF(h2, c))
                ps2 = psf()
                for c in range(NCH):
                    for h2 in range(2):
                        mm(blk(ps2, h2, c), blk(BPt, h2, c), zu(h2, c), start=True, stop=False)
                        mm(blk(ps2, h2, c), blk(KPt, h2, c), blk(Vt, h2, c), start=False, stop=True, inc=LASTF(h2, c))
                tt(PMT, ps, DG, ALU.add)
                QTb = B(QT.ap.bitcast(BF16)[:, 0:NT], QT.key)
                act(QTb, ps2, AF.Identity)
                yield
                STA = T["STA"]
                cp(STA.v(STA.ap[:, 0:CH]), STR[hp])
                for c in range(NCH):
                    ps = psf()
                    for h2 in range(2):
                        mm(ps.v(ps.ap[64 * h2:64 * h2 + 64, 0:CH]), blk(PMT, h2, c), blk(STA, h2, c), start=True, stop=False)
                        mm(ps.v(ps.ap[64 * h2:64 * h2 + 64, 0:CH]), IDB.v(IDB.ap[64 * h2:64 * h2 + 64, 64 * h2:64 * h2 + 64]), blk(QTb, h2, c),
                           start=False, stop=True, inc=(h2 == 1))
                    if c == 0 and need_out:
                        ps3 = psf()
                        for cc in range(NCH):
                            for h2 in range(2):
                                mm(blk(ps3, h2, cc), zw(h2, cc), blk(ARBT, h2, cc), inc=LASTF(h2, cc))
                    dst = STA.v(STA.ap[:, (c + 1) * CH:(c + 2) * CH]) if c < NCH - 1 else STR[hp]
                    act(dst, ps.v(ps.ap[:, 0:CH]), AF.Identity)
                    if c == 0 and need_out:
                        tt(RH, ps3, RT, ALU.add)
                    yield
                if not need_out:
                    return
                psy = psf()
                for c in range(NCH):
                    for h2 in range(2):
                        mm(blk(psy, h2, c), blk(STA, h2, c), blk(RH, h2, c), start=True, stop=False)
                        mm(blk(psy, h2, c), zu(h2, c), blk(ARBT, h2, c), start=False, stop=False)
                        mm(blk(psy, h2, c), blk(Vt, h2, c), blk(ARKT, h2, c), start=False, stop=True, inc=LASTF(h2, c))
                Yf, YQ, Yc, VR = Z32.v(Z32.ap[:, 0:NT]), Z32.v(Z32.ap[:, NT:2 * NT]), T["QT"], T["DG"]
                act(Yf, psy, AF.Identity)
                yield
                pc = psf()
                mm(pc, CEN32, Yf)
                act(Yc, pc, AF.Identity)
                act(YQ, pc, AF.Square)
                yield
                pq = psf()
                mm(pq, BONV, YQ)
                act(VR, pq, AF.Ln, bias=smc_gn)
                yield
                act(VR, VR, AF.Exp, scale=-0.5)
                tt(Yc, Yc, VR, ALU.mult)
                yield
                tsc(Yf, Yc, smc(c_lnw + hp), ALU.mult, smc(c_lnb + hp), ALU.add)
                yield
                tt(Yf, Yf, O["BON"], ALU.add)
                tt(YT[hp], Yf, O["GG"], ALU.mult)
                yield

            def interleave(ga, gb):
                gens = [g for g in (ga, gb) if g is not None]
                while gens:
                    for g in list(gens):
                        try:
                            next(g)
                        except StopIteration:
                            gens.remove(g)

            def rwkv_all(need_out, T, need_r):
                for r in range(KR + 1):
                    gp = rwkv_prep(r, need_out, T, need_r, OSET[r % 2]) if r < KR else None
                    gm = rwkv_mm(r - 1, need_out, T, OSET[(r - 1) % 2]) if r > 0 else None
                    interleave(gm, gp)

            def hgrn_prep(hh, T, O):
                Qf, FG, LF, KG = T["Rf"], T["Kf"], T["LW"], T["KM"]
                slab = ws.get(winqg[hh], KD * 256)
                ps = proj(slab, 0, 128, 256)
                act(Qf, ps, AF.Silu)
                yield
                ps = proj(slab, 128, 128, 256)
                act(O["GS"], ps, AF.Silu)
                yield
                slab = ws.get(winfi[hh], KD * 256)
                ps = proj(slab, 0, 128, 256)
                act(FG, ps, AF.Sigmoid)
                tsc(FG, FG, derc(2 * KR + hh), ALU.mult, derc(KR + hh), ALU.add)
                yield
                act(LF, FG, AF.Ln)
                tsc(KG, FG, -1.0, ALU.mult, 1.0, ALU.add)
                ps = proj(slab, 128, 128, 256)
                act(O["VB"], ps, AF.Identity)
                yield
                G, EG, EGM, EP = T["G"], T["EG"], T["EGM"], T["EP"]
                scan(G, RMASK, LF)
                yield
                act(EG, G, AF.Exp)
                act(EGM, G, AF.Exp, scale=-1.0)
                yield
                cp(O["EGC"], EG.v(c3(EG)[:, :, CH - 1:CH]))
                egc = O["EGC"].v(O["EGC"].ap.to_broadcast([128, NCH, CH]))
                tt(EP.v(c3(EP)), EGM.v(c3(EGM)), egc, ALU.mult)
                tt(O["KMm"], KG, EGM, ALU.mult)
                yield
                tt(O["KP"], KG, EP, ALU.mult)
                tt(O["QTl"], Qf, EG, ALU.mult)
                yield

            def hgrn_mm(hh, T, O):
                QTl, KMm, KP, VB, GS = O["QTl"], O["KMm"], O["KP"], O["VB"], O["GS"]
                KPt, Vt = T["KPth"], T["Vth"]
                for src, dst in ((KP, KPt), (VB, Vt)):
                    pb = psb()
                    for c in range(NCH):
                        tr(pb.v(pb.ap[0:64, c * 128:(c + 1) * 128]), src.v(src.ap[:, c * CH:(c + 1) * CH]), IDB, inc=(c == NCH - 1))
                    cp(dst.v(dst.ap[0:64, :]), pb.v(pb.ap[0:64, :]), "act")
                    yield
                ATm = T["ATm"]
                ps = psf()
                for c in range(NCH):
                    mm(ps.v(ps.ap[0:64, c * CH:(c + 1) * CH]), KMm.v(KMm.ap[:, c * CH:(c + 1) * CH]), QTl.v(QTl.ap[:, c * CH:(c + 1) * CH]), inc=(c == NCH - 1))
                tt(ATm.v(ATm.ap[0:64, :]), ps.v(ps.ap[0:64, :]), MU_I.v(MU_I.ap[0:64, :]), ALU.mult)
                yield
                SALL = T["SALL"]
                cp(SALL.v(SALL.ap[:, 0:128]), SGB[hh], "act")
                for c in range(NCH):
                    ps = psf()
                    mm(ps.v(ps.ap[:, 0:128]), KPt.v(KPt.ap[0:64, c * 128:(c + 1) * 128]), Vt.v(Vt.ap[0:64, c * 128:(c + 1) * 128]))
                    dst = SALL.v(SALL.ap[:, (c + 1) * 128:(c + 2) * 128]) if c < NCH - 1 else SGB[hh]
                    stt(dst, SALL.v(SALL.ap[:, c * 128:(c + 1) * 128]), O["EGC"].v(O["EGC"].ap[:, c, :]), ps.v(ps.ap[:, 0:128]), ALU.mult, ALU.add)
                    yield
                pso = psf()
                for c in range(NCH):
                    oc = pso.v(pso.ap[:, c * CH:(c + 1) * CH])
                    mm(oc, SALL.v(SALL.ap[:, c * 128:(c + 1) * 128]), QTl.v(QTl.ap[:, c * CH:(c + 1) * CH]), start=True, stop=False)
                    mm(oc, Vt.v(Vt.ap[0:64, c * 128:(c + 1) * 128]), ATm.v(ATm.ap[0:64, c * CH:(c + 1) * CH]), start=False, stop=True, inc=(c == NCH - 1))
                Z32 = T["Z32"]
                OQ, T1 = Z32.v(Z32.ap[:, 0:NT]), Z32.v(Z32.ap[:, NT:2 * NT])
                act(OQ, pso, AF.Square)
                yield
                pn = psf()
                mm(pn, ONES32, OQ)
                act(T1, pn, AF.Ln, scale=1.0 / 128, bias=smc_eps)
                yield
                act(T1, T1, AF.Exp, scale=-0.5)
                stt(T1, pso, smc(c_hn + hh), T1, ALU.mult, ALU.mult)
                yield
                tt(YT[KR + hh], T1, GS, ALU.mult)
                yield

            def xslab_prep(t):
                HTcur[0] = HTB[t % 2]
                slab = ws.get(winx[0], KD * 288)
                ps = proj(slab, 0, 64, 288)
                t64 = T["T1"].v(T["T1"].ap[0:64, :])
                shift_lerp(ps, 64, smc(c_mxw, 64), 3 * KR, t64)
                act(TW, t64, AF.Tanh)
                yield
                ps = proj(slab, 64, 64, 288)
                shift_lerp(ps, 64, smc(c_mxa, 64), 3 * KR + 1, t64)
                cp(XA, t64, "act")
                yield
                ps = proj(slab, 128, 128, 288)
                shift_lerp(ps, 128, smc(c_mga), 3 * KR + 2, T["T1"])
                act(SGA, T["T1"], AF.Sigmoid)
                yield
                ps = proj(slab, 256, 32, 288)
                t32 = T["T1"].v(T["T1"].ap[0:32, :])
                shift_lerp(ps, 32, smc(c_mgb, 32), 3 * KR + 3, t32)
                act(SGBb, t32, AF.Sigmoid)
                yield

            def chain_gens(*gs):
                for g in gs:
                    for _ in g:
                        yield

            def phase_b(load_h, finish_tile):
                units = []
                for t in range(NTB):
                    units += [("r", t, h) for h in range(KR)] + [("h", t, h) for h in range(KR)]
                nu = len(units)

                def mk_prep(i):
                    k, t, h = units[i]
                    if k == "r":
                        g = rwkv_prep(h, True, T, True, OSET[i % 2])
                        if h == 0:
                            if t + 1 < NTB:
                                load_h(t + 1)
                            g = chain_gens(xslab_prep(t), g)
                        return g
                    return hgrn_prep(h, T, OHS[i % 2])

                def mk_mm(i):
                    k, t, h = units[i]
                    return rwkv_mm(h, True, T, OSET[i % 2]) if k == "r" else hgrn_mm(h, T, OHS[i % 2])

                for r in range(nu + 1):
                    gp = mk_prep(r) if r < nu else None
                    gm = mk_mm(r - 1) if r > 0 else None
                    interleave(gm, gp)
                    if r > 0 and units[r - 1][0] == "h" and units[r - 1][2] == KR - 1:
                        finish_tile(units[r - 1][1])

            EPSC = B(sb("EPSC", [128, 2])[:], "EPSC")
            memset(EPSC.v(EPSC.ap[:, 0:1]), RMS_EPS)
            memset(EPSC.v(EPSC.ap[:, 1:2]), GN_EPS)
            smc_eps = EPSC.v(EPSC.ap[:, 0:1])
            smc_gn = EPSC.v(EPSC.ap[:, 1:2])

            T = {}
            for n in ("Rf", "Kf", "Vf", "LW", "Af", "KK", "T1", "T2", "KM", "G", "EG", "EGM", "Z32a", "Z32b", "QT", "DG"):
                T[n] = f32t(n)
            ZBOFF = [0]
            for n in ("ALt", "BPt", "KPt", "Vt", "Zba", "Zbb", "AKT", "PMT", "RH", "STA"):
                if n == "Zba":
                    ZBOFF[0] = arena_off[0]
                T[n] = b16t(n)
            for pair, nm in ((("PTa", "PTb"), "KPth"), (("PNa", "PNb"), "Vth"), (("ARBT", "ARKT"), "SALL")):
                o0 = arena_off[0]
                T[pair[0]] = b16t(pair[0])
                T[pair[1]] = b16t(pair[1])
                T[pair[1]] = B(T[pair[1]].ap, pair[0])
                T[nm] = B(ARt[:, o0:o0 + NT].bitcast(BF16), pair[0])
            OSET = []
            for si in range(2):
                Od = {}
                for n in ("AL", "BM", "KMm", "BP", "KP", "VB", "RT", "GG"):
                    Od[n] = b16t("%s_%d" % (n, si))
                Od["BON"] = f32t("BON_%d" % si)
                Od["EGC"] = B(sb("EGC%d" % si, [128, NCH, 1])[:], "EGC%d" % si)
                OSET.append(Od)
            oz = [i for i, n in enumerate(mix_keys) if n == "Z32a"][0]
            T["Z32"] = B(ARt[:, oz * NT:(oz + 2) * NT], "Z32a")
            zb0 = T["Zba"].ap
            T["Zb"] = B(ARt[:, ZBOFF[0]:ZBOFF[0] + NT].bitcast(BF16), "Zba")
            OHS = []
            ohk = []
            for si in range(2):
                Od = {}
                for j, n in enumerate(("QTl", "KMm", "KP", "VB", "GS")):
                    if XW // 2 >= 10 * (NT // 2):
                        o_ = XW // 2 + (si * 5 + j) * (NT // 2)
                        Od[n] = B(XTt[:, o_:o_ + NT // 2].bitcast(BF16), "OH%s_%d" % (n, si))
                    else:
                        Od[n] = B(sb("OH%s_%d" % (n, si), [128, NT], BF16)[:], "OH%s_%d" % (n, si))
                    ohk.append(Od[n].key)
                Od["EGC"] = B(sb("EGCH%d" % si, [128, NCH, 1])[:], "EGCH%d" % si)
                OHS.append(Od)
            T["BE"], T["EGX"], T["EP"] = T["Af"], T["T2"], T["G"]
            T["GG"] = OSET[0]["GG"]
            T["VBh"], T["QTl"], T["KMmh"], T["KPh"], T["ATm"] = OSET[0]["VB"], OSET[0]["RT"], OSET[0]["KMm"], OSET[0]["KP"], T["AKT"]
            at_keys = [a.key for a in AT]

            half = XW // 2
            xk0 = [x.key for x in XT[0:KD // 2]]
            xk1 = [x.key for x in XT[KD // 2:KD]]
            htk = [h.key for h in HT]
            hxk = [h.key for h in HX]
            ytk = [y.key for y in YT]
            groups = [[0, 1], [2, 3], [4, 5], [6, 7]]
            ATW_ = KF * NT // 2
            n_top = min(KD, (ARW - ATW_) // NT)
            assert (KD - n_top) * NT <= YW // 2
            X2 = [B(ARt[:, ATW_ + k * NT:ATW_ + (k + 1) * NT], "X2_%d" % k) for k in range(n_top)] + \
                 [B(YTt[:, :].bitcast(F32)[:, (k - n_top) * NT:(k - n_top + 1) * NT], "X2_%d" % k) for k in range(n_top, KD)]
            x2k = [x.key for x in X2]
            nq = 4 if KD % 4 == 0 else 1
            qw = XW // nq
            def load_x(tj):
                for q in range(nq):
                    P.dma("sp", XTt[:, q * qw:(q + 1) * qw], xmain[tj, :, q * qw:(q + 1) * qw],
                          writes=[x.key for x in XT[q * (KD // nq):(q + 1) * (KD // nq)]])

            load_x(0)
            for ti in range(ntm):
                XSRC[0] = XT
                rmsnorm(c_g1, HT)
                XSRC[0] = X2
                ffn(0, sumsq_chunk, dst=X2)
                if ti + 1 < ntm:
                    load_x(ti + 1)
                P.dma("sp", X1[ti * 128:(ti + 1) * 128, 0:n_top * NT], ARt[:, ATW_:ATW_ + n_top * NT], reads=x2k[0:n_top], writes=["X1a%d" % ti])
                if n_top < KD:
                    P.dma("sp", X1[ti * 128:(ti + 1) * 128, n_top * NT:XW], YTt[:, :].bitcast(F32)[:, 0:(KD - n_top) * NT], reads=x2k[n_top:], writes=["X1b%d" % ti])
                norm_apply(c_gm, AT[0:KD])
                P.dma("sp", HBin[ti][:, :], ATv[:, 0:XW], reads=[a.key for a in AT[0:KD]], writes=["HBin%d" % ti])
                P.collective(HBin[ti], HBall[ti], groups, reads=["HBin%d" % ti], writes=["HBall%d" % ti])
            XSRC[0] = XT
            P.transfer(at_keys + x2k, mix_keys + ytk)
            P.transfer(xk0 + xk1, hxk + ohk)

            def load_h(t):
                buf = HTB[t % 2]
                dst = HTt[:, :] if t % 2 == 0 else XTt[:, 0:half].bitcast(BF16)
                ci, ro = (t, 0) if t < ntm else (t - ntm, 128)
                P.dma("sp", dst, HBall[ci][ro:ro + 128, :], reads=["HBall%d" % ci], writes=[b.key for b in buf])

            if cfg.stage >= 2:
                load_h(0)

                def finish_tile(t):
                    P.dma("sp", YBin[t][:, :], YTt[:, :], reads=ytk, writes=["YBin%d" % t])
                    P.collective(YBin[t], YBall[t], groups, reads=["YBin%d" % t], writes=["YBall%d" % t])

                phase_b(load_h, finish_tile)
            P.transfer(hxk + ohk, xk0 + xk1)
            HTcur[0] = HT
            ATW = KF * NT // 2
            if ARW - ATW >= 3 * (YW // 2):
                yl_aps = [ARt[:, ATW + i * (YW // 2):ATW + (i + 1) * (YW // 2)].bitcast(BF16) for i in range(3)] + [YTt[:, :]]
                yl_pref = True
            else:
                yl_aps = [ARt[:, i * (YW // 2):(i + 1) * (YW // 2)].bitcast(BF16) for i in range(4)]
                yl_pref = False
            YL = [[B(yl_aps[2 * r + h], "YL%d%d" % (r, h)) for h in range(2)] for r in range(2)]
            ylk = [YL[r][h].key for r in range(2) for h in range(2)]
            for ti in range(ntm):
                P.dma("sp", XTt[:, 0:half], X1[ti * 128:(ti + 1) * 128, 0:half], reads=["X1a%d" % ti, "X1b%d" % ti], writes=xk0)
                P.dma("sp", XTt[:, half:XW], X1[ti * 128:(ti + 1) * 128, half:XW], reads=["X1a%d" % ti, "X1b%d" % ti], writes=xk1)
                def load_y(tj):
                    for r in range(2):
                        for h in range(2):
                            tq = h * ntm + tj
                            P.dma("sp", YL[r][h].ap, YBall[tq][r * 128:(r + 1) * 128, :], reads=["YBall%d" % tq], writes=[YL[r][h].key])

                if ti == 0:
                    P.transfer(at_keys + mix_keys + ytk, ylk)
                    load_y(0)
                elif not yl_pref:
                    P.transfer(at_keys + mix_keys, ylk)
                    load_y(ti)
                for r in range(2):
                    for j in range(2 * KR):
                        kglob = (r * KR + j) if j < KR else (2 * KR + r * KR + (j - KR))
                        ya = YL[r][0].v(YL[r][0].ap[:, j * NT:(j + 1) * NT])
                        yb = YL[r][1].v(YL[r][1].ap[:, j * NT:(j + 1) * NT])
                        tsc(SQ[0], ya, smc(c_1ms), ALU.mult)
                        stt(HT[kglob], yb, smc(c_s), SQ[0], ALU.mult, ALU.add)
                if yl_pref and ti + 1 < ntm:
                    load_y(ti + 1)
                if cfg.stage >= 5:
                    for sl in range(cfg.NSO):
                        slab = ws.get(wout[sl], KD * 256)
                        for j in range(2):
                            dch = sl * 2 + j
                            po = psf()
                            for k in range(KD):
                                mm(po, slab.v(slab.ap[:, k * 256 + j * 128:k * 256 + (j + 1) * 128]), HT[k], start=(k == 0), stop=(k == KD - 1))
                            if dch > 0:
                                sumsq_chunk(dch - 1)
                            tt(XT[dch], XT[dch], po, ALU.add)
                    sumsq_chunk(KD - 1)
                if not yl_pref:
                    P.transfer(ylk, at_keys)
                norm_apply(c_g2, HT)
                ffn(1, sumsq_chunk)
                norm_apply(c_gf, inplace=True)
                t1 = P.dma("sp", outT[ti, :, 0:half], XTt[:, 0:half], reads=xk0)
                t2 = P.dma("sp", outT[ti, :, half:XW], XTt[:, half:XW], reads=xk1)
                out_toks.extend([t1, t2])
            P.wait_all("sp", out_toks)
            if dry:
                saved_reqs = ws.reqs
                n_dry = P.ninstr
            else:
                print("instructions:", P.ninstr, "weight slabs:", len(ws.reqs), "engine counts:", P.cnt)
    return nc


def _cols(v, n):
    return np.ascontiguousarray(np.asarray(v, np.float32).reshape(n, 128).T)


def _slab(W, c0, width):
    K = W.shape[0]
    blk = W[:, c0:c0 + width].reshape(K // 128, 128, width).transpose(1, 0, 2)
    return np.ascontiguousarray(blk).reshape(128, (K // 128) * width)


def _slabs_cols(W, cols_list):
    out = []
    K = W.shape[0]
    for ranges in cols_list:
        parts = [W[:, c0:c0 + w] for c0, w in ranges]
        Wc = np.concatenate(parts, axis=1)
        out.append(_slab(Wc, 0, Wc.shape[1]))
    return np.stack(out)


def prepare_weights(cfg, inp):
    D, F, KD, KF, C, KR = cfg.D, cfg.F, cfg.KD, cfg.KF, cfg.C, cfg.KR
    CL, KRL = C // 2, KR // 2
    m = {}
    for i, tag in ((1, "ffn1"), (2, "ffn2")):
        Wg = np.asarray(inp[tag + "_w_gate"][0], np.float32)
        Wu = np.asarray(inp[tag + "_w_up"][0], np.float32)
        Wd = np.asarray(inp[tag + "_w_down"][0], np.float32)
        m["wg%d" % i] = np.stack([_slab(Wg, s * 256, 256) for s in range(cfg.NSF)])
        m["wu%d" % i] = np.stack([_slab(Wu, s * 256, 256) for s in range(cfg.NSF)])
        m["wd%d" % i] = np.stack([_slab(Wd, d * 128, 128) for d in range(KD)])
    Win = np.asarray(inp["w_in"][0], np.float32)
    m["winx"] = _slabs_cols(Win, [[(3 * C, 288)]])
    Wo = np.asarray(inp["w_out"][0], np.float32)
    m["wout"] = np.stack([_slab(Wo, s * 256, 256) for s in range(cfg.NSO)])
    R0 = 3 * C + 288
    mu = np.asarray(inp["rwkv_mu"][0], np.float32)
    w2 = np.asarray(inp["rwkv_w2"][0], np.float32)
    a2 = np.asarray(inp["rwkv_a2"][0], np.float32)
    g2 = np.asarray(inp["rwkv_g2"][0], np.float32)
    per = []
    for s in range(2):
        p = {}
        prs = range(s * KRL, (s + 1) * KRL)
        p["winr"] = _slabs_cols(Win, [[(hp * 128, 128)] for hp in prs])
        p["winkv"] = _slabs_cols(Win, [[(C + hp * 128, 128), (2 * C + hp * 128, 128)] for hp in prs])
        p["winqg"] = _slabs_cols(Win, [[(R0 + hh * 128, 128), (R0 + 3 * C + hh * 128, 128)] for hh in prs])
        p["winfi"] = _slabs_cols(Win, [[(R0 + C + hh * 128, 128), (R0 + 2 * C + hh * 128, 128)] for hh in prs])
        ch = slice(s * CL, (s + 1) * CL)

        def lc(v):
            return _cols(np.asarray(v, np.float32).reshape(-1)[ch], KRL)

        cols = [_cols(inp["ffn1_norm"][0], KD), _cols(inp["mix_norm"][0], KD), _cols(inp["ffn2_norm"][0], KD),
                _cols(inp["final_norm"], KD),
                lc(mu[0:C]), lc(mu[C:2 * C]), lc(mu[2 * C:3 * C]),
                lc(inp["rwkv_w0"][0]), lc(inp["rwkv_a0"][0]), lc(inp["rwkv_k_k"][0]),
                lc(inp["rwkv_k_a"][0]), lc(inp["rwkv_r_k"][0]),
                lc(inp["rwkv_ln_w"][0]), lc(inp["rwkv_ln_b"][0]),
                lc(inp["hgrn_lb_logits"][0]), lc(inp["hgrn_lb_logits"][1]), lc(inp["hgrn_norm"][0])]
        extra = np.zeros((128, 6), np.float32)
        extra[0:64, 0] = mu[3 * C:3 * C + 64]
        extra[0:64, 1] = mu[3 * C + 64:3 * C + 128]
        extra[0:128, 2] = mu[3 * C + 128:3 * C + 256]
        extra[0:32, 3] = mu[3 * C + 256:3 * C + 288]
        extra[:, 4] = float(s)
        extra[:, 5] = float(1 - s)
        p["smalls"] = np.ascontiguousarray(np.concatenate(cols + [extra], axis=1))
        p["w2"] = np.ascontiguousarray(w2[:, ch])
        p["a2"] = np.ascontiguousarray(a2[:, ch])
        p["g2"] = np.ascontiguousarray(g2[:, ch])
        per.append(p)
    return m, per


def _tiles(xseg, KD):
    Tn = xseg.shape[0]
    a = xseg.reshape(Tn // NT, NT, KD, 128).transpose(0, 3, 2, 1)
    return np.ascontiguousarray(a).reshape(Tn // NT, 128, KD * NT)


def run(cfg, inp, debug=False):
    x = np.asarray(inp["x"], np.float32)
    Bn, Tn, D = x.shape
    assert Bn * 2 == 8 and Tn // 2 == cfg.ntm * NT
    wm, per = prepare_weights(cfg, inp)
    nc = build_program(cfg, debug=debug)
    in_maps = []
    hl = Tn // 2
    for c in range(8):
        b, s = c // 2, c % 2
        d = dict(wm)
        d.update(per[s])
        d["xmain"] = _tiles(x[b, s * hl:(s + 1) * hl], cfg.KD)
        in_maps.append(d)
    res = run_bass_kernel_spmd(nc, in_maps, core_ids=list(range(8)))
    out = np.empty((Bn, Tn, D), np.float32)
    for c in range(8):
        b, s = c // 2, c % 2
        o = res.results[c]["outT"].reshape(cfg.ntm, 128, cfg.KD, NT).transpose(0, 3, 2, 1).reshape(hl, D)
        out[b, s * hl:(s + 1) * hl] = o
    return out


def kernel(**inputs):
    cfg = Cfg(D=2048, F=5632, ntp=8, ntm=8)
    return run(cfg, inputs)
```

```python
from contextlib import ExitStack
import numpy as np
import concourse.bass as bass
import concourse.mybir as mybir
from concourse.bass_utils import run_bass_kernel_spmd

F32 = mybir.dt.float32
BF16 = mybir.dt.bfloat16
AF = mybir.ActivationFunctionType
ALU = mybir.AluOpType

NT = 512
CH = 64
NCH = NT // CH
NSLOT = 4
SLOT = 5632
RMS_EPS = 1e-6
GN_EPS = 64e-5
WSCALE = -0.6065306597126334


class Cfg:
    def __init__(self, D=2048, F=5632, ntp=8, ntm=8, stage=99):
        self.D, self.F, self.ntp, self.ntm, self.stage = D, F, ntp, ntm, stage
        self.KD = D // 128
        self.KF = F // 128
        self.C = D // 2
        self.KR = self.C // 128
        self.NSF = F // 256
        self.NSO = D // 256


class B:
    __slots__ = ("ap", "key")

    def __init__(self, ap, key):
        self.ap, self.key = ap, key

    def __getitem__(self, idx):
        return B(self.ap[idx], self.key)

    def v(self, ap):
        return B(ap, self.key)


class Prog:
    def __init__(self, nc, st, dry):
        self.nc, self.dry = nc, dry
        self.eng = {"pe": nc.tensor, "act": nc.scalar, "dve": nc.vector, "pool": nc.gpsimd, "sp": nc.sync}
        self.semobj = {}
        self.cnt = {k: 0 for k in self.eng}
        self.seen = {k: {} for k in self.eng}
        self.nd = 12
        self.dcnt = [0] * self.nd
        self.dnext = 0
        if not dry:
            for k in self.eng:
                self.semobj["s_" + k] = st.enter_context(nc.semaphore("s_" + k))
            for i in range(self.nd):
                self.semobj["d%d" % i] = st.enter_context(nc.semaphore("d%d" % i))
            self.semobj["cc"] = st.enter_context(nc.semaphore("cc"))
        self.ncc = 0
        self.bufs = {}
        self.ninstr = 0
        self.pe_open = False

    def _wait(self, e, sname, val):
        if e == "pe" and sname == "s_pe":
            return
        if self.seen[e].get(sname, 0) >= val:
            return
        self.eng[e].wait_ge(self.semobj[sname], val)
        self.seen[e][sname] = val

    def _deps(self, e, reads, writes):
        for k in reads:
            b = self.bufs.get(k)
            if b and b[0]:
                self._wait(e, b[0][0], b[0][1])
        for k in writes:
            b = self.bufs.get(k)
            if b:
                if b[0]:
                    self._wait(e, b[0][0], b[0][1])
                for s, v in b[1].items():
                    self._wait(e, s, v)

    def _mark(self, tok, reads, writes):
        for k in reads:
            b = self.bufs.get(k)
            if b is None:
                b = self.bufs[k] = [None, {}]
            if b[1].get(tok[0], 0) < tok[1]:
                b[1][tok[0]] = tok[1]
        for k in writes:
            self.bufs[k] = [tok, {}]

    def op(self, e, fn, reads=(), writes=(), inc=True):
        self.ninstr += 1
        if self.dry:
            return None
        if e != "pe":
            assert not self.pe_open, "other-engine op emitted inside an open PE group"
            extra = [k for k in reads if k.startswith("ps")]
            if extra:
                writes = list(writes) + extra
        self._deps(e, reads, writes)
        ins = fn(self.eng[e])
        if inc:
            self.cnt[e] += 1
            ins.then_inc(self.semobj["s_" + e], 1)
            tok = ("s_" + e, self.cnt[e])
            if e == "pe":
                self.pe_open = False
        else:
            assert e == "pe"
            tok = ("s_" + e, self.cnt[e] + 1)
            self.pe_open = True
        self._mark(tok, reads, writes)
        return tok

    def dma(self, q, out, in_, reads=(), writes=()):
        self.ninstr += 1
        if self.dry:
            return None
        assert not self.pe_open
        i = self.dnext
        self.dnext = (self.dnext + 1) % self.nd
        if self.dcnt[i] > 0:
            self._wait(q, "d%d" % i, 16 * self.dcnt[i])
        self._deps(q, reads, writes)
        ins = self.eng[q].dma_start(out=out, in_=in_)
        self.dcnt[i] += 1
        ins.then_inc(self.semobj["d%d" % i], 16)
        tok = ("d%d" % i, 16 * self.dcnt[i])
        self._mark(tok, reads, writes)
        return tok

    def collective(self, in_t, out_t, groups, reads, writes):
        self.ninstr += 1
        if self.dry:
            return None
        assert not self.pe_open
        self._deps("pool", reads, writes)
        ins = self.eng["pool"].collective_compute("AllGather", ALU.bypass, replica_groups=groups,
                                                  ins=[in_t.ap().opt()], outs=[out_t.ap().opt()])
        self.ncc += 1
        ins.then_inc(self.semobj["cc"], 1)
        tok = ("cc", self.ncc)
        self._mark(tok, reads, writes)
        return tok

    def transfer(self, from_keys, to_keys):
        if self.dry:
            return
        acc = {}
        for k in from_keys:
            b = self.bufs.get(k)
            if not b:
                continue
            if b[0]:
                acc[b[0][0]] = max(acc.get(b[0][0], 0), b[0][1])
            for s, v in b[1].items():
                acc[s] = max(acc.get(s, 0), v)
        for k in to_keys:
            b = self.bufs.get(k)
            if b is None:
                b = self.bufs[k] = [None, {}]
            for s, v in acc.items():
                if b[1].get(s, 0) < v:
                    b[1][s] = v

    def wait_all(self, e, toks):
        if self.dry:
            return
        for t in toks:
            if t:
                self._wait(e, t[0], t[1])


class WStream:
    def __init__(self, P, slots):
        self.P, self.slots = P, slots
        self.reqs = []
        self.pos = 0
        self.loaded = 0

    def get(self, dram_ap, nelem):
        if self.P.dry:
            self.reqs.append((dram_ap, nelem))
            return self.slots[0]
        idx = self.pos
        self.pos += 1
        lim = min(len(self.reqs), idx + NSLOT - 1)
        while self.loaded < lim:
            j = self.loaded
            ap, n = self.reqs[j]
            s = self.slots[j % NSLOT]
            self.P.dma("pool", s.ap[:, 0:n], ap, writes=[s.key])
            self.loaded += 1
        return self.slots[idx % NSLOT]


def build_program(cfg, debug=False):
    nc = bass.Bass("TRN2", target_bir_lowering=False)
    D, F, KD, KF = cfg.D, cfg.F, cfg.KD, cfg.KF
    C, KR = cfg.C // 2, cfg.KR // 2
    ntm = cfg.ntm
    NTB = 2 * ntm
    XW = KD * NT
    YW = 2 * KR * NT

    def din(name, shape):
        return nc.dram_tensor(name, shape, F32, kind="ExternalInput").ap()

    xmain = din("xmain", [ntm, 128, XW])
    outT = nc.dram_tensor("outT", [ntm, 128, XW], F32, kind="ExternalOutput").ap()
    wg = [din("wg%d" % i, [cfg.NSF, 128, KD * 256]) for i in (1, 2)]
    wu = [din("wu%d" % i, [cfg.NSF, 128, KD * 256]) for i in (1, 2)]
    wd = [din("wd%d" % i, [KD, 128, KF * 128]) for i in (1, 2)]
    winx = din("winx", [1, 128, KD * 288])
    winr = din("winr", [KR, 128, KD * 128])
    winkv = din("winkv", [KR, 128, KD * 256])
    winfi = din("winfi", [KR, 128, KD * 256])
    winqg = din("winqg", [KR, 128, KD * 256])
    wout = din("wout", [cfg.NSO, 128, KD * 256])
    X1 = nc.dram_tensor("X1s", [ntm * 128, XW], F32)
    HBin = [nc.dram_tensor("HBin%d" % i, [128, XW], BF16) for i in range(ntm)]
    HBall = [nc.dram_tensor("HBall%d" % i, [256, XW], BF16) for i in range(ntm)]
    YBin = [nc.dram_tensor("YBin%d" % i, [128, YW], BF16) for i in range(NTB)]
    YBall = [nc.dram_tensor("YBall%d" % i, [256, YW], BF16) for i in range(NTB)]
    NSM = 4 * KD + 13 * KR + 6
    smalls_d = din("smalls", [128, NSM])
    w2_d = din("w2", [64, C])
    a2_d = din("a2", [64, C])
    g2_d = din("g2", [160, C])
    dbg = None
    if debug:
        dbg = nc.dram_tensor("dbg", [16, 128, NT], F32, kind="ExternalOutput").ap()

    for dry in (True, False):
        with ExitStack() as st:
            P = Prog(nc, st, dry)
            if dry:
                ws_prev = None

            pfx = "dry_" if dry else ""

            def sb(name, shape, dt=F32):
                return st.enter_context(nc.sbuf_tensor(pfx + name, shape, dt))

            XTt = sb("XT", [128, XW])
            HTt = sb("HT", [128, XW], BF16)
            YTt = sb("YT", [128, YW], BF16)
            ARW = max(KF * NT // 2, 16 * NT + 2 * (8 * NT // 2 + NT) + 16 * NT // 2)
            ARt = sb("ARENA", [128, ARW])
            WRt = [sb("WR%d" % i, [128, SLOT], BF16) for i in range(NSLOT)]
            XT = [B(XTt[:, k * NT:(k + 1) * NT], "XT%d" % k) for k in range(KD)]
            HT = [B(HTt[:, k * NT:(k + 1) * NT], "HT%d" % k) for k in range(KD)]
            YT = [B(YTt[:, k * NT:(k + 1) * NT], "YT%d" % k) for k in range(2 * KR)]
            HX = [B(XTt[:, 0:XW // 2].bitcast(BF16)[:, k * NT:(k + 1) * NT], "HX%d" % k) for k in range(KD)]
            HTB = [HT, HX]
            HTcur = [HT]
            ATv = ARt[:].bitcast(BF16)
            AT = [B(ATv[:, f * NT:(f + 1) * NT], "AT%d" % f) for f in range(KF)]
            WR = [B(WRt[i][:], "WR%d" % i) for i in range(NSLOT)]
            ws = WStream(P, WR)
            if not dry:
                ws.reqs = saved_reqs
            arena_off = [0]
            mix_keys = []

            def carve(name, words, dt=F32):
                o = arena_off[0]
                arena_off[0] += words
                assert arena_off[0] <= ARW, "arena overflow"
                ap = ARt[:, o:o + words]
                if dt == BF16:
                    ap = ap.bitcast(BF16)
                mix_keys.append(name)
                return B(ap, name)

            def f32t(name):
                return carve(name, NT)

            def b16t(name):
                return carve(name, NT // 2, BF16)

            SM = B(sb("SM", [128, NSM])[:], "SM")
            ONESB = B(sb("ONESB", [128, 128], BF16)[:], "ONESB")
            ONES32 = B(sb("ONES32", [128, 128])[:], "ONES32")
            BONES32 = B(sb("BONES32", [128, 128])[:], "BONES32")
            BONESB = B(sb("BONESB", [128, 128], BF16)[:], "BONESB")
            BONV = B(sb("BONV", [128, 128])[:], "BONV")
            CEN32 = B(sb("CEN32", [128, 128])[:], "CEN32")
            IDB = B(sb("IDB", [128, 128], BF16)[:], "IDB")
            IDF = B(sb("IDF", [128, 128])[:], "IDF")
            MU_S = B(sb("MU_S", [128, NT], BF16)[:], "MU_S")
            MU_I = B(sb("MU_I", [128, NT], BF16)[:], "MU_I")
            ML_S = B(sb("ML_S", [128, NT], BF16)[:], "ML_S")
            IDS = B(sb("IDS", [128, NT])[:], "IDS")
            IDSB = B(sb("IDSB", [128, NT], BF16)[:], "IDSB")
            RMASK = B(sb("RMASK", [128, NT])[:], "RMASK")
            TMPC = B(sb("TMPC", [128, NT])[:], "TMPC")
            W2B = B(sb("W2B", [64, C], BF16)[:], "W2B")
            A2B = B(sb("A2B", [64, C], BF16)[:], "A2B")
            G2A = B(sb("G2A", [128, C], BF16)[:], "G2A")
            G2B = B(sb("G2B", [32, C], BF16)[:], "G2B")
            NCC = 3 * KR + 4
            CARRY = B(sb("CARRY", [128, NCC])[:], "CARRY")
            DER = B(sb("DER", [128, 4 * KR])[:], "DER")
            OM = B(sb("OM", [128, NCC])[:], "OM")
            STR = [B(sb("STR%d" % h, [128, 64], BF16)[:], "STR%d" % h) for h in range(KR)]
            SG32 = [B(sb("SG32_%d" % h, [128, 128])[:], "SG32_%d" % h) for h in range(KR)]
            SGB = [B(sb("SGB_%d" % h, [128, 128], BF16)[:], "SGB_%d" % h) for h in range(KR)]
            TW = B(sb("TW", [64, NT], BF16)[:], "TW")
            XA = B(sb("XA", [64, NT], BF16)[:], "XA")
            SGA = B(sb("SGA", [128, NT], BF16)[:], "SGA")
            SGBb = B(sb("SGBb", [32, NT], BF16)[:], "SGBb")
            PF = [B(sb("PF%d" % i, [128, NT + 1])[:], "PF%d" % i) for i in range(2)]
            SQ = [B(sb("SQ%d" % i, [128, NT], BF16)[:], "SQ%d" % i) for i in range(2)]
            RS = B(sb("RS", [128, NT])[:], "RS")
            PSF = [B(st.enter_context(nc.psum_tensor(pfx + "psf%d" % i, [128, NT], F32))[:], "psf%d" % i) for i in range(6)]
            PSB = [B(st.enter_context(nc.psum_tensor(pfx + "psb%d" % i, [128, 2 * NT], BF16))[:], "psb%d" % i) for i in range(2)]
            rr = {"f": 0, "b": 0, "pf": 0, "sq": 0}

            def psf():
                rr["f"] = (rr["f"] + 1) % len(PSF)
                return PSF[rr["f"]]

            def psb():
                rr["b"] = (rr["b"] + 1) % len(PSB)
                return PSB[rr["b"]]

            def mm(out, lhsT, rhs, start=True, stop=True, inc=None):
                if inc is None:
                    inc = stop
                P.op("pe", lambda e: e.matmul(out.ap, lhsT.ap, rhs.ap, start=start, stop=stop),
                     reads=[lhsT.key, rhs.key], writes=[out.key], inc=inc)

            def tr(out, in_, ident, inc=True):
                P.op("pe", lambda e: e.transpose(out.ap, in_.ap, ident.ap), reads=[in_.key, ident.key], writes=[out.key], inc=inc)

            def act(out, in_, func, scale=None, bias=None):
                kw = {}
                rd = [in_.key]
                if scale is not None:
                    if isinstance(scale, B):
                        kw["scale"] = scale.ap
                        rd.append(scale.key)
                    else:
                        kw["scale"] = float(scale)
                if bias is not None:
                    if isinstance(bias, B):
                        kw["bias"] = bias.ap
                        rd.append(bias.key)
                    else:
                        kw["bias"] = float(bias)
                P.op("act", lambda e: e.activation(out=out.ap, in_=in_.ap, func=func, **kw), reads=rd, writes=[out.key])

            def tt(out, in0, in1, op, eng="dve"):
                P.op(eng, lambda e: e.tensor_tensor(out=out.ap, in0=in0.ap, in1=in1.ap, op=op),
                     reads=[in0.key, in1.key], writes=[out.key])

            def tsc(out, in0, s1, op0, s2=None, op1=None, eng="dve"):
                rd = [in0.key]
                a1 = s1.ap if isinstance(s1, B) else float(s1)
                if isinstance(s1, B):
                    rd.append(s1.key)
                if s2 is None:
                    P.op(eng, lambda e: e.tensor_scalar(out=out.ap, in0=in0.ap, scalar1=a1, scalar2=None, op0=op0),
                         reads=rd, writes=[out.key])
                else:
                    a2 = s2.ap if isinstance(s2, B) else float(s2)
                    if isinstance(s2, B):
                        rd.append(s2.key)
                    P.op(eng, lambda e: e.tensor_scalar(out=out.ap, in0=in0.ap, scalar1=a1, scalar2=a2, op0=op0, op1=op1),
                         reads=rd, writes=[out.key])

            def stt(out, in0, s, in1, op0, op1):
                rd = [in0.key, in1.key]
                a = s.ap if isinstance(s, B) else float(s)
                if isinstance(s, B):
                    rd.append(s.key)
                P.op("dve", lambda e: e.scalar_tensor_tensor(out=out.ap, in0=in0.ap, scalar=a, in1=in1.ap, op0=op0, op1=op1),
                     reads=rd, writes=[out.key])

            def cp(out, in_, eng="dve"):
                if eng == "act":
                    act(out, in_, AF.Identity)
                    return
                P.op(eng, lambda e: e.tensor_copy(out=out.ap, in_=in_.ap), reads=[in_.key], writes=[out.key])

            def recip(out, in_):
                P.op("dve", lambda e: e.reciprocal(out=out.ap, in_=in_.ap), reads=[in_.key], writes=[out.key])

            def memset(buf, val, eng="dve"):
                P.op(eng, lambda e: e.memset(buf.ap, val), writes=[buf.key])

            def scan(out, d0, d1):
                P.op("dve", lambda e: e.tensor_tensor_scan(out=out.ap, data0=d0.ap, data1=d1.ap, initial=0.0,
                                                           op0=ALU.mult, op1=ALU.add),
                     reads=[d0.key, d1.key], writes=[out.key])

            def asel(buf, rows, pattern, cmp, fill, base, cm):
                P.op("pool", lambda e: e.affine_select(out=buf.ap[rows], in_=buf.ap[rows], pattern=pattern, compare_op=cmp,
                                                       fill=fill, base=base, channel_multiplier=cm),
                     reads=[buf.key], writes=[buf.key])

            dbg_n = [0]

            def dump(buf):
                if debug and dbg_n[0] < 16:
                    if buf.ap.dtype != F32:
                        cp(TMPC.v(TMPC.ap[0:buf.ap.shape[0], 0:buf.ap.shape[1]]), buf)
                        src = TMPC.v(TMPC.ap[0:buf.ap.shape[0], 0:buf.ap.shape[1]])
                    else:
                        src = buf
                    t = P.dma("sp", dbg[dbg_n[0], 0:src.ap.shape[0], 0:src.ap.shape[1]], src.ap, reads=[src.key])
                    out_toks.append(t)
                    dbg_n[0] += 1

            out_toks = []

            P.dma("sp", SM.ap, smalls_d[:, :], writes=["SM"])
            P.dma("pool", W2B.ap, w2_d[:, :], writes=["W2B"])
            P.dma("pool", A2B.ap, a2_d[:, :], writes=["A2B"])
            P.dma("pool", G2A.ap, g2_d[0:128, :], writes=["G2A"])
            P.dma("pool", G2B.ap, g2_d[128:160, :], writes=["G2B"])
            memset(ONESB, 1.0)
            memset(ONES32, 1.0)
            memset(BONES32, 0.0)
            memset(BONES32.v(BONES32.ap[0:64, 0:64]), 1.0)
            memset(BONES32.v(BONES32.ap[64:128, 64:128]), 1.0)
            cp(BONESB, BONES32)
            memset(IDF, 0.0, "pool")
            asel(IDF, slice(0, 128), [[-1, 128]], ALU.not_equal, 1.0, 0, 1)
            cp(IDB, IDF)
            tsc(BONV, BONES32, 1.0 / 64, ALU.mult)
            tt(CEN32, IDF, BONV, ALU.subtract)
            memset(TMPC, 1.0, "pool")
            TMPM = RS
            for buf, pat, base, cm, cmpop in (
                (MU_S, [[0, NCH], [1, CH]], -1, -1, ALU.is_ge),
                (MU_I, [[0, NCH], [1, CH]], 0, -1, ALU.is_ge),
                (ML_S, [[0, NCH], [-1, CH]], -1, 1, ALU.is_ge),
                (IDS, [[0, NCH], [-1, CH]], 0, 1, ALU.is_equal),
            ):
                tmpm = TMPM
                memset(tmpm, 1.0, "pool")
                for h2 in range(2):
                    asel(tmpm, slice(64 * h2, 64 * h2 + 64), pat, cmpop, 0.0, base, cm)
                cp(buf, tmpm)
            cp(IDSB, IDS)
            memset(RMASK, 1.0)
            memset(RMASK.v(RMASK.ap.rearrange("p (c t) -> p c t", t=CH)[:, :, 0:1]), 0.0)
            memset(CARRY, 0.0)
            for h in range(KR):
                memset(STR[h], 0.0)
                memset(SG32[h], 0.0)
                memset(SGB[h], 0.0)
            o = [0]

            def col(n):
                s = o[0]
                o[0] += n
                return s

            c_g1, c_gm, c_g2, c_gf = col(KD), col(KD), col(KD), col(KD)
            c_mur, c_muk, c_muv = col(KR), col(KR), col(KR)
            c_w0, c_a0, c_kk, c_ka, c_rk, c_lnw, c_lnb = (col(KR) for _ in range(7))
            c_l0, c_l1, c_hn = col(KR), col(KR), col(KR)
            c_mxw, c_mxa, c_mga, c_mgb = col(1), col(1), col(1), col(1)
            c_s, c_1ms = col(1), col(1)
            assert o[0] == NSM

            def smc(c, rows=128):
                return SM.v(SM.ap[0:rows, c:c + 1])

            def derc(c, rows=128):
                return DER.v(DER.ap[0:rows, c:c + 1])

            tsc(OM.v(OM.ap[:, 0:3 * KR]), SM.v(SM.ap[:, c_mur:c_mur + 3 * KR]), -1.0, ALU.mult, 1.0, ALU.add)
            tsc(OM.v(OM.ap[:, 3 * KR:3 * KR + 4]), SM.v(SM.ap[:, c_mxw:c_mxw + 4]), -1.0, ALU.mult, 1.0, ALU.add)
            tsc(DER.v(DER.ap[:, 0:KR]), SM.v(SM.ap[:, c_ka:c_ka + KR]), -1.0, ALU.mult, 1.0, ALU.add)
            tt(DER.v(DER.ap[:, 3 * KR:4 * KR]), SM.v(SM.ap[:, c_l0:c_l0 + KR]), SM.v(SM.ap[:, c_l1:c_l1 + KR]), ALU.subtract)
            act(DER.v(DER.ap[:, KR:2 * KR]), DER.v(DER.ap[:, 3 * KR:4 * KR]), AF.Sigmoid)
            tsc(DER.v(DER.ap[:, 2 * KR:3 * KR]), DER.v(DER.ap[:, KR:2 * KR]), -1.0, ALU.mult, 1.0, ALU.add)

            PSS = B(PSB[0].ap.bitcast(F32)[:, 0:NT], PSB[0].key)

            XSRC = [XT]

            def sumsq_chunk(k):
                s_ = SQ[rr["sq"] % 2]
                rr["sq"] += 1
                act(s_, XSRC[0][k], AF.Square)
                mm(PSS, ONESB, s_, start=(k == 0), stop=(k == KD - 1), inc=True)

            def norm_apply(cg, out_list=None, inplace=False, on_done=None):
                act(RS, PSS, AF.Ln, scale=1.0 / D, bias=smc_eps)
                act(RS, RS, AF.Exp, scale=-0.5)
                for k in range(KD):
                    dst = XSRC[0][k] if inplace else out_list[k]
                    stt(dst, XSRC[0][k], smc(cg + k), RS, ALU.mult, ALU.mult)
                    if on_done:
                        on_done(k)

            def rmsnorm(cg, out_list=None, inplace=False):
                for k in range(KD):
                    sumsq_chunk(k)
                norm_apply(cg, out_list, inplace)

            def ffn(i, on_chunk=None, dst=None):
                for s in range(cfg.NSF):
                    sg = ws.get(wg[i][s], KD * 256)
                    su = ws.get(wu[i][s], KD * 256)
                    for j in range(2):
                        f = s * 2 + j
                        pg, pu = psf(), psf()
                        for k in range(KD):
                            mm(pg, sg.v(sg.ap[:, k * 256 + j * 128:k * 256 + (j + 1) * 128]), HT[k], start=(k == 0), stop=(k == KD - 1))
                        for k in range(KD):
                            mm(pu, su.v(su.ap[:, k * 256 + j * 128:k * 256 + (j + 1) * 128]), HT[k], start=(k == 0), stop=(k == KD - 1))
                        act(TMPC, pg, AF.Silu)
                        tt(AT[f], TMPC, pu, ALU.mult)
                for dch in range(KD):
                    sd = ws.get(wd[i][dch], KF * 128)
                    po = psf()
                    for f in range(KF):
                        mm(po, sd.v(sd.ap[:, f * 128:(f + 1) * 128]), AT[f], start=(f == 0), stop=(f == KF - 1))
                    if on_chunk and dch > 0:
                        on_chunk(dch - 1)
                    stt((dst or XT)[dch], po, 0.5, XT[dch], ALU.mult, ALU.add)
                if on_chunk:
                    on_chunk(KD - 1)

            def shift_lerp(ps, rows, mu, cidx, out):
                pf = PF[rr["pf"] % 2]
                rr["pf"] += 1
                pfr = pf.v(pf.ap[0:rows, :])
                cp(pfr.v(pf.ap[0:rows, 0:1]), CARRY.v(CARRY.ap[0:rows, cidx:cidx + 1]))
                act(pfr.v(pf.ap[0:rows, 1:NT + 1]), ps.v(ps.ap[0:rows, :]), AF.Identity)
                tmp = TMPC.v(TMPC.ap[0:rows, :])
                act(tmp, ps.v(ps.ap[0:rows, :]), AF.Identity, scale=OM.v(OM.ap[0:rows, cidx:cidx + 1]))
                cp(CARRY.v(CARRY.ap[0:rows, cidx:cidx + 1]), pfr.v(pf.ap[0:rows, NT:NT + 1]), "act")
                stt(out, pfr.v(pf.ap[0:rows, 0:NT]), mu, tmp, ALU.mult, ALU.add)

            def proj(slab, coff, ncols, width):
                ps = psf()
                for k in range(KD):
                    mm(ps.v(ps.ap[0:ncols, :]), slab.v(slab.ap[:, k * width + coff:k * width + coff + ncols]), HTcur[0][k],
                       start=(k == 0), stop=(k == KD - 1))
                return ps

            def c3(b):
                return b.ap.rearrange("p (c t) -> p c t", t=CH)

            def blk(buf, h2, c, w=CH):
                return buf.v(buf.ap[64 * h2:64 * h2 + 64, c * w:(c + 1) * w])

            LASTF = lambda h2, c: (h2 == 1 and c == NCH - 1)

            def rwkv_prep(hp, need_out, T, need_r, O):
                Rf, Kf, Vf = T["Rf"], T["Kf"], T["Vf"]
                if need_r:
                    slab = ws.get(winr[hp], KD * 128)
                    ps = proj(slab, 0, 128, 128)
                    shift_lerp(ps, 128, smc(c_mur + hp), hp, Rf)
                    yield
                slab = ws.get(winkv[hp], KD * 256)
                ps = proj(slab, 0, 128, 256)
                shift_lerp(ps, 128, smc(c_muk + hp), KR + hp, Kf)
                yield
                ps = proj(slab, 128, 128, 256)
                shift_lerp(ps, 128, smc(c_muv + hp), 2 * KR + hp, Vf)
                yield
                chs = slice(hp * 128, (hp + 1) * 128)
                LW, Af, GG = T["LW"], T["Af"], O["GG"]
                ps = psf()
                mm(ps, W2B.v(W2B.ap[:, chs]), TW)
                act(LW, ps, AF.Sigmoid, bias=smc(c_w0 + hp))
                tsc(LW, LW, WSCALE, ALU.mult)
                yield
                ps = psf()
                mm(ps, A2B.v(A2B.ap[:, chs]), XA)
                act(Af, ps, AF.Sigmoid, bias=smc(c_a0 + hp))
                yield
                if need_out:
                    ps = psf()
                    mm(ps, G2A.v(G2A.ap[:, chs]), SGA, start=True, stop=False)
                    mm(ps, G2B.v(G2B.ap[:, chs]), SGBb, start=False, stop=True)
                    act(GG, ps, AF.Identity)
                    yield
                KK, T1, T2 = T["KK"], T["T1"], T["T2"]
                tsc(KK, Kf, smc(c_kk + hp), ALU.mult)
                sq = SQ[rr["sq"] % 2]
                rr["sq"] += 1
                act(sq, KK, AF.Square)
                ps = psf()
                mm(ps, BONESB, sq)
                yield
                act(T1, ps, AF.Sqrt)
                tsc(T1, T1, 1e-12, ALU.max)
                yield
                recip(T1, T1)
                tt(KK, KK, T1, ALU.mult)
                yield
                KM, BE = T["KM"], T["BE"]
                tsc(T1, Af, smc(c_ka + hp), ALU.mult, derc(hp), ALU.add)
                tt(KM, Kf, T1, ALU.mult)
                yield
                tt(BE, KK, Af, ALU.mult)
                G, EG, EGM, EGX, EP = T["G"], T["EG"], T["EGM"], T["EGX"], T["EP"]
                scan(G, RMASK, LW)
                yield
                act(EG, G, AF.Exp)
                act(EGM, G, AF.Exp, scale=-1.0)
                tt(T2, G, LW, ALU.subtract)
                yield
                act(EGX, T2, AF.Exp)
                cp(O["EGC"], EG.v(c3(EG)[:, :, CH - 1:CH]))
                egc = O["EGC"].v(O["EGC"].ap.to_broadcast([128, NCH, CH]))
                tt(EP.v(c3(EP)), EGM.v(c3(EGM)), egc, ALU.mult)
                yield
                stt(O["AL"], KK, -1.0, EGX, ALU.mult, ALU.mult)
                tt(O["BM"], BE, EGM, ALU.mult)
                yield
                tt(O["KMm"], KM, EGM, ALU.mult)
                tt(O["BP"], BE, EP, ALU.mult)
                yield
                tt(O["KP"], KM, EP, ALU.mult)
                cp(O["VB"], Vf, "act")
                yield
                if need_out:
                    tt(O["RT"], Rf, EG, ALU.mult)
                    stt(T2, Rf, smc(c_rk + hp), KM, ALU.mult, ALU.mult)
                    pb_ = psf()
                    mm(pb_, BONES32, T2)
                    yield
                    tt(O["BON"], pb_, Vf, ALU.mult)
                    yield

            def rwkv_mm(hp, need_out, T, O):
                AL, BM, KMm, BP, KP, VB, RT = (O[n] for n in ("AL", "BM", "KMm", "BP", "KP", "VB", "RT"))

                def transposed(src, dst):
                    pb = psb()
                    for c in range(NCH):
                        for h2 in range(2):
                            tr(blk(pb, h2, c), blk(src, h2, c), IDB.v(IDB.ap[64 * h2:64 * h2 + 64, 64 * h2:64 * h2 + 64]), inc=LASTF(h2, c))
                    cp(dst, pb.v(pb.ap[:, 0:NT]), "act")

                Zb, Z32 = T["Zb"], T["Z32"]
                Zb3 = Zb.ap.rearrange("p (c j) -> p c j", j=2 * CH)
                Z323 = Z32.ap.rearrange("p (c j) -> p c j", j=2 * CH)

                def zw(h2, c):
                    return Zb.v(Zb.ap[64 * h2:64 * h2 + 64, c * 2 * CH:c * 2 * CH + CH])

                def zu(h2, c):
                    return Zb.v(Zb.ap[64 * h2:64 * h2 + 64, c * 2 * CH + CH:(c + 1) * 2 * CH])

                ALt, BPt, KPt, Vt = T["ALt"], T["BPt"], T["KPt"], T["Vt"]
                transposed(AL, ALt)
                yield
                transposed(VB, Vt)
                yield

                def prod(lhs, rhs, mask, dst, eng="dve"):
                    ps = psf()
                    for c in range(NCH):
                        for h2 in range(2):
                            mm(blk(ps, h2, c), blk(lhs, h2, c), blk(rhs, h2, c), inc=LASTF(h2, c))
                    if mask is None:
                        if eng == "act":
                            act(dst, ps, AF.Identity)
                        else:
                            cp(dst, ps)
                    else:
                        tt(dst, ps, mask, ALU.mult)

                PT, PN = [T["PTa"], T["PTb"]], [T["PNa"], T["PNb"]]
                AKT, ARBT, ARKT = T["AKT"], T["ARBT"], T["ARKT"]
                prod(KMm, AL, MU_S, AKT)
                yield
                prod(BM, AL, MU_S, PT[0])
                yield
                prod(AL, BM, ML_S, PN[0])
                yield
                ps = psf()
                for c in range(NCH):
                    for h2 in range(2):
                        mm(blk(ps, h2, c), blk(AKT, h2, c), blk(Vt, h2, c), inc=LASTF(h2, c))
                cp(Zb.v(Zb3[:, :, 0:CH]), ALt.v(c3(ALt)), "act")
                act(Zb.v(Zb3[:, :, CH:2 * CH]), ps.v(c3(ps)), AF.Identity)
                PTI = T["AKT"]
                tt(PTI, PT[0], IDSB, ALU.add)
                yield
                transposed(BP, BPt)
                yield
                transposed(KP, KPt)
                yield
                if need_out:
                    prod(BM, RT, MU_I, ARBT)
                    yield
                    prod(KMm, RT, MU_I, ARKT)
                    yield
                cur = 0
                for lvl in range(6):
                    psa, psb_ = psf(), psf()
                    for c in range(NCH):
                        pz = psa if c < NCH // 2 else psb_
                        cc_ = c % (NCH // 2)
                        for h2 in range(2):
                            mm(pz.v(pz.ap[64 * h2:64 * h2 + 64, cc_ * 2 * CH:(cc_ + 1) * 2 * CH]), blk(PTI, h2, c),
                               Zb.v(Zb.ap[64 * h2:64 * h2 + 64, c * 2 * CH:(c + 1) * 2 * CH]),
                               inc=(h2 == 1 and (c == NCH // 2 - 1 or c == NCH - 1)))
                    nxt = 1 - cur
                    if lvl < 5:
                        psq = psf()
                        for c in range(NCH):
                            for h2 in range(2):
                                mm(blk(psq, h2, c), blk(PN[cur], h2, c), blk(PT[cur], h2, c), inc=LASTF(h2, c))
                        if lvl < 4:
                            psn = psf()
                            for c in range(NCH):
                                for h2 in range(2):
                                    mm(blk(psn, h2, c), blk(PT[cur], h2, c), blk(PN[cur], h2, c), inc=LASTF(h2, c))
                    act(Zb.v(Zb.ap[:, 0:NT]), psa, AF.Identity)
                    if lvl < 5:
                        tt(PTI, psq, IDSB, ALU.add)
                    act(Zb.v(Zb.ap[:, NT:2 * NT]), psb_, AF.Identity)
                    yield
                    if lvl < 4:
                        act(PT[nxt], psq, AF.Identity)
                        act(PN[nxt], psn, AF.Identity)
                        cur = nxt
                        yield
                PMT, QT, RH, DG = T["PMT"], T["QT"], T["RH"], T["DG"]
                egc = O["EGC"].v(O["EGC"].ap.to_broadcast([128, NCH, CH]))
                tt(DG.v(c3(DG)), IDS.v(c3(IDS)), egc, ALU.mult)
                ps = psf()
                for c in range(NCH):
                    for h2 in range(2):
                        mm(blk(ps, h2, c), zw(h2, c), blk(BPt, h2, c), inc=LASTF(h2, c))
                ps2 = psf()
                for c in range(NCH):
                    for h2 in range(2):
                        mm(blk(ps2, h2, c), blk(BPt, h2, c), zu(h2, c), start=True, stop=False)
                        mm(blk(ps2, h2, c), blk(KPt, h2, c), blk(Vt, h2, c), start=False, stop=True, inc=LASTF(h2, c))
                tt(PMT, ps, DG, ALU.add)
                QTb = B(QT.ap.bitcast(BF16)[:, 0:NT], QT.key)
                act(QTb, ps2, AF.Identity)
                yield
                STA = T["STA"]
                cp(STA.v(STA.ap[:, 0:CH]), STR[hp])
                for c in range(NCH):
                    ps = psf()
                    for h2 in range(2):
                        mm(ps.v(ps.ap[64 * h2:64 * h2 + 64, 0:CH]), blk(PMT, h2, c), blk(STA, h2, c), start=True, stop=False)
                        mm(ps.v(ps.ap[64 * h2:64 * h2 + 64, 0:CH]), IDB.v(IDB.ap[64 * h2:64 * h2 + 64, 64 * h2:64 * h2 + 64]), blk(QTb, h2, c),
                           start=False, stop=True, inc=(h2 == 1))
                    if c == 0 and need_out:
                        ps3 = psf()
                        for cc in range(NCH):
                            for h2 in range(2):
                                mm(blk(ps3, h2, cc), zw(h2, cc), blk(ARBT, h2, cc), inc=LASTF(h2, cc))
                    dst = STA.v(STA.ap[:, (c + 1) * CH:(c + 2) * CH]) if c < NCH - 1 else STR[hp]
                    act(dst, ps.v(ps.ap[:, 0:CH]), AF.Identity)
                    if c == 0 and need_out:
                        tt(RH, ps3, RT, ALU.add)
                    yield
                if not need_out:
                    return
                psy = psf()
                for c in range(NCH):
                    for h2 in range(2):
                        mm(blk(psy, h2, c), blk(STA, h2, c), blk(RH, h2, c), start=True, stop=False)
                        mm(blk(psy, h2, c), zu(h2, c), blk(ARBT, h2, c), start=False, stop=False)
                        mm(blk(psy, h2, c), blk(Vt, h2, c), blk(ARKT, h2, c), start=False, stop=True, inc=LASTF(h2, c))
                Yf, YQ, Yc, VR = Z32.v(Z32.ap[:, 0:NT]), Z32.v(Z32.ap[:, NT:2 * NT]), T["QT"], T["DG"]
                act(Yf, psy, AF.Identity)
                yield
                pc = psf()
                mm(pc, CEN32, Yf)
                act(Yc, pc, AF.Identity)
                act(YQ, pc, AF.Square)
                yield
                pq = psf()
                mm(pq, BONV, YQ)
                act(VR, pq, AF.Ln, bias=smc_gn)
                yield
                act(VR, VR, AF.Exp, scale=-0.5)
                tt(Yc, Yc, VR, ALU.mult)
                yield
                tsc(Yf, Yc, smc(c_lnw + hp), ALU.mult, smc(c_lnb + hp), ALU.add)
                yield
                tt(Yf, Yf, O["BON"], ALU.add)
                tt(YT[hp], Yf, O["GG"], ALU.mult)
                yield

            def interleave(ga, gb):
                gens = [g for g in (ga, gb) if g is not None]
                while gens:
                    for g in list(gens):
                        try:
                            next(g)
                        except StopIteration:
                            gens.remove(g)

            def rwkv_all(need_out, T, need_r):
                for r in range(KR + 1):
                    gp = rwkv_prep(r, need_out, T, need_r, OSET[r % 2]) if r < KR else None
                    gm = rwkv_mm(r - 1, need_out, T, OSET[(r - 1) % 2]) if r > 0 else None
                    interleave(gm, gp)

            def hgrn_prep(hh, T, O):
                Qf, FG, LF, KG = T["Rf"], T["Kf"], T["LW"], T["KM"]
                slab = ws.get(winqg[hh], KD * 256)
                ps = proj(slab, 0, 128, 256)
                act(Qf, ps, AF.Silu)
                yield
                ps = proj(slab, 128, 128, 256)
                act(O["GS"], ps, AF.Silu)
                yield
                slab = ws.get(winfi[hh], KD * 256)
                ps = proj(slab, 0, 128, 256)
                act(FG, ps, AF.Sigmoid)
                tsc(FG, FG, derc(2 * KR + hh), ALU.mult, derc(KR + hh), ALU.add)
                yield
                act(LF, FG, AF.Ln)
                tsc(KG, FG, -1.0, ALU.mult, 1.0, ALU.add)
                ps = proj(slab, 128, 128, 256)
                act(O["VB"], ps, AF.Identity)
                yield
                G, EG, EGM, EP = T["G"], T["EG"], T["EGM"], T["EP"]
                scan(G, RMASK, LF)
                yield
                act(EG, G, AF.Exp)
                act(EGM, G, AF.Exp, scale=-1.0)
                yield
                cp(O["EGC"], EG.v(c3(EG)[:, :, CH - 1:CH]))
                egc = O["EGC"].v(O["EGC"].ap.to_broadcast([128, NCH, CH]))
                tt(EP.v(c3(EP)), EGM.v(c3(EGM)), egc, ALU.mult)
                tt(O["KMm"], KG, EGM, ALU.mult)
                yield
                tt(O["KP"], KG, EP, ALU.mult)
                tt(O["QTl"], Qf, EG, ALU.mult)
                yield

            def hgrn_mm(hh, T, O):
                QTl, KMm, KP, VB, GS = O["QTl"], O["KMm"], O["KP"], O["VB"], O["GS"]
                KPt, Vt = T["KPth"], T["Vth"]
                for src, dst in ((KP, KPt), (VB, Vt)):
                    pb = psb()
                    for c in range(NCH):
                        tr(pb.v(pb.ap[0:64, c * 128:(c + 1) * 128]), src.v(src.ap[:, c * CH:(c + 1) * CH]), IDB, inc=(c == NCH - 1))
                    cp(dst.v(dst.ap[0:64, :]), pb.v(pb.ap[0:64, :]), "act")
                    yield
                ATm = T["ATm"]
                ps = psf()
                for c in range(NCH):
                    mm(ps.v(ps.ap[0:64, c * CH:(c + 1) * CH]), KMm.v(KMm.ap[:, c * CH:(c + 1) * CH]), QTl.v(QTl.ap[:, c * CH:(c + 1) * CH]), inc=(c == NCH - 1))
                tt(ATm.v(ATm.ap[0:64, :]), ps.v(ps.ap[0:64, :]), MU_I.v(MU_I.ap[0:64, :]), ALU.mult)
                yield
                SALL = T["SALL"]
                cp(SALL.v(SALL.ap[:, 0:128]), SGB[hh], "act")
                for c in range(NCH):
                    ps = psf()
                    mm(ps.v(ps.ap[:, 0:128]), KPt.v(KPt.ap[0:64, c * 128:(c + 1) * 128]), Vt.v(Vt.ap[0:64, c * 128:(c + 1) * 128]))
                    dst = SALL.v(SALL.ap[:, (c + 1) * 128:(c + 2) * 128]) if c < NCH - 1 else SGB[hh]
                    stt(dst, SALL.v(SALL.ap[:, c * 128:(c + 1) * 128]), O["EGC"].v(O["EGC"].ap[:, c, :]), ps.v(ps.ap[:, 0:128]), ALU.mult, ALU.add)
                    yield
                pso = psf()
                for c in range(NCH):
                    oc = pso.v(pso.ap[:, c * CH:(c + 1) * CH])
                    mm(oc, SALL.v(SALL.ap[:, c * 128:(c + 1) * 128]), QTl.v(QTl.ap[:, c * CH:(c + 1) * CH]), start=True, stop=False)
                    mm(oc, Vt.v(Vt.ap[0:64, c * 128:(c + 1) * 128]), ATm.v(ATm.ap[0:64, c * CH:(c + 1) * CH]), start=False, stop=True, inc=(c == NCH - 1))
                Z32 = T["Z32"]
                OQ, T1 = Z32.v(Z32.ap[:, 0:NT]), Z32.v(Z32.ap[:, NT:2 * NT])
                act(OQ, pso, AF.Square)
                yield
                pn = psf()
                mm(pn, ONES32, OQ)
                act(T1, pn, AF.Ln, scale=1.0 / 128, bias=smc_eps)
                yield
                act(T1, T1, AF.Exp, scale=-0.5)
                stt(T1, pso, smc(c_hn + hh), T1, ALU.mult, ALU.mult)
                yield
                tt(YT[KR + hh], T1, GS, ALU.mult)
                yield

            def xslab_prep(t):
                HTcur[0] = HTB[t % 2]
                slab = ws.get(winx[0], KD * 288)
                ps = proj(slab, 0, 64, 288)
                t64 = T["T1"].v(T["T1"].ap[0:64, :])
                shift_lerp(ps, 64, smc(c_mxw, 64), 3 * KR, t64)
                act(TW, t64, AF.Tanh)
                yield
                ps = proj(slab, 64, 64, 288)
                shift_lerp(ps, 64, smc(c_mxa, 64), 3 * KR + 1, t64)
                cp(XA, t64, "act")
                yield
                ps = proj(slab, 128, 128, 288)
                shift_lerp(ps, 128, smc(c_mga), 3 * KR + 2, T["T1"])
                act(SGA, T["T1"], AF.Sigmoid)
                yield
                ps = proj(slab, 256, 32, 288)
                t32 = T["T1"].v(T["T1"].ap[0:32, :])
                shift_lerp(ps, 32, smc(c_mgb, 32), 3 * KR + 3, t32)
                act(SGBb, t32, AF.Sigmoid)
                yield

            def chain_gens(*gs):
                for g in gs:
                    for _ in g:
                        yield

            def phase_b(load_h, finish_tile):
                units = []
                for t in range(NTB):
                    units += [("r", t, h) for h in range(KR)] + [("h", t, h) for h in range(KR)]
                nu = len(units)

                def mk_prep(i):
                    k, t, h = units[i]
                    if k == "r":
                        g = rwkv_prep(h, True, T, True, OSET[i % 2])
                        if h == 0:
                            if t + 1 < NTB:
                                load_h(t + 1)
                            g = chain_gens(xslab_prep(t), g)
                        return g
                    return hgrn_prep(h, T, OHS[i % 2])

                def mk_mm(i):
                    k, t, h = units[i]
                    return rwkv_mm(h, True, T, OSET[i % 2]) if k == "r" else hgrn_mm(h, T, OHS[i % 2])

                for r in range(nu + 1):
                    gp = mk_prep(r) if r < nu else None
                    gm = mk_mm(r - 1) if r > 0 else None
                    interleave(gm, gp)
                    if r > 0 and units[r - 1][0] == "h" and units[r - 1][2] == KR - 1:
                        finish_tile(units[r - 1][1])

            EPSC = B(sb("EPSC", [128, 2])[:], "EPSC")
            memset(EPSC.v(EPSC.ap[:, 0:1]), RMS_EPS)
            memset(EPSC.v(EPSC.ap[:, 1:2]), GN_EPS)
            smc_eps = EPSC.v(EPSC.ap[:, 0:1])
            smc_gn = EPSC.v(EPSC.ap[:, 1:2])

            T = {}
            for n in ("Rf", "Kf", "Vf", "LW", "Af", "KK", "T1", "T2", "KM", "G", "EG", "EGM", "Z32a", "Z32b", "QT", "DG"):
                T[n] = f32t(n)
            ZBOFF = [0]
            for n in ("ALt", "BPt", "KPt", "Vt", "Zba", "Zbb", "AKT", "PMT", "RH", "STA"):
                if n == "Zba":
                    ZBOFF[0] = arena_off[0]
                T[n] = b16t(n)
            for pair, nm in ((("PTa", "PTb"), "KPth"), (("PNa", "PNb"), "Vth"), (("ARBT", "ARKT"), "SALL")):
                o0 = arena_off[0]
                T[pair[0]] = b16t(pair[0])
                T[pair[1]] = b16t(pair[1])
                T[pair[1]] = B(T[pair[1]].ap, pair[0])
                T[nm] = B(ARt[:, o0:o0 + NT].bitcast(BF16), pair[0])
            OSET = []
            for si in range(2):
                Od = {}
                for n in ("AL", "BM", "KMm", "BP", "KP", "VB", "RT", "GG"):
                    Od[n] = b16t("%s_%d" % (n, si))
                Od["BON"] = f32t("BON_%d" % si)
                Od["EGC"] = B(sb("EGC%d" % si, [128, NCH, 1])[:], "EGC%d" % si)
                OSET.append(Od)
            oz = [i for i, n in enumerate(mix_keys) if n == "Z32a"][0]
            T["Z32"] = B(ARt[:, oz * NT:(oz + 2) * NT], "Z32a")
            zb0 = T["Zba"].ap
            T["Zb"] = B(ARt[:, ZBOFF[0]:ZBOFF[0] + NT].bitcast(BF16), "Zba")
            OHS = []
            ohk = []
            for si in range(2):
                Od = {}
                for j, n in enumerate(("QTl", "KMm", "KP", "VB", "GS")):
                    if XW // 2 >= 10 * (NT // 2):
                        o_ = XW // 2 + (si * 5 + j) * (NT // 2)
                        Od[n] = B(XTt[:, o_:o_ + NT // 2].bitcast(BF16), "OH%s_%d" % (n, si))
                    else:
                        Od[n] = B(sb("OH%s_%d" % (n, si), [128, NT], BF16)[:], "OH%s_%d" % (n, si))
                    ohk.append(Od[n].key)
                Od["EGC"] = B(sb("EGCH%d" % si, [128, NCH, 1])[:], "EGCH%d" % si)
                OHS.append(Od)
            T["BE"], T["EGX"], T["EP"] = T["Af"], T["T2"], T["G"]
            T["GG"] = OSET[0]["GG"]
            T["VBh"], T["QTl"], T["KMmh"], T["KPh"], T["ATm"] = OSET[0]["VB"], OSET[0]["RT"], OSET[0]["KMm"], OSET[0]["KP"], T["AKT"]
            at_keys = [a.key for a in AT]

            half = XW // 2
            xk0 = [x.key for x in XT[0:KD // 2]]
            xk1 = [x.key for x in XT[KD // 2:KD]]
            htk = [h.key for h in HT]
            hxk = [h.key for h in HX]
            ytk = [y.key for y in YT]
            groups = [[0, 1], [2, 3], [4, 5], [6, 7]]
            ATW_ = KF * NT // 2
            n_top = min(KD, (ARW - ATW_) // NT)
            assert (KD - n_top) * NT <= YW // 2
            X2 = [B(ARt[:, ATW_ + k * NT:ATW_ + (k + 1) * NT], "X2_%d" % k) for k in range(n_top)] + \
                 [B(YTt[:, :].bitcast(F32)[:, (k - n_top) * NT:(k - n_top + 1) * NT], "X2_%d" % k) for k in range(n_top, KD)]
            x2k = [x.key for x in X2]
            nq = 4 if KD % 4 == 0 else 1
            qw = XW // nq
            def load_x(tj):
                for q in range(nq):
                    P.dma("sp", XTt[:, q * qw:(q + 1) * qw], xmain[tj, :, q * qw:(q + 1) * qw],
                          writes=[x.key for x in XT[q * (KD // nq):(q + 1) * (KD // nq)]])

            load_x(0)
            for ti in range(ntm):
                XSRC[0] = XT
                rmsnorm(c_g1, HT)
                XSRC[0] = X2
                ffn(0, sumsq_chunk, dst=X2)
                if ti + 1 < ntm:
                    load_x(ti + 1)
                P.dma("sp", X1[ti * 128:(ti + 1) * 128, 0:n_top * NT], ARt[:, ATW_:ATW_ + n_top * NT], reads=x2k[0:n_top], writes=["X1a%d" % ti])
                if n_top < KD:
                    P.dma("sp", X1[ti * 128:(ti + 1) * 128, n_top * NT:XW], YTt[:, :].bitcast(F32)[:, 0:(KD - n_top) * NT], reads=x2k[n_top:], writes=["X1b%d" % ti])
                norm_apply(c_gm, AT[0:KD])
                P.dma("sp", HBin[ti][:, :], ATv[:, 0:XW], reads=[a.key for a in AT[0:KD]], writes=["HBin%d" % ti])
                P.collective(HBin[ti], HBall[ti], groups, reads=["HBin%d" % ti], writes=["HBall%d" % ti])
            XSRC[0] = XT
            P.transfer(at_keys + x2k, mix_keys + ytk)
            P.transfer(xk0 + xk1, hxk + ohk)

            def load_h(t):
                buf = HTB[t % 2]
                dst = HTt[:, :] if t % 2 == 0 else XTt[:, 0:half].bitcast(BF16)
                ci, ro = (t, 0) if t < ntm else (t - ntm, 128)
                P.dma("sp", dst, HBall[ci][ro:ro + 128, :], reads=["HBall%d" % ci], writes=[b.key for b in buf])

            if cfg.stage >= 2:
                load_h(0)

                def finish_tile(t):
                    P.dma("sp", YBin[t][:, :], YTt[:, :], reads=ytk, writes=["YBin%d" % t])
                    P.collective(YBin[t], YBall[t], groups, reads=["YBin%d" % t], writes=["YBall%d" % t])

                phase_b(load_h, finish_tile)
            P.transfer(hxk + ohk, xk0 + xk1)
            HTcur[0] = HT
            ATW = KF * NT // 2
            if ARW - ATW >= 3 * (YW // 2):
                yl_aps = [ARt[:, ATW + i * (YW // 2):ATW + (i + 1) * (YW // 2)].bitcast(BF16) for i in range(3)] + [YTt[:, :]]
                yl_pref = True
            else:
                yl_aps = [ARt[:, i * (YW // 2):(i + 1) * (YW // 2)].bitcast(BF16) for i in range(4)]
                yl_pref = False
            YL = [[B(yl_aps[2 * r + h], "YL%d%d" % (r, h)) for h in range(2)] for r in range(2)]
            ylk = [YL[r][h].key for r in range(2) for h in range(2)]
            for ti in range(ntm):
                for q in range(nq):
                    P.dma("sp", XTt[:, q * qw:(q + 1) * qw], X1[ti * 128:(ti + 1) * 128, q * qw:(q + 1) * qw],
                          reads=["X1a%d" % ti, "X1b%d" % ti], writes=[x.key for x in XT[q * (KD // nq):(q + 1) * (KD // nq)]])
                def load_y(tj):
                    for r in range(2):
                        for h in range(2):
                            tq = h * ntm + tj
                            P.dma("sp", YL[r][h].ap, YBall[tq][r * 128:(r + 1) * 128, :], reads=["YBall%d" % tq], writes=[YL[r][h].key])

                if ti == 0:
                    P.transfer(at_keys + mix_keys + ytk, ylk)
                    load_y(0)
                elif not yl_pref:
                    P.transfer(at_keys + mix_keys, ylk)
                    load_y(ti)
                for r in range(2):
                    for j in range(2 * KR):
                        kglob = (r * KR + j) if j < KR else (2 * KR + r * KR + (j - KR))
                        ya = YL[r][0].v(YL[r][0].ap[:, j * NT:(j + 1) * NT])
                        yb = YL[r][1].v(YL[r][1].ap[:, j * NT:(j + 1) * NT])
                        tsc(SQ[0], ya, smc(c_1ms), ALU.mult)
                        stt(HT[kglob], yb, smc(c_s), SQ[0], ALU.mult, ALU.add)
                if yl_pref and ti + 1 < ntm:
                    load_y(ti + 1)
                if cfg.stage >= 5:
                    for sl in range(cfg.NSO):
                        slab = ws.get(wout[sl], KD * 256)
                        for j in range(2):
                            dch = sl * 2 + j
                            po = psf()
                            for k in range(KD):
                                mm(po, slab.v(slab.ap[:, k * 256 + j * 128:k * 256 + (j + 1) * 128]), HT[k], start=(k == 0), stop=(k == KD - 1))
                            if dch > 0:
                                sumsq_chunk(dch - 1)
                            tt(XT[dch], XT[dch], po, ALU.add)
                    sumsq_chunk(KD - 1)
                if not yl_pref:
                    P.transfer(ylk, at_keys)
                norm_apply(c_g2, HT)
                ffn(1, sumsq_chunk)
                def store_q(k, ti=ti):
                    cq = KD // nq
                    if k % cq == cq - 1:
                        q = k // cq
                        out_toks.append(P.dma("sp", outT[ti, :, q * qw:(q + 1) * qw], XTt[:, q * qw:(q + 1) * qw],
                                              reads=[x.key for x in XT[q * cq:(q + 1) * cq]]))

                norm_apply(c_gf, inplace=True, on_done=store_q)
            P.wait_all("sp", out_toks)
            if dry:
                saved_reqs = ws.reqs
                n_dry = P.ninstr
            else:
                print("instructions:", P.ninstr, "weight slabs:", len(ws.reqs), "engine counts:", P.cnt)
    return nc


def _cols(v, n):
    return np.ascontiguousarray(np.asarray(v, np.float32).reshape(n, 128).T)


def _slab(W, c0, width):
    K = W.shape[0]
    blk = W[:, c0:c0 + width].reshape(K // 128, 128, width).transpose(1, 0, 2)
    return np.ascontiguousarray(blk).reshape(128, (K // 128) * width)


def _slabs_cols(W, cols_list):
    out = []
    K = W.shape[0]
    for ranges in cols_list:
        parts = [W[:, c0:c0 + w] for c0, w in ranges]
        Wc = np.concatenate(parts, axis=1)
        out.append(_slab(Wc, 0, Wc.shape[1]))
    return np.stack(out)


def prepare_weights(cfg, inp):
    D, F, KD, KF, C, KR = cfg.D, cfg.F, cfg.KD, cfg.KF, cfg.C, cfg.KR
    CL, KRL = C // 2, KR // 2
    m = {}
    for i, tag in ((1, "ffn1"), (2, "ffn2")):
        Wg = np.asarray(inp[tag + "_w_gate"][0], np.float32)
        Wu = np.asarray(inp[tag + "_w_up"][0], np.float32)
        Wd = np.asarray(inp[tag + "_w_down"][0], np.float32)
        m["wg%d" % i] = np.stack([_slab(Wg, s * 256, 256) for s in range(cfg.NSF)])
        m["wu%d" % i] = np.stack([_slab(Wu, s * 256, 256) for s in range(cfg.NSF)])
        m["wd%d" % i] = np.stack([_slab(Wd, d * 128, 128) for d in range(KD)])
    Win = np.asarray(inp["w_in"][0], np.float32)
    m["winx"] = _slabs_cols(Win, [[(3 * C, 288)]])
    Wo = np.asarray(inp["w_out"][0], np.float32)
    m["wout"] = np.stack([_slab(Wo, s * 256, 256) for s in range(cfg.NSO)])
    R0 = 3 * C + 288
    mu = np.asarray(inp["rwkv_mu"][0], np.float32)
    w2 = np.asarray(inp["rwkv_w2"][0], np.float32)
    a2 = np.asarray(inp["rwkv_a2"][0], np.float32)
    g2 = np.asarray(inp["rwkv_g2"][0], np.float32)
    per = []
    for s in range(2):
        p = {}
        prs = range(s * KRL, (s + 1) * KRL)
        p["winr"] = _slabs_cols(Win, [[(hp * 128, 128)] for hp in prs])
        p["winkv"] = _slabs_cols(Win, [[(C + hp * 128, 128), (2 * C + hp * 128, 128)] for hp in prs])
        p["winqg"] = _slabs_cols(Win, [[(R0 + hh * 128, 128), (R0 + 3 * C + hh * 128, 128)] for hh in prs])
        p["winfi"] = _slabs_cols(Win, [[(R0 + C + hh * 128, 128), (R0 + 2 * C + hh * 128, 128)] for hh in prs])
        ch = slice(s * CL, (s + 1) * CL)

        def lc(v):
            return _cols(np.asarray(v, np.float32).reshape(-1)[ch], KRL)

        cols = [_cols(inp["ffn1_norm"][0], KD), _cols(inp["mix_norm"][0], KD), _cols(inp["ffn2_norm"][0], KD),
                _cols(inp["final_norm"], KD),
                lc(mu[0:C]), lc(mu[C:2 * C]), lc(mu[2 * C:3 * C]),
                lc(inp["rwkv_w0"][0]), lc(inp["rwkv_a0"][0]), lc(inp["rwkv_k_k"][0]),
                lc(inp["rwkv_k_a"][0]), lc(inp["rwkv_r_k"][0]),
                lc(inp["rwkv_ln_w"][0]), lc(inp["rwkv_ln_b"][0]),
                lc(inp["hgrn_lb_logits"][0]), lc(inp["hgrn_lb_logits"][1]), lc(inp["hgrn_norm"][0])]
        extra = np.zeros((128, 6), np.float32)
        extra[0:64, 0] = mu[3 * C:3 * C + 64]
        extra[0:64, 1] = mu[3 * C + 64:3 * C + 128]
        extra[0:128, 2] = mu[3 * C + 128:3 * C + 256]
        extra[0:32, 3] = mu[3 * C + 256:3 * C + 288]
        extra[:, 4] = float(s)
        extra[:, 5] = float(1 - s)
        p["smalls"] = np.ascontiguousarray(np.concatenate(cols + [extra], axis=1))
        p["w2"] = np.ascontiguousarray(w2[:, ch])
        p["a2"] = np.ascontiguousarray(a2[:, ch])
        p["g2"] = np.ascontiguousarray(g2[:, ch])
        per.append(p)
    return m, per


def _tiles(xseg, KD):
    Tn = xseg.shape[0]
    a = xseg.reshape(Tn // NT, NT, KD, 128).transpose(0, 3, 2, 1)
    return np.ascontiguousarray(a).reshape(Tn // NT, 128, KD * NT)


def run(cfg, inp, debug=False):
    x = np.asarray(inp["x"], np.float32)
    Bn, Tn, D = x.shape
    assert Bn * 2 == 8 and Tn // 2 == cfg.ntm * NT
    wm, per = prepare_weights(cfg, inp)
    nc = build_program(cfg, debug=debug)
    in_maps = []
    hl = Tn // 2
    for c in range(8):
        b, s = c // 2, c % 2
        d = dict(wm)
        d.update(per[s])
        d["xmain"] = _tiles(x[b, s * hl:(s + 1) * hl], cfg.KD)
        in_maps.append(d)
    res = run_bass_kernel_spmd(nc, in_maps, core_ids=list(range(8)))
    out = np.empty((Bn, Tn, D), np.float32)
    for c in range(8):
        b, s = c // 2, c % 2
        o = res.results[c]["outT"].reshape(cfg.ntm, 128, cfg.KD, NT).transpose(0, 3, 2, 1).reshape(hl, D)
        out[b, s * hl:(s + 1) * hl] = o
    return out


def kernel(**inputs):
    cfg = Cfg(D=2048, F=5632, ntp=8, ntm=8)
    return run(cfg, inputs)
```

```python
from contextlib import ExitStack
import numpy as np
import concourse.bass as bass
import concourse.mybir as mybir
from concourse.bass_utils import run_bass_kernel_spmd

F32 = mybir.dt.float32
BF16 = mybir.dt.bfloat16
AF = mybir.ActivationFunctionType
ALU = mybir.AluOpType

NT = 512
CH = 64
NCH = NT // CH
NSLOT = 4
SLOT = 5632
RMS_EPS = 1e-6
GN_EPS = 64e-5
WSCALE = -0.6065306597126334


class Cfg:
    def __init__(self, D=2048, F=5632, ntp=8, ntm=8, stage=99):
        self.D, self.F, self.ntp, self.ntm, self.stage = D, F, ntp, ntm, stage
        self.KD = D // 128
        self.KF = F // 128
        self.C = D // 2
        self.KR = self.C // 128
        self.NSF = F // 256
        self.NSO = D // 256


class B:
    __slots__ = ("ap", "key")

    def __init__(self, ap, key):
        self.ap, self.key = ap, key

    def __getitem__(self, idx):
        return B(self.ap[idx], self.key)

    def v(self, ap):
        return B(ap, self.key)


class Prog:
    def __init__(self, nc, st, dry):
        self.nc, self.dry = nc, dry
        self.eng = {"pe": nc.tensor, "act": nc.scalar, "dve": nc.vector, "pool": nc.gpsimd, "sp": nc.sync}
        self.semobj = {}
        self.cnt = {k: 0 for k in self.eng}
        self.seen = {k: {} for k in self.eng}
        self.dq = {"sp": ["d%d" % i for i in range(8)], "pool": ["e%d" % i for i in range(6)]}
        self.dcnt = {n: 0 for q in self.dq.values() for n in q}
        self.dnext = {"sp": 0, "pool": 0}
        if not dry:
            for k in self.eng:
                self.semobj["s_" + k] = st.enter_context(nc.semaphore("s_" + k))
            for q in self.dq.values():
                for n in q:
                    self.semobj[n] = st.enter_context(nc.semaphore(n))
            self.semobj["cc"] = st.enter_context(nc.semaphore("cc"))
        self.ncc = 0
        self.bufs = {}
        self.ninstr = 0
        self.pe_open = False

    def _wait(self, e, sname, val):
        if e == "pe" and sname == "s_pe":
            return
        if self.seen[e].get(sname, 0) >= val:
            return
        self.eng[e].wait_ge(self.semobj[sname], val)
        self.seen[e][sname] = val

    def _deps(self, e, reads, writes):
        for k in reads:
            b = self.bufs.get(k)
            if b and b[0]:
                self._wait(e, b[0][0], b[0][1])
        for k in writes:
            b = self.bufs.get(k)
            if b:
                if b[0]:
                    self._wait(e, b[0][0], b[0][1])
                for s, v in b[1].items():
                    self._wait(e, s, v)

    def _mark(self, tok, reads, writes):
        for k in reads:
            b = self.bufs.get(k)
            if b is None:
                b = self.bufs[k] = [None, {}]
            if b[1].get(tok[0], 0) < tok[1]:
                b[1][tok[0]] = tok[1]
        for k in writes:
            self.bufs[k] = [tok, {}]

    def op(self, e, fn, reads=(), writes=(), inc=True):
        self.ninstr += 1
        if self.dry:
            return None
        if e != "pe":
            assert not self.pe_open, "other-engine op emitted inside an open PE group"
            extra = [k for k in reads if k.startswith("ps")]
            if extra:
                writes = list(writes) + extra
        self._deps(e, reads, writes)
        ins = fn(self.eng[e])
        if inc:
            self.cnt[e] += 1
            ins.then_inc(self.semobj["s_" + e], 1)
            tok = ("s_" + e, self.cnt[e])
            if e == "pe":
                self.pe_open = False
        else:
            assert e == "pe"
            tok = ("s_" + e, self.cnt[e] + 1)
            self.pe_open = True
        self._mark(tok, reads, writes)
        return tok

    def dma(self, q, out, in_, reads=(), writes=()):
        self.ninstr += 1
        if self.dry:
            return None
        assert not self.pe_open
        names = self.dq[q]
        n = names[self.dnext[q]]
        self.dnext[q] = (self.dnext[q] + 1) % len(names)
        if self.dcnt[n] > 0:
            self._wait(q, n, 16 * self.dcnt[n])
        self._deps(q, reads, writes)
        ins = self.eng[q].dma_start(out=out, in_=in_)
        self.dcnt[n] += 1
        ins.then_inc(self.semobj[n], 16)
        tok = (n, 16 * self.dcnt[n])
        self._mark(tok, reads, writes)
        return tok

    def collective(self, in_t, out_t, groups, reads, writes):
        self.ninstr += 1
        if self.dry:
            return None
        assert not self.pe_open
        self._deps("pool", reads, writes)
        ins = self.eng["pool"].collective_compute("AllGather", ALU.bypass, replica_groups=groups,
                                                  ins=[in_t.ap().opt()], outs=[out_t.ap().opt()])
        self.ncc += 1
        ins.then_inc(self.semobj["cc"], 1)
        tok = ("cc", self.ncc)
        self._mark(tok, reads, writes)
        return tok

    def transfer(self, from_keys, to_keys):
        if self.dry:
            return
        acc = {}
        for k in from_keys:
            b = self.bufs.get(k)
            if not b:
                continue
            if b[0]:
                acc[b[0][0]] = max(acc.get(b[0][0], 0), b[0][1])
            for s, v in b[1].items():
                acc[s] = max(acc.get(s, 0), v)
        for k in to_keys:
            b = self.bufs.get(k)
            if b is None:
                b = self.bufs[k] = [None, {}]
            for s, v in acc.items():
                if b[1].get(s, 0) < v:
                    b[1][s] = v

    def wait_all(self, e, toks):
        if self.dry:
            return
        for t in toks:
            if t:
                self._wait(e, t[0], t[1])


class WStream:
    def __init__(self, P, slots):
        self.P, self.slots = P, slots
        self.reqs = []
        self.pos = 0
        self.loaded = 0

    def get(self, dram_ap, nelem):
        if self.P.dry:
            self.reqs.append((dram_ap, nelem))
            return self.slots[0]
        idx = self.pos
        self.pos += 1
        lim = min(len(self.reqs), idx + NSLOT - 1)
        while self.loaded < lim:
            j = self.loaded
            ap, n = self.reqs[j]
            s = self.slots[j % NSLOT]
            self.P.dma("pool", s.ap[:, 0:n], ap, writes=[s.key])
            self.loaded += 1
        return self.slots[idx % NSLOT]


def build_program(cfg, debug=False):
    nc = bass.Bass("TRN2", target_bir_lowering=False)
    D, F, KD, KF = cfg.D, cfg.F, cfg.KD, cfg.KF
    C, KR = cfg.C // 2, cfg.KR // 2
    ntm = cfg.ntm
    NTB = 2 * ntm
    XW = KD * NT
    YW = 2 * KR * NT

    def din(name, shape):
        return nc.dram_tensor(name, shape, F32, kind="ExternalInput").ap()

    xmain = din("xmain", [ntm, 128, XW])
    outT = nc.dram_tensor("outT", [ntm, 128, XW], F32, kind="ExternalOutput").ap()
    wg = [din("wg%d" % i, [cfg.NSF, 128, KD * 256]) for i in (1, 2)]
    wu = [din("wu%d" % i, [cfg.NSF, 128, KD * 256]) for i in (1, 2)]
    wd = [din("wd%d" % i, [KD, 128, KF * 128]) for i in (1, 2)]
    winx = din("winx", [1, 128, KD * 288])
    winr = din("winr", [KR, 128, KD * 128])
    winkv = din("winkv", [KR, 128, KD * 256])
    winfi = din("winfi", [KR, 128, KD * 256])
    winqg = din("winqg", [KR, 128, KD * 256])
    wout = din("wout", [cfg.NSO, 128, KD * 256])
    X1 = nc.dram_tensor("X1s", [ntm * 128, XW], F32)
    HBin = [nc.dram_tensor("HBin%d" % i, [128, XW], BF16) for i in range(ntm)]
    HBall = [nc.dram_tensor("HBall%d" % i, [256, XW], BF16) for i in range(ntm)]
    YBin = [nc.dram_tensor("YBin%d" % i, [128, YW], BF16) for i in range(NTB)]
    YBall = [nc.dram_tensor("YBall%d" % i, [256, YW], BF16) for i in range(NTB)]
    NSM = 4 * KD + 13 * KR + 6
    smalls_d = din("smalls", [128, NSM])
    w2_d = din("w2", [64, C])
    a2_d = din("a2", [64, C])
    g2_d = din("g2", [160, C])
    dbg = None
    if debug:
        dbg = nc.dram_tensor("dbg", [16, 128, NT], F32, kind="ExternalOutput").ap()

    for dry in (True, False):
        with ExitStack() as st:
            P = Prog(nc, st, dry)
            if dry:
                ws_prev = None

            pfx = "dry_" if dry else ""

            def sb(name, shape, dt=F32):
                return st.enter_context(nc.sbuf_tensor(pfx + name, shape, dt))

            XTt = sb("XT", [128, XW])
            HTt = sb("HT", [128, XW], BF16)
            YTt = sb("YT", [128, YW], BF16)
            ARW = max(KF * NT // 2, 16 * NT + 2 * (8 * NT // 2 + NT) + 16 * NT // 2)
            ARt = sb("ARENA", [128, ARW])
            WRt = [sb("WR%d" % i, [128, SLOT], BF16) for i in range(NSLOT)]
            XT = [B(XTt[:, k * NT:(k + 1) * NT], "XT%d" % k) for k in range(KD)]
            HT = [B(HTt[:, k * NT:(k + 1) * NT], "HT%d" % k) for k in range(KD)]
            YT = [B(YTt[:, k * NT:(k + 1) * NT], "YT%d" % k) for k in range(2 * KR)]
            HX = [B(XTt[:, 0:XW // 2].bitcast(BF16)[:, k * NT:(k + 1) * NT], "HX%d" % k) for k in range(KD)]
            HTB = [HT, HX]
            HTcur = [HT]
            ATv = ARt[:].bitcast(BF16)
            AT = [B(ATv[:, f * NT:(f + 1) * NT], "AT%d" % f) for f in range(KF)]
            WR = [B(WRt[i][:], "WR%d" % i) for i in range(NSLOT)]
            ws = WStream(P, WR)
            if not dry:
                ws.reqs = saved_reqs
            arena_off = [0]
            mix_keys = []

            def carve(name, words, dt=F32):
                o = arena_off[0]
                arena_off[0] += words
                assert arena_off[0] <= ARW, "arena overflow"
                ap = ARt[:, o:o + words]
                if dt == BF16:
                    ap = ap.bitcast(BF16)
                mix_keys.append(name)
                return B(ap, name)

            def f32t(name):
                return carve(name, NT)

            def b16t(name):
                return carve(name, NT // 2, BF16)

            SM = B(sb("SM", [128, NSM])[:], "SM")
            ONESB = B(sb("ONESB", [128, 128], BF16)[:], "ONESB")
            ONES32 = B(sb("ONES32", [128, 128])[:], "ONES32")
            BONES32 = B(sb("BONES32", [128, 128])[:], "BONES32")
            BONESB = B(sb("BONESB", [128, 128], BF16)[:], "BONESB")
            BONV = B(sb("BONV", [128, 128])[:], "BONV")
            CEN32 = B(sb("CEN32", [128, 128])[:], "CEN32")
            IDB = B(sb("IDB", [128, 128], BF16)[:], "IDB")
            IDF = B(sb("IDF", [128, 128])[:], "IDF")
            MU_S = B(sb("MU_S", [128, NT], BF16)[:], "MU_S")
            MU_I = B(sb("MU_I", [128, NT], BF16)[:], "MU_I")
            ML_S = B(sb("ML_S", [128, NT], BF16)[:], "ML_S")
            IDS = B(sb("IDS", [128, NT])[:], "IDS")
            IDSB = B(sb("IDSB", [128, NT], BF16)[:], "IDSB")
            RMASK = B(sb("RMASK", [128, NT])[:], "RMASK")
            TMPC = B(sb("TMPC", [128, NT])[:], "TMPC")
            W2B = B(sb("W2B", [64, C], BF16)[:], "W2B")
            A2B = B(sb("A2B", [64, C], BF16)[:], "A2B")
            G2A = B(sb("G2A", [128, C], BF16)[:], "G2A")
            G2B = B(sb("G2B", [32, C], BF16)[:], "G2B")
            NCC = 3 * KR + 4
            CARRY = B(sb("CARRY", [128, NCC])[:], "CARRY")
            DER = B(sb("DER", [128, 4 * KR])[:], "DER")
            OM = B(sb("OM", [128, NCC])[:], "OM")
            STR = [B(sb("STR%d" % h, [128, 64], BF16)[:], "STR%d" % h) for h in range(KR)]
            SG32 = [B(sb("SG32_%d" % h, [128, 128])[:], "SG32_%d" % h) for h in range(KR)]
            SGB = [B(sb("SGB_%d" % h, [128, 128], BF16)[:], "SGB_%d" % h) for h in range(KR)]
            TW = B(sb("TW", [64, NT], BF16)[:], "TW")
            XA = B(sb("XA", [64, NT], BF16)[:], "XA")
            SGA = B(sb("SGA", [128, NT], BF16)[:], "SGA")
            SGBb = B(sb("SGBb", [32, NT], BF16)[:], "SGBb")
            PF = [B(sb("PF%d" % i, [128, NT + 1])[:], "PF%d" % i) for i in range(2)]
            SQ = [B(sb("SQ%d" % i, [128, NT], BF16)[:], "SQ%d" % i) for i in range(2)]
            RS = B(sb("RS", [128, NT])[:], "RS")
            PSF = [B(st.enter_context(nc.psum_tensor(pfx + "psf%d" % i, [128, NT], F32))[:], "psf%d" % i) for i in range(6)]
            PSB = [B(st.enter_context(nc.psum_tensor(pfx + "psb%d" % i, [128, 2 * NT], BF16))[:], "psb%d" % i) for i in range(2)]
            rr = {"f": 0, "b": 0, "pf": 0, "sq": 0}

            def psf():
                rr["f"] = (rr["f"] + 1) % len(PSF)
                return PSF[rr["f"]]

            def psb():
                rr["b"] = (rr["b"] + 1) % len(PSB)
                return PSB[rr["b"]]

            def mm(out, lhsT, rhs, start=True, stop=True, inc=None):
                if inc is None:
                    inc = stop
                P.op("pe", lambda e: e.matmul(out.ap, lhsT.ap, rhs.ap, start=start, stop=stop),
                     reads=[lhsT.key, rhs.key], writes=[out.key], inc=inc)

            def tr(out, in_, ident, inc=True):
                P.op("pe", lambda e: e.transpose(out.ap, in_.ap, ident.ap), reads=[in_.key, ident.key], writes=[out.key], inc=inc)

            def act(out, in_, func, scale=None, bias=None):
                kw = {}
                rd = [in_.key]
                if scale is not None:
                    if isinstance(scale, B):
                        kw["scale"] = scale.ap
                        rd.append(scale.key)
                    else:
                        kw["scale"] = float(scale)
                if bias is not None:
                    if isinstance(bias, B):
                        kw["bias"] = bias.ap
                        rd.append(bias.key)
                    else:
                        kw["bias"] = float(bias)
                P.op("act", lambda e: e.activation(out=out.ap, in_=in_.ap, func=func, **kw), reads=rd, writes=[out.key])

            def tt(out, in0, in1, op, eng="dve"):
                P.op(eng, lambda e: e.tensor_tensor(out=out.ap, in0=in0.ap, in1=in1.ap, op=op),
                     reads=[in0.key, in1.key], writes=[out.key])

            def tsc(out, in0, s1, op0, s2=None, op1=None, eng="dve"):
                rd = [in0.key]
                a1 = s1.ap if isinstance(s1, B) else float(s1)
                if isinstance(s1, B):
                    rd.append(s1.key)
                if s2 is None:
                    P.op(eng, lambda e: e.tensor_scalar(out=out.ap, in0=in0.ap, scalar1=a1, scalar2=None, op0=op0),
                         reads=rd, writes=[out.key])
                else:
                    a2 = s2.ap if isinstance(s2, B) else float(s2)
                    if isinstance(s2, B):
                        rd.append(s2.key)
                    P.op(eng, lambda e: e.tensor_scalar(out=out.ap, in0=in0.ap, scalar1=a1, scalar2=a2, op0=op0, op1=op1),
                         reads=rd, writes=[out.key])

            def stt(out, in0, s, in1, op0, op1):
                rd = [in0.key, in1.key]
                a = s.ap if isinstance(s, B) else float(s)
                if isinstance(s, B):
                    rd.append(s.key)
                P.op("dve", lambda e: e.scalar_tensor_tensor(out=out.ap, in0=in0.ap, scalar=a, in1=in1.ap, op0=op0, op1=op1),
                     reads=rd, writes=[out.key])

            def cp(out, in_, eng="dve"):
                if eng == "act":
                    act(out, in_, AF.Identity)
                    return
                P.op(eng, lambda e: e.tensor_copy(out=out.ap, in_=in_.ap), reads=[in_.key], writes=[out.key])

            def recip(out, in_):
                P.op("dve", lambda e: e.reciprocal(out=out.ap, in_=in_.ap), reads=[in_.key], writes=[out.key])

            def memset(buf, val, eng="dve"):
                P.op(eng, lambda e: e.memset(buf.ap, val), writes=[buf.key])

            def scan(out, d0, d1):
                P.op("dve", lambda e: e.tensor_tensor_scan(out=out.ap, data0=d0.ap, data1=d1.ap, initial=0.0,
                                                           op0=ALU.mult, op1=ALU.add),
                     reads=[d0.key, d1.key], writes=[out.key])

            def asel(buf, rows, pattern, cmp, fill, base, cm):
                P.op("pool", lambda e: e.affine_select(out=buf.ap[rows], in_=buf.ap[rows], pattern=pattern, compare_op=cmp,
                                                       fill=fill, base=base, channel_multiplier=cm),
                     reads=[buf.key], writes=[buf.key])

            dbg_n = [0]

            def dump(buf):
                if debug and dbg_n[0] < 16:
                    if buf.ap.dtype != F32:
                        cp(TMPC.v(TMPC.ap[0:buf.ap.shape[0], 0:buf.ap.shape[1]]), buf)
                        src = TMPC.v(TMPC.ap[0:buf.ap.shape[0], 0:buf.ap.shape[1]])
                    else:
                        src = buf
                    t = P.dma("sp", dbg[dbg_n[0], 0:src.ap.shape[0], 0:src.ap.shape[1]], src.ap, reads=[src.key])
                    out_toks.append(t)
                    dbg_n[0] += 1

            out_toks = []

            P.dma("sp", SM.ap, smalls_d[:, :], writes=["SM"])
            P.dma("pool", W2B.ap, w2_d[:, :], writes=["W2B"])
            P.dma("pool", A2B.ap, a2_d[:, :], writes=["A2B"])
            P.dma("pool", G2A.ap, g2_d[0:128, :], writes=["G2A"])
            P.dma("pool", G2B.ap, g2_d[128:160, :], writes=["G2B"])
            memset(ONESB, 1.0)
            memset(ONES32, 1.0)
            memset(BONES32, 0.0)
            memset(BONES32.v(BONES32.ap[0:64, 0:64]), 1.0)
            memset(BONES32.v(BONES32.ap[64:128, 64:128]), 1.0)
            cp(BONESB, BONES32)
            memset(IDF, 0.0, "pool")
            asel(IDF, slice(0, 128), [[-1, 128]], ALU.not_equal, 1.0, 0, 1)
            cp(IDB, IDF)
            tsc(BONV, BONES32, 1.0 / 64, ALU.mult)
            tt(CEN32, IDF, BONV, ALU.subtract)
            memset(TMPC, 1.0, "pool")
            TMPM = RS
            for buf, pat, base, cm, cmpop in (
                (MU_S, [[0, NCH], [1, CH]], -1, -1, ALU.is_ge),
                (MU_I, [[0, NCH], [1, CH]], 0, -1, ALU.is_ge),
                (ML_S, [[0, NCH], [-1, CH]], -1, 1, ALU.is_ge),
                (IDS, [[0, NCH], [-1, CH]], 0, 1, ALU.is_equal),
            ):
                tmpm = TMPM
                memset(tmpm, 1.0, "pool")
                for h2 in range(2):
                    asel(tmpm, slice(64 * h2, 64 * h2 + 64), pat, cmpop, 0.0, base, cm)
                cp(buf, tmpm)
            cp(IDSB, IDS)
            memset(RMASK, 1.0)
            memset(RMASK.v(RMASK.ap.rearrange("p (c t) -> p c t", t=CH)[:, :, 0:1]), 0.0)
            memset(CARRY, 0.0)
            for h in range(KR):
                memset(STR[h], 0.0)
                memset(SG32[h], 0.0)
                memset(SGB[h], 0.0)
            o = [0]

            def col(n):
                s = o[0]
                o[0] += n
                return s

            c_g1, c_gm, c_g2, c_gf = col(KD), col(KD), col(KD), col(KD)
            c_mur, c_muk, c_muv = col(KR), col(KR), col(KR)
            c_w0, c_a0, c_kk, c_ka, c_rk, c_lnw, c_lnb = (col(KR) for _ in range(7))
            c_l0, c_l1, c_hn = col(KR), col(KR), col(KR)
            c_mxw, c_mxa, c_mga, c_mgb = col(1), col(1), col(1), col(1)
            c_s, c_1ms = col(1), col(1)
            assert o[0] == NSM

            def smc(c, rows=128):
                return SM.v(SM.ap[0:rows, c:c + 1])

            def derc(c, rows=128):
                return DER.v(DER.ap[0:rows, c:c + 1])

            tsc(OM.v(OM.ap[:, 0:3 * KR]), SM.v(SM.ap[:, c_mur:c_mur + 3 * KR]), -1.0, ALU.mult, 1.0, ALU.add)
            tsc(OM.v(OM.ap[:, 3 * KR:3 * KR + 4]), SM.v(SM.ap[:, c_mxw:c_mxw + 4]), -1.0, ALU.mult, 1.0, ALU.add)
            tsc(DER.v(DER.ap[:, 0:KR]), SM.v(SM.ap[:, c_ka:c_ka + KR]), -1.0, ALU.mult, 1.0, ALU.add)
            tt(DER.v(DER.ap[:, 3 * KR:4 * KR]), SM.v(SM.ap[:, c_l0:c_l0 + KR]), SM.v(SM.ap[:, c_l1:c_l1 + KR]), ALU.subtract)
            act(DER.v(DER.ap[:, KR:2 * KR]), DER.v(DER.ap[:, 3 * KR:4 * KR]), AF.Sigmoid)
            tsc(DER.v(DER.ap[:, 2 * KR:3 * KR]), DER.v(DER.ap[:, KR:2 * KR]), -1.0, ALU.mult, 1.0, ALU.add)

            PSS = B(PSB[0].ap.bitcast(F32)[:, 0:NT], PSB[0].key)

            XSRC = [XT]

            def sumsq_chunk(k):
                s_ = SQ[rr["sq"] % 2]
                rr["sq"] += 1
                act(s_, XSRC[0][k], AF.Square)
                mm(PSS, ONESB, s_, start=(k == 0), stop=(k == KD - 1), inc=True)

            def norm_apply(cg, out_list=None, inplace=False, on_done=None):
                act(RS, PSS, AF.Ln, scale=1.0 / D, bias=smc_eps)
                act(RS, RS, AF.Exp, scale=-0.5)
                for k in range(KD):
                    dst = XSRC[0][k] if inplace else out_list[k]
                    stt(dst, XSRC[0][k], smc(cg + k), RS, ALU.mult, ALU.mult)
                    if on_done:
                        on_done(k)

            def rmsnorm(cg, out_list=None, inplace=False):
                for k in range(KD):
                    sumsq_chunk(k)
                norm_apply(cg, out_list, inplace)

            def ffn(i, on_chunk=None, dst=None):
                for s in range(cfg.NSF):
                    sg = ws.get(wg[i][s], KD * 256)
                    su = ws.get(wu[i][s], KD * 256)
                    for j in range(2):
                        f = s * 2 + j
                        pg, pu = psf(), psf()
                        for k in range(KD):
                            mm(pg, sg.v(sg.ap[:, k * 256 + j * 128:k * 256 + (j + 1) * 128]), HT[k], start=(k == 0), stop=(k == KD - 1))
                        for k in range(KD):
                            mm(pu, su.v(su.ap[:, k * 256 + j * 128:k * 256 + (j + 1) * 128]), HT[k], start=(k == 0), stop=(k == KD - 1))
                        act(TMPC, pg, AF.Silu)
                        tt(AT[f], TMPC, pu, ALU.mult)
                for dch in range(KD):
                    sd = ws.get(wd[i][dch], KF * 128)
                    po = psf()
                    for f in range(KF):
                        mm(po, sd.v(sd.ap[:, f * 128:(f + 1) * 128]), AT[f], start=(f == 0), stop=(f == KF - 1))
                    if on_chunk and dch > 0:
                        on_chunk(dch - 1)
                    stt((dst or XT)[dch], po, 0.5, XT[dch], ALU.mult, ALU.add)
                if on_chunk:
                    on_chunk(KD - 1)

            def shift_lerp(ps, rows, mu, cidx, out):
                pf = PF[rr["pf"] % 2]
                rr["pf"] += 1
                pfr = pf.v(pf.ap[0:rows, :])
                cp(pfr.v(pf.ap[0:rows, 0:1]), CARRY.v(CARRY.ap[0:rows, cidx:cidx + 1]))
                act(pfr.v(pf.ap[0:rows, 1:NT + 1]), ps.v(ps.ap[0:rows, :]), AF.Identity)
                tmp = TMPC.v(TMPC.ap[0:rows, :])
                act(tmp, ps.v(ps.ap[0:rows, :]), AF.Identity, scale=OM.v(OM.ap[0:rows, cidx:cidx + 1]))
                cp(CARRY.v(CARRY.ap[0:rows, cidx:cidx + 1]), pfr.v(pf.ap[0:rows, NT:NT + 1]), "act")
                stt(out, pfr.v(pf.ap[0:rows, 0:NT]), mu, tmp, ALU.mult, ALU.add)

            def proj(slab, coff, ncols, width):
                ps = psf()
                for k in range(KD):
                    mm(ps.v(ps.ap[0:ncols, :]), slab.v(slab.ap[:, k * width + coff:k * width + coff + ncols]), HTcur[0][k],
                       start=(k == 0), stop=(k == KD - 1))
                return ps

            def c3(b):
                return b.ap.rearrange("p (c t) -> p c t", t=CH)

            def blk(buf, h2, c, w=CH):
                return buf.v(buf.ap[64 * h2:64 * h2 + 64, c * w:(c + 1) * w])

            LASTF = lambda h2, c: (h2 == 1 and c == NCH - 1)

            def rwkv_prep(hp, need_out, T, need_r, O):
                Rf, Kf, Vf = T["Rf"], T["Kf"], T["Vf"]
                if need_r:
                    slab = ws.get(winr[hp], KD * 128)
                    ps = proj(slab, 0, 128, 128)
                    shift_lerp(ps, 128, smc(c_mur + hp), hp, Rf)
                    yield
                slab = ws.get(winkv[hp], KD * 256)
                ps = proj(slab, 0, 128, 256)
                shift_lerp(ps, 128, smc(c_muk + hp), KR + hp, Kf)
                yield
                ps = proj(slab, 128, 128, 256)
                shift_lerp(ps, 128, smc(c_muv + hp), 2 * KR + hp, Vf)
                yield
                chs = slice(hp * 128, (hp + 1) * 128)
                LW, Af, GG = T["LW"], T["Af"], O["GG"]
                ps = psf()
                mm(ps, W2B.v(W2B.ap[:, chs]), TW)
                act(LW, ps, AF.Sigmoid, bias=smc(c_w0 + hp))
                tsc(LW, LW, WSCALE, ALU.mult)
                yield
                ps = psf()
                mm(ps, A2B.v(A2B.ap[:, chs]), XA)
                act(Af, ps, AF.Sigmoid, bias=smc(c_a0 + hp))
                yield
                if need_out:
                    ps = psf()
                    mm(ps, G2A.v(G2A.ap[:, chs]), SGA, start=True, stop=False)
                    mm(ps, G2B.v(G2B.ap[:, chs]), SGBb, start=False, stop=True)
                    act(GG, ps, AF.Identity)
                    yield
                KK, T1, T2 = T["KK"], T["T1"], T["T2"]
                tsc(KK, Kf, smc(c_kk + hp), ALU.mult)
                sq = SQ[rr["sq"] % 2]
                rr["sq"] += 1
                act(sq, KK, AF.Square)
                ps = psf()
                mm(ps, BONESB, sq)
                yield
                act(T1, ps, AF.Sqrt)
                tsc(T1, T1, 1e-12, ALU.max)
                yield
                recip(T1, T1)
                tt(KK, KK, T1, ALU.mult)
                yield
                KM, BE = T["KM"], T["BE"]
                tsc(T1, Af, smc(c_ka + hp), ALU.mult, derc(hp), ALU.add)
                tt(KM, Kf, T1, ALU.mult)
                yield
                tt(BE, KK, Af, ALU.mult)
                G, EG, EGM, EGX, EP = T["G"], T["EG"], T["EGM"], T["EGX"], T["EP"]
                scan(G, RMASK, LW)
                yield
                act(EG, G, AF.Exp)
                act(EGM, G, AF.Exp, scale=-1.0)
                tt(T2, G, LW, ALU.subtract)
                yield
                act(EGX, T2, AF.Exp)
                cp(O["EGC"], EG.v(c3(EG)[:, :, CH - 1:CH]))
                egc = O["EGC"].v(O["EGC"].ap.to_broadcast([128, NCH, CH]))
                tt(EP.v(c3(EP)), EGM.v(c3(EGM)), egc, ALU.mult)
                yield
                stt(O["AL"], KK, -1.0, EGX, ALU.mult, ALU.mult)
                tt(O["BM"], BE, EGM, ALU.mult)
                yield
                tt(O["KMm"], KM, EGM, ALU.mult)
                tt(O["BP"], BE, EP, ALU.mult)
                yield
                tt(O["KP"], KM, EP, ALU.mult)
                cp(O["VB"], Vf, "act")
                yield
                if need_out:
                    tt(O["RT"], Rf, EG, ALU.mult)
                    stt(T2, Rf, smc(c_rk + hp), KM, ALU.mult, ALU.mult)
                    pb_ = psf()
                    mm(pb_, BONES32, T2)
                    yield
                    tt(O["BON"], pb_, Vf, ALU.mult)
                    yield

            def rwkv_mm(hp, need_out, T, O):
                AL, BM, KMm, BP, KP, VB, RT = (O[n] for n in ("AL", "BM", "KMm", "BP", "KP", "VB", "RT"))

                def transposed(src, dst):
                    pb = psb()
                    for c in range(NCH):
                        for h2 in range(2):
                            tr(blk(pb, h2, c), blk(src, h2, c), IDB.v(IDB.ap[64 * h2:64 * h2 + 64, 64 * h2:64 * h2 + 64]), inc=LASTF(h2, c))
                    cp(dst, pb.v(pb.ap[:, 0:NT]), "act")

                Zb, Z32 = T["Zb"], T["Z32"]
                Zb3 = Zb.ap.rearrange("p (c j) -> p c j", j=2 * CH)
                Z323 = Z32.ap.rearrange("p (c j) -> p c j", j=2 * CH)

                def zw(h2, c):
                    return Zb.v(Zb.ap[64 * h2:64 * h2 + 64, c * 2 * CH:c * 2 * CH + CH])

                def zu(h2, c):
                    return Zb.v(Zb.ap[64 * h2:64 * h2 + 64, c * 2 * CH + CH:(c + 1) * 2 * CH])

                ALt, BPt, KPt, Vt = T["ALt"], T["BPt"], T["KPt"], T["Vt"]
                transposed(AL, ALt)
                yield
                transposed(VB, Vt)
                yield

                def prod(lhs, rhs, mask, dst, eng="dve"):
                    ps = psf()
                    for c in range(NCH):
                        for h2 in range(2):
                            mm(blk(ps, h2, c), blk(lhs, h2, c), blk(rhs, h2, c), inc=LASTF(h2, c))
                    if mask is None:
                        if eng == "act":
                            act(dst, ps, AF.Identity)
                        else:
                            cp(dst, ps)
                    else:
                        tt(dst, ps, mask, ALU.mult)

                PT, PN = [T["PTa"], T["PTb"]], [T["PNa"], T["PNb"]]
                AKT, ARBT, ARKT = T["AKT"], T["ARBT"], T["ARKT"]
                prod(KMm, AL, MU_S, AKT)
                yield
                prod(BM, AL, MU_S, PT[0])
                yield
                prod(AL, BM, ML_S, PN[0])
                yield
                ps = psf()
                for c in range(NCH):
                    for h2 in range(2):
                        mm(blk(ps, h2, c), blk(AKT, h2, c), blk(Vt, h2, c), inc=LASTF(h2, c))
                cp(Zb.v(Zb3[:, :, 0:CH]), ALt.v(c3(ALt)), "act")
                act(Zb.v(Zb3[:, :, CH:2 * CH]), ps.v(c3(ps)), AF.Identity)
                PTI = T["AKT"]
                tt(PTI, PT[0], IDSB, ALU.add)
                yield
                transposed(BP, BPt)
                yield
                transposed(KP, KPt)
                yield
                if need_out:
                    prod(BM, RT, MU_I, ARBT)
                    yield
                    prod(KMm, RT, MU_I, ARKT)
                    yield
                cur = 0
                for lvl in range(6):
                    psa, psb_ = psf(), psf()
                    for c in range(NCH):
                        pz = psa if c < NCH // 2 else psb_
                        cc_ = c % (NCH // 2)
                        for h2 in range(2):
                            mm(pz.v(pz.ap[64 * h2:64 * h2 + 64, cc_ * 2 * CH:(cc_ + 1) * 2 * CH]), blk(PTI, h2, c),
                               Zb.v(Zb.ap[64 * h2:64 * h2 + 64, c * 2 * CH:(c + 1) * 2 * CH]),
                               inc=(h2 == 1 and (c == NCH // 2 - 1 or c == NCH - 1)))
                    nxt = 1 - cur
                    if lvl < 5:
                        psq = psf()
                        for c in range(NCH):
                            for h2 in range(2):
                                mm(blk(psq, h2, c), blk(PN[cur], h2, c), blk(PT[cur], h2, c), inc=LASTF(h2, c))
                        if lvl < 4:
                            psn = psf()
                            for c in range(NCH):
                                for h2 in range(2):
                                    mm(blk(psn, h2, c), blk(PT[cur], h2, c), blk(PN[cur], h2, c), inc=LASTF(h2, c))
                    act(Zb.v(Zb.ap[:, 0:NT]), psa, AF.Identity)
                    if lvl < 5:
                        tt(PTI, psq, IDSB, ALU.add)
                    act(Zb.v(Zb.ap[:, NT:2 * NT]), psb_, AF.Identity)
                    yield
                    if lvl < 4:
                        act(PT[nxt], psq, AF.Identity)
                        act(PN[nxt], psn, AF.Identity)
                        cur = nxt
                        yield
                PMT, QT, RH, DG = T["PMT"], T["QT"], T["RH"], T["DG"]
                egc = O["EGC"].v(O["EGC"].ap.to_broadcast([128, NCH, CH]))
                tt(DG.v(c3(DG)), IDS.v(c3(IDS)), egc, ALU.mult)
                ps = psf()
                for c in range(NCH):
                    for h2 in range(2):
                        mm(blk(ps, h2, c), zw(h2, c), blk(BPt, h2, c), inc=LASTF(h2, c))
                ps2 = psf()
                for c in range(NCH):
                    for h2 in range(2):
                        mm(blk(ps2, h2, c), blk(BPt, h2, c), zu(h2, c), start=True, stop=False)
                        mm(blk(ps2, h2, c), blk(KPt, h2, c), blk(Vt, h2, c), start=False, stop=True, inc=LASTF(h2, c))
                tt(PMT, ps, DG, ALU.add)
                QTb = B(QT.ap.bitcast(BF16)[:, 0:NT], QT.key)
                act(QTb, ps2, AF.Identity)
                yield
                STA = T["STA"]
                cp(STA.v(STA.ap[:, 0:CH]), STR[hp])
                for c in range(NCH):
                    ps = psf()
                    for h2 in range(2):
                        mm(ps.v(ps.ap[64 * h2:64 * h2 + 64, 0:CH]), blk(PMT, h2, c), blk(STA, h2, c), start=True, stop=False)
                        mm(ps.v(ps.ap[64 * h2:64 * h2 + 64, 0:CH]), IDB.v(IDB.ap[64 * h2:64 * h2 + 64, 64 * h2:64 * h2 + 64]), blk(QTb, h2, c),
                           start=False, stop=True, inc=(h2 == 1))
                    if c == 0 and need_out:
                        ps3 = psf()
                        for cc in range(NCH):
                            for h2 in range(2):
                                mm(blk(ps3, h2, cc), zw(h2, cc), blk(ARBT, h2, cc), inc=LASTF(h2, cc))
                    dst = STA.v(STA.ap[:, (c + 1) * CH:(c + 2) * CH]) if c < NCH - 1 else STR[hp]
                    act(dst, ps.v(ps.ap[:, 0:CH]), AF.Identity)
                    if c == 0 and need_out:
                        tt(RH, ps3, RT, ALU.add)
                    yield
                if not need_out:
                    return
                psy = psf()
                for c in range(NCH):
                    for h2 in range(2):
                        mm(blk(psy, h2, c), blk(STA, h2, c), blk(RH, h2, c), start=True, stop=False)
                        mm(blk(psy, h2, c), zu(h2, c), blk(ARBT, h2, c), start=False, stop=False)
                        mm(blk(psy, h2, c), blk(Vt, h2, c), blk(ARKT, h2, c), start=False, stop=True, inc=LASTF(h2, c))
                Yf, YQ, Yc, VR = Z32.v(Z32.ap[:, 0:NT]), Z32.v(Z32.ap[:, NT:2 * NT]), T["QT"], T["DG"]
                act(Yf, psy, AF.Identity)
                yield
                pc = psf()
                mm(pc, CEN32, Yf)
                act(Yc, pc, AF.Identity)
                act(YQ, pc, AF.Square)
                yield
                pq = psf()
                mm(pq, BONV, YQ)
                act(VR, pq, AF.Ln, bias=smc_gn)
                yield
                act(VR, VR, AF.Exp, scale=-0.5)
                tt(Yc, Yc, VR, ALU.mult)
                yield
                tsc(Yf, Yc, smc(c_lnw + hp), ALU.mult, smc(c_lnb + hp), ALU.add)
                yield
                tt(Yf, Yf, O["BON"], ALU.add)
                tt(YT[hp], Yf, O["GG"], ALU.mult)
                yield

            def interleave(ga, gb):
                gens = [g for g in (ga, gb) if g is not None]
                while gens:
                    for g in list(gens):
                        try:
                            next(g)
                        except StopIteration:
                            gens.remove(g)

            def rwkv_all(need_out, T, need_r):
                for r in range(KR + 1):
                    gp = rwkv_prep(r, need_out, T, need_r, OSET[r % 2]) if r < KR else None
                    gm = rwkv_mm(r - 1, need_out, T, OSET[(r - 1) % 2]) if r > 0 else None
                    interleave(gm, gp)

            def hgrn_prep(hh, T, O):
                Qf, FG, LF, KG = T["Rf"], T["Kf"], T["LW"], T["KM"]
                slab = ws.get(winqg[hh], KD * 256)
                ps = proj(slab, 0, 128, 256)
                act(Qf, ps, AF.Silu)
                yield
                ps = proj(slab, 128, 128, 256)
                act(O["GS"], ps, AF.Silu)
                yield
                slab = ws.get(winfi[hh], KD * 256)
                ps = proj(slab, 0, 128, 256)
                act(FG, ps, AF.Sigmoid)
                tsc(FG, FG, derc(2 * KR + hh), ALU.mult, derc(KR + hh), ALU.add)
                yield
                act(LF, FG, AF.Ln)
                tsc(KG, FG, -1.0, ALU.mult, 1.0, ALU.add)
                ps = proj(slab, 128, 128, 256)
                act(O["VB"], ps, AF.Identity)
                yield
                G, EG, EGM, EP = T["G"], T["EG"], T["EGM"], T["EP"]
                scan(G, RMASK, LF)
                yield
                act(EG, G, AF.Exp)
                act(EGM, G, AF.Exp, scale=-1.0)
                yield
                cp(O["EGC"], EG.v(c3(EG)[:, :, CH - 1:CH]))
                egc = O["EGC"].v(O["EGC"].ap.to_broadcast([128, NCH, CH]))
                tt(EP.v(c3(EP)), EGM.v(c3(EGM)), egc, ALU.mult)
                tt(O["KMm"], KG, EGM, ALU.mult)
                yield
                tt(O["KP"], KG, EP, ALU.mult)
                tt(O["QTl"], Qf, EG, ALU.mult)
                yield

            def hgrn_mm(hh, T, O):
                QTl, KMm, KP, VB, GS = O["QTl"], O["KMm"], O["KP"], O["VB"], O["GS"]
                KPt, Vt = T["KPth"], T["Vth"]
                for src, dst in ((KP, KPt), (VB, Vt)):
                    pb = psb()
                    for c in range(NCH):
                        tr(pb.v(pb.ap[0:64, c * 128:(c + 1) * 128]), src.v(src.ap[:, c * CH:(c + 1) * CH]), IDB, inc=(c == NCH - 1))
                    cp(dst.v(dst.ap[0:64, :]), pb.v(pb.ap[0:64, :]), "act")
                    yield
                ATm = T["ATm"]
                ps = psf()
                for c in range(NCH):
                    mm(ps.v(ps.ap[0:64, c * CH:(c + 1) * CH]), KMm.v(KMm.ap[:, c * CH:(c + 1) * CH]), QTl.v(QTl.ap[:, c * CH:(c + 1) * CH]), inc=(c == NCH - 1))
                tt(ATm.v(ATm.ap[0:64, :]), ps.v(ps.ap[0:64, :]), MU_I.v(MU_I.ap[0:64, :]), ALU.mult)
                yield
                SALL = T["SALL"]
                cp(SALL.v(SALL.ap[:, 0:128]), SGB[hh], "act")
                for c in range(NCH):
                    ps = psf()
                    mm(ps.v(ps.ap[:, 0:128]), KPt.v(KPt.ap[0:64, c * 128:(c + 1) * 128]), Vt.v(Vt.ap[0:64, c * 128:(c + 1) * 128]))
                    dst = SALL.v(SALL.ap[:, (c + 1) * 128:(c + 2) * 128]) if c < NCH - 1 else SGB[hh]
                    stt(dst, SALL.v(SALL.ap[:, c * 128:(c + 1) * 128]), O["EGC"].v(O["EGC"].ap[:, c, :]), ps.v(ps.ap[:, 0:128]), ALU.mult, ALU.add)
                    yield
                pso = psf()
                for c in range(NCH):
                    oc = pso.v(pso.ap[:, c * CH:(c + 1) * CH])
                    mm(oc, SALL.v(SALL.ap[:, c * 128:(c + 1) * 128]), QTl.v(QTl.ap[:, c * CH:(c + 1) * CH]), start=True, stop=False)
                    mm(oc, Vt.v(Vt.ap[0:64, c * 128:(c + 1) * 128]), ATm.v(ATm.ap[0:64, c * CH:(c + 1) * CH]), start=False, stop=True, inc=(c == NCH - 1))
                Z32 = T["Z32"]
                OQ, T1 = Z32.v(Z32.ap[:, 0:NT]), Z32.v(Z32.ap[:, NT:2 * NT])
                act(OQ, pso, AF.Square)
                yield
                pn = psf()
                mm(pn, ONES32, OQ)
                act(T1, pn, AF.Ln, scale=1.0 / 128, bias=smc_eps)
                yield
                act(T1, T1, AF.Exp, scale=-0.5)
                stt(T1, pso, smc(c_hn + hh), T1, ALU.mult, ALU.mult)
                yield
                tt(YT[KR + hh], T1, GS, ALU.mult)
                yield

            def xslab_prep(t):
                HTcur[0] = HTB[t % 2]
                slab = ws.get(winx[0], KD * 288)
                ps = proj(slab, 0, 64, 288)
                t64 = T["T1"].v(T["T1"].ap[0:64, :])
                shift_lerp(ps, 64, smc(c_mxw, 64), 3 * KR, t64)
                act(TW, t64, AF.Tanh)
                yield
                ps = proj(slab, 64, 64, 288)
                shift_lerp(ps, 64, smc(c_mxa, 64), 3 * KR + 1, t64)
                cp(XA, t64, "act")
                yield
                ps = proj(slab, 128, 128, 288)
                shift_lerp(ps, 128, smc(c_mga), 3 * KR + 2, T["T1"])
                act(SGA, T["T1"], AF.Sigmoid)
                yield
                ps = proj(slab, 256, 32, 288)
                t32 = T["T1"].v(T["T1"].ap[0:32, :])
                shift_lerp(ps, 32, smc(c_mgb, 32), 3 * KR + 3, t32)
                act(SGBb, t32, AF.Sigmoid)
                yield

            def chain_gens(*gs):
                for g in gs:
                    for _ in g:
                        yield

            def phase_b(load_h, finish_tile):
                units = []
                for t in range(NTB):
                    units += [("r", t, h) for h in range(KR)] + [("h", t, h) for h in range(KR)]
                nu = len(units)

                def mk_prep(i):
                    k, t, h = units[i]
                    if k == "r":
                        g = rwkv_prep(h, True, T, True, OSET[i % 2])
                        if h == 0:
                            if t + 1 < NTB:
                                load_h(t + 1)
                            g = chain_gens(xslab_prep(t), g)
                        return g
                    return hgrn_prep(h, T, OHS[i % 2])

                def mk_mm(i):
                    k, t, h = units[i]
                    return rwkv_mm(h, True, T, OSET[i % 2]) if k == "r" else hgrn_mm(h, T, OHS[i % 2])

                for r in range(nu + 1):
                    gp = mk_prep(r) if r < nu else None
                    gm = mk_mm(r - 1) if r > 0 else None
                    interleave(gm, gp)
                    if r > 0 and units[r - 1][0] == "h" and units[r - 1][2] == KR - 1:
                        finish_tile(units[r - 1][1])

            EPSC = B(sb("EPSC", [128, 2])[:], "EPSC")
            memset(EPSC.v(EPSC.ap[:, 0:1]), RMS_EPS)
            memset(EPSC.v(EPSC.ap[:, 1:2]), GN_EPS)
            smc_eps = EPSC.v(EPSC.ap[:, 0:1])
            smc_gn = EPSC.v(EPSC.ap[:, 1:2])

            T = {}
            for n in ("Rf", "Kf", "Vf", "LW", "Af", "KK", "T1", "T2", "KM", "G", "EG", "EGM", "Z32a", "Z32b", "QT", "DG"):
                T[n] = f32t(n)
            ZBOFF = [0]
            for n in ("ALt", "BPt", "KPt", "Vt", "Zba", "Zbb", "AKT", "PMT", "RH", "STA"):
                if n == "Zba":
                    ZBOFF[0] = arena_off[0]
                T[n] = b16t(n)
            for pair, nm in ((("PTa", "PTb"), "KPth"), (("PNa", "PNb"), "Vth"), (("ARBT", "ARKT"), "SALL")):
                o0 = arena_off[0]
                T[pair[0]] = b16t(pair[0])
                T[pair[1]] = b16t(pair[1])
                T[pair[1]] = B(T[pair[1]].ap, pair[0])
                T[nm] = B(ARt[:, o0:o0 + NT].bitcast(BF16), pair[0])
            OSET = []
            for si in range(2):
                Od = {}
                for n in ("AL", "BM", "KMm", "BP", "KP", "VB", "RT", "GG"):
                    Od[n] = b16t("%s_%d" % (n, si))
                Od["BON"] = f32t("BON_%d" % si)
                Od["EGC"] = B(sb("EGC%d" % si, [128, NCH, 1])[:], "EGC%d" % si)
                OSET.append(Od)
            oz = [i for i, n in enumerate(mix_keys) if n == "Z32a"][0]
            T["Z32"] = B(ARt[:, oz * NT:(oz + 2) * NT], "Z32a")
            zb0 = T["Zba"].ap
            T["Zb"] = B(ARt[:, ZBOFF[0]:ZBOFF[0] + NT].bitcast(BF16), "Zba")
            OHS = []
            ohk = []
            for si in range(2):
                Od = {}
                for j, n in enumerate(("QTl", "KMm", "KP", "VB", "GS")):
                    if XW // 2 >= 10 * (NT // 2):
                        o_ = XW // 2 + (si * 5 + j) * (NT // 2)
                        Od[n] = B(XTt[:, o_:o_ + NT // 2].bitcast(BF16), "OH%s_%d" % (n, si))
                    else:
                        Od[n] = B(sb("OH%s_%d" % (n, si), [128, NT], BF16)[:], "OH%s_%d" % (n, si))
                    ohk.append(Od[n].key)
                Od["EGC"] = B(sb("EGCH%d" % si, [128, NCH, 1])[:], "EGCH%d" % si)
                OHS.append(Od)
            T["BE"], T["EGX"], T["EP"] = T["Af"], T["T2"], T["G"]
            T["GG"] = OSET[0]["GG"]
            T["VBh"], T["QTl"], T["KMmh"], T["KPh"], T["ATm"] = OSET[0]["VB"], OSET[0]["RT"], OSET[0]["KMm"], OSET[0]["KP"], T["AKT"]
            at_keys = [a.key for a in AT]

            half = XW // 2
            xk0 = [x.key for x in XT[0:KD // 2]]
            xk1 = [x.key for x in XT[KD // 2:KD]]
            htk = [h.key for h in HT]
            hxk = [h.key for h in HX]
            ytk = [y.key for y in YT]
            groups = [[0, 1], [2, 3], [4, 5], [6, 7]]
            ATW_ = KF * NT // 2
            n_top = min(KD, (ARW - ATW_) // NT)
            assert (KD - n_top) * NT <= YW // 2
            X2 = [B(ARt[:, ATW_ + k * NT:ATW_ + (k + 1) * NT], "X2_%d" % k) for k in range(n_top)] + \
                 [B(YTt[:, :].bitcast(F32)[:, (k - n_top) * NT:(k - n_top + 1) * NT], "X2_%d" % k) for k in range(n_top, KD)]
            x2k = [x.key for x in X2]
            nq = 4 if KD % 4 == 0 else 1
            qw = XW // nq
            def load_x(tj):
                for q in range(nq):
                    P.dma("sp", XTt[:, q * qw:(q + 1) * qw], xmain[tj, :, q * qw:(q + 1) * qw],
                          writes=[x.key for x in XT[q * (KD // nq):(q + 1) * (KD // nq)]])

            load_x(0)
            for ti in range(ntm):
                XSRC[0] = XT
                rmsnorm(c_g1, HT)
                XSRC[0] = X2
                ffn(0, sumsq_chunk, dst=X2)
                if ti + 1 < ntm:
                    load_x(ti + 1)
                P.dma("sp", X1[ti * 128:(ti + 1) * 128, 0:n_top * NT], ARt[:, ATW_:ATW_ + n_top * NT], reads=x2k[0:n_top], writes=["X1a%d" % ti])
                if n_top < KD:
                    P.dma("sp", X1[ti * 128:(ti + 1) * 128, n_top * NT:XW], YTt[:, :].bitcast(F32)[:, 0:(KD - n_top) * NT], reads=x2k[n_top:], writes=["X1b%d" % ti])
                norm_apply(c_gm, AT[0:KD])
                P.dma("sp", HBin[ti][:, :], ATv[:, 0:XW], reads=[a.key for a in AT[0:KD]], writes=["HBin%d" % ti])
                P.collective(HBin[ti], HBall[ti], groups, reads=["HBin%d" % ti], writes=["HBall%d" % ti])
            XSRC[0] = XT
            P.transfer(at_keys + x2k, mix_keys + ytk)
            P.transfer(xk0 + xk1, hxk + ohk)

            def load_h(t):
                buf = HTB[t % 2]
                dst = HTt[:, :] if t % 2 == 0 else XTt[:, 0:half].bitcast(BF16)
                ci, ro = (t, 0) if t < ntm else (t - ntm, 128)
                P.dma("sp", dst, HBall[ci][ro:ro + 128, :], reads=["HBall%d" % ci], writes=[b.key for b in buf])

            if cfg.stage >= 2:
                load_h(0)

                def finish_tile(t):
                    P.dma("sp", YBin[t][:, :], YTt[:, :], reads=ytk, writes=["YBin%d" % t])
                    P.collective(YBin[t], YBall[t], groups, reads=["YBin%d" % t], writes=["YBall%d" % t])

                phase_b(load_h, finish_tile)
            P.transfer(hxk + ohk, xk0 + xk1)
            HTcur[0] = HT
            ATW = KF * NT // 2
            if ARW - ATW >= 3 * (YW // 2):
                yl_aps = [ARt[:, ATW + i * (YW // 2):ATW + (i + 1) * (YW // 2)].bitcast(BF16) for i in range(3)] + [YTt[:, :]]
                yl_pref = True
            else:
                yl_aps = [ARt[:, i * (YW // 2):(i + 1) * (YW // 2)].bitcast(BF16) for i in range(4)]
                yl_pref = False
            YL = [[B(yl_aps[2 * r + h], "YL%d%d" % (r, h)) for h in range(2)] for r in range(2)]
            ylk = [YL[r][h].key for r in range(2) for h in range(2)]
            for ti in range(ntm):
                for q in range(nq):
                    P.dma("sp", XTt[:, q * qw:(q + 1) * qw], X1[ti * 128:(ti + 1) * 128, q * qw:(q + 1) * qw],
                          reads=["X1a%d" % ti, "X1b%d" % ti], writes=[x.key for x in XT[q * (KD // nq):(q + 1) * (KD // nq)]])
                def load_y(tj):
                    for r in range(2):
                        for h in range(2):
                            tq = h * ntm + tj
                            P.dma("sp", YL[r][h].ap, YBall[tq][r * 128:(r + 1) * 128, :], reads=["YBall%d" % tq], writes=[YL[r][h].key])

                if ti == 0:
                    P.transfer(at_keys + mix_keys + ytk, ylk)
                    load_y(0)
                elif not yl_pref:
                    P.transfer(at_keys + mix_keys, ylk)
                    load_y(ti)
                for r in range(2):
                    for j in range(2 * KR):
                        kglob = (r * KR + j) if j < KR else (2 * KR + r * KR + (j - KR))
                        ya = YL[r][0].v(YL[r][0].ap[:, j * NT:(j + 1) * NT])
                        yb = YL[r][1].v(YL[r][1].ap[:, j * NT:(j + 1) * NT])
                        tsc(SQ[0], ya, smc(c_1ms), ALU.mult)
                        stt(HT[kglob], yb, smc(c_s), SQ[0], ALU.mult, ALU.add)
                if yl_pref and ti + 1 < ntm:
                    load_y(ti + 1)
                if cfg.stage >= 5:
                    for sl in range(cfg.NSO):
                        slab = ws.get(wout[sl], KD * 256)
                        for j in range(2):
                            dch = sl * 2 + j
                            po = psf()
                            for k in range(KD):
                                mm(po, slab.v(slab.ap[:, k * 256 + j * 128:k * 256 + (j + 1) * 128]), HT[k], start=(k == 0), stop=(k == KD - 1))
                            if dch > 0:
                                sumsq_chunk(dch - 1)
                            tt(XT[dch], XT[dch], po, ALU.add)
                    sumsq_chunk(KD - 1)
                if not yl_pref:
                    P.transfer(ylk, at_keys)
                norm_apply(c_g2, HT)
                ffn(1, sumsq_chunk)
                def store_q(k, ti=ti):
                    cq = KD // nq
                    if k % cq == cq - 1:
                        q = k // cq
                        out_toks.append(P.dma("sp", outT[ti, :, q * qw:(q + 1) * qw], XTt[:, q * qw:(q + 1) * qw],
                                              reads=[x.key for x in XT[q * cq:(q + 1) * cq]]))

                norm_apply(c_gf, inplace=True, on_done=store_q)
            P.wait_all("sp", out_toks)
            if dry:
                saved_reqs = ws.reqs
                n_dry = P.ninstr
            else:
                print("instructions:", P.ninstr, "weight slabs:", len(ws.reqs), "engine counts:", P.cnt)
    return nc


def _cols(v, n):
    return np.ascontiguousarray(np.asarray(v, np.float32).reshape(n, 128).T)


def _slab(W, c0, width):
    K = W.shape[0]
    blk = W[:, c0:c0 + width].reshape(K // 128, 128, width).transpose(1, 0, 2)
    return np.ascontiguousarray(blk).reshape(128, (K // 128) * width)


def _slabs_cols(W, cols_list):
    out = []
    K = W.shape[0]
    for ranges in cols_list:
        parts = [W[:, c0:c0 + w] for c0, w in ranges]
        Wc = np.concatenate(parts, axis=1)
        out.append(_slab(Wc, 0, Wc.shape[1]))
    return np.stack(out)


def prepare_weights(cfg, inp):
    D, F, KD, KF, C, KR = cfg.D, cfg.F, cfg.KD, cfg.KF, cfg.C, cfg.KR
    CL, KRL = C // 2, KR // 2
    m = {}
    for i, tag in ((1, "ffn1"), (2, "ffn2")):
        Wg = np.asarray(inp[tag + "_w_gate"][0], np.float32)
        Wu = np.asarray(inp[tag + "_w_up"][0], np.float32)
        Wd = np.asarray(inp[tag + "_w_down"][0], np.float32)
        m["wg%d" % i] = np.stack([_slab(Wg, s * 256, 256) for s in range(cfg.NSF)])
        m["wu%d" % i] = np.stack([_slab(Wu, s * 256, 256) for s in range(cfg.NSF)])
        m["wd%d" % i] = np.stack([_slab(Wd, d * 128, 128) for d in range(KD)])
    Win = np.asarray(inp["w_in"][0], np.float32)
    m["winx"] = _slabs_cols(Win, [[(3 * C, 288)]])
    Wo = np.asarray(inp["w_out"][0], np.float32)
    m["wout"] = np.stack([_slab(Wo, s * 256, 256) for s in range(cfg.NSO)])
    R0 = 3 * C + 288
    mu = np.asarray(inp["rwkv_mu"][0], np.float32)
    w2 = np.asarray(inp["rwkv_w2"][0], np.float32)
    a2 = np.asarray(inp["rwkv_a2"][0], np.float32)
    g2 = np.asarray(inp["rwkv_g2"][0], np.float32)
    per = []
    for s in range(2):
        p = {}
        prs = range(s * KRL, (s + 1) * KRL)
        p["winr"] = _slabs_cols(Win, [[(hp * 128, 128)] for hp in prs])
        p["winkv"] = _slabs_cols(Win, [[(C + hp * 128, 128), (2 * C + hp * 128, 128)] for hp in prs])
        p["winqg"] = _slabs_cols(Win, [[(R0 + hh * 128, 128), (R0 + 3 * C + hh * 128, 128)] for hh in prs])
        p["winfi"] = _slabs_cols(Win, [[(R0 + C + hh * 128, 128), (R0 + 2 * C + hh * 128, 128)] for hh in prs])
        ch = slice(s * CL, (s + 1) * CL)

        def lc(v):
            return _cols(np.asarray(v, np.float32).reshape(-1)[ch], KRL)

        cols = [_cols(inp["ffn1_norm"][0], KD), _cols(inp["mix_norm"][0], KD), _cols(inp["ffn2_norm"][0], KD),
                _cols(inp["final_norm"], KD),
                lc(mu[0:C]), lc(mu[C:2 * C]), lc(mu[2 * C:3 * C]),
                lc(inp["rwkv_w0"][0]), lc(inp["rwkv_a0"][0]), lc(inp["rwkv_k_k"][0]),
                lc(inp["rwkv_k_a"][0]), lc(inp["rwkv_r_k"][0]),
                lc(inp["rwkv_ln_w"][0]), lc(inp["rwkv_ln_b"][0]),
                lc(inp["hgrn_lb_logits"][0]), lc(inp["hgrn_lb_logits"][1]), lc(inp["hgrn_norm"][0])]
        extra = np.zeros((128, 6), np.float32)
        extra[0:64, 0] = mu[3 * C:3 * C + 64]
        extra[0:64, 1] = mu[3 * C + 64:3 * C + 128]
        extra[0:128, 2] = mu[3 * C + 128:3 * C + 256]
        extra[0:32, 3] = mu[3 * C + 256:3 * C + 288]
        extra[:, 4] = float(s)
        extra[:, 5] = float(1 - s)
        p["smalls"] = np.ascontiguousarray(np.concatenate(cols + [extra], axis=1))
        p["w2"] = np.ascontiguousarray(w2[:, ch])
        p["a2"] = np.ascontiguousarray(a2[:, ch])
        p["g2"] = np.ascontiguousarray(g2[:, ch])
        per.append(p)
    return m, per


def _tiles(xseg, KD):
    Tn = xseg.shape[0]
    a = xseg.reshape(Tn // NT, NT, KD, 128).transpose(0, 3, 2, 1)
    return np.ascontiguousarray(a).reshape(Tn // NT, 128, KD * NT)


def run(cfg, inp, debug=False):
    x = np.asarray(inp["x"], np.float32)
    Bn, Tn, D = x.shape
    assert Bn * 2 == 8 and Tn // 2 == cfg.ntm * NT
    wm, per = prepare_weights(cfg, inp)
    nc = build_program(cfg, debug=debug)
    in_maps = []
    hl = Tn // 2
    for c in range(8):
        b, s = c // 2, c % 2
        d = dict(wm)
        d.update(per[s])
        d["xmain"] = _tiles(x[b, s * hl:(s + 1) * hl], cfg.KD)
        in_maps.append(d)
    res = run_bass_kernel_spmd(nc, in_maps, core_ids=list(range(8)))
    out = np.empty((Bn, Tn, D), np.float32)
    for c in range(8):
        b, s = c // 2, c % 2
        o = res.results[c]["outT"].reshape(cfg.ntm, 128, cfg.KD, NT).transpose(0, 3, 2, 1).reshape(hl, D)
        out[b, s * hl:(s + 1) * hl] = o
    return out


def kernel(**inputs):
    cfg = Cfg(D=2048, F=5632, ntp=8, ntm=8)
    return run(cfg, inputs)
```

```python
from contextlib import ExitStack
import numpy as np
import concourse.bass as bass
import concourse.mybir as mybir
from concourse.bass_utils import run_bass_kernel_spmd

F32 = mybir.dt.float32
BF16 = mybir.dt.bfloat16
AF = mybir.ActivationFunctionType
ALU = mybir.AluOpType

NT = 512
CH = 64
NCH = NT // CH
NSLOT = 4
SLOT = 5632
RMS_EPS = 1e-6
GN_EPS = 64e-5
WSCALE = -0.6065306597126334


class Cfg:
    def __init__(self, D=2048, F=5632, ntp=8, ntm=8, stage=99):
        self.D, self.F, self.ntp, self.ntm, self.stage = D, F, ntp, ntm, stage
        self.KD = D // 128
        self.KF = F // 128
        self.C = D // 2
        self.KR = self.C // 128
        self.NSF = F // 256
        self.NSO = D // 256


class B:
    __slots__ = ("ap", "key")

    def __init__(self, ap, key):
        self.ap, self.key = ap, key

    def __getitem__(self, idx):
        return B(self.ap[idx], self.key)

    def v(self, ap):
        return B(ap, self.key)


class Prog:
    def __init__(self, nc, st, dry):
        self.nc, self.dry = nc, dry
        self.eng = {"pe": nc.tensor, "act": nc.scalar, "dve": nc.vector, "pool": nc.gpsimd, "sp": nc.sync}
        self.semobj = {}
        self.cnt = {k: 0 for k in self.eng}
        self.seen = {k: {} for k in self.eng}
        self.dq = {"sp": ["d%d" % i for i in range(8)], "pool": ["e%d" % i for i in range(6)]}
        self.dcnt = {n: 0 for q in self.dq.values() for n in q}
        self.dnext = {"sp": 0, "pool": 0}
        if not dry:
            for k in self.eng:
                self.semobj["s_" + k] = st.enter_context(nc.semaphore("s_" + k))
            for q in self.dq.values():
                for n in q:
                    self.semobj[n] = st.enter_context(nc.semaphore(n))
            self.semobj["cc"] = st.enter_context(nc.semaphore("cc"))
        self.ncc = 0
        self.bufs = {}
        self.ninstr = 0
        self.pe_open = False

    def _wait(self, e, sname, val):
        if e == "pe" and sname == "s_pe":
            return
        if self.seen[e].get(sname, 0) >= val:
            return
        self.eng[e].wait_ge(self.semobj[sname], val)
        self.seen[e][sname] = val

    def _deps(self, e, reads, writes):
        for k in reads:
            b = self.bufs.get(k)
            if b and b[0]:
                self._wait(e, b[0][0], b[0][1])
        for k in writes:
            b = self.bufs.get(k)
            if b:
                if b[0]:
                    self._wait(e, b[0][0], b[0][1])
                for s, v in b[1].items():
                    self._wait(e, s, v)

    def _mark(self, tok, reads, writes):
        for k in reads:
            b = self.bufs.get(k)
            if b is None:
                b = self.bufs[k] = [None, {}]
            if b[1].get(tok[0], 0) < tok[1]:
                b[1][tok[0]] = tok[1]
        for k in writes:
            self.bufs[k] = [tok, {}]

    def op(self, e, fn, reads=(), writes=(), inc=True):
        self.ninstr += 1
        if self.dry:
            return None
        if e != "pe":
            assert not self.pe_open, "other-engine op emitted inside an open PE group"
            extra = [k for k in reads if k.startswith("ps")]
            if extra:
                writes = list(writes) + extra
        self._deps(e, reads, writes)
        ins = fn(self.eng[e])
        if inc:
            self.cnt[e] += 1
            ins.then_inc(self.semobj["s_" + e], 1)
            tok = ("s_" + e, self.cnt[e])
            if e == "pe":
                self.pe_open = False
        else:
            assert e == "pe"
            tok = ("s_" + e, self.cnt[e] + 1)
            self.pe_open = True
        self._mark(tok, reads, writes)
        return tok

    def dma(self, q, out, in_, reads=(), writes=()):
        self.ninstr += 1
        if self.dry:
            return None
        assert not self.pe_open
        names = self.dq[q]
        n = names[self.dnext[q]]
        self.dnext[q] = (self.dnext[q] + 1) % len(names)
        if self.dcnt[n] > 0:
            self._wait(q, n, 16 * self.dcnt[n])
        self._deps(q, reads, writes)
        ins = self.eng[q].dma_start(out=out, in_=in_)
        self.dcnt[n] += 1
        ins.then_inc(self.semobj[n], 16)
        tok = (n, 16 * self.dcnt[n])
        self._mark(tok, reads, writes)
        return tok

    def collective(self, in_t, out_t, groups, reads, writes):
        self.ninstr += 1
        if self.dry:
            return None
        assert not self.pe_open
        self._deps("pool", reads, writes)
        ins = self.eng["pool"].collective_compute("AllGather", ALU.bypass, replica_groups=groups,
                                                  ins=[in_t.ap().opt()], outs=[out_t.ap().opt()])
        self.ncc += 1
        ins.then_inc(self.semobj["cc"], 1)
        tok = ("cc", self.ncc)
        self._mark(tok, reads, writes)
        return tok

    def transfer(self, from_keys, to_keys):
        if self.dry:
            return
        acc = {}
        for k in from_keys:
            b = self.bufs.get(k)
            if not b:
                continue
            if b[0]:
                acc[b[0][0]] = max(acc.get(b[0][0], 0), b[0][1])
            for s, v in b[1].items():
                acc[s] = max(acc.get(s, 0), v)
        for k in to_keys:
            b = self.bufs.get(k)
            if b is None:
                b = self.bufs[k] = [None, {}]
            for s, v in acc.items():
                if b[1].get(s, 0) < v:
                    b[1][s] = v

    def wait_all(self, e, toks):
        if self.dry:
            return
        for t in toks:
            if t:
                self._wait(e, t[0], t[1])


class WStream:
    def __init__(self, P, slots):
        self.P, self.slots = P, slots
        self.reqs = []
        self.pos = 0
        self.loaded = 0

    def get(self, dram_ap, nelem):
        if self.P.dry:
            self.reqs.append((dram_ap, nelem))
            return self.slots[0]
        idx = self.pos
        self.pos += 1
        lim = min(len(self.reqs), idx + NSLOT - 1)
        while self.loaded < lim:
            j = self.loaded
            ap, n = self.reqs[j]
            s = self.slots[j % NSLOT]
            self.P.dma("pool", s.ap[:, 0:n], ap, writes=[s.key])
            self.loaded += 1
        return self.slots[idx % NSLOT]


def build_program(cfg, debug=False):
    nc = bass.Bass("TRN2", target_bir_lowering=False)
    D, F, KD, KF = cfg.D, cfg.F, cfg.KD, cfg.KF
    C, KR = cfg.C // 2, cfg.KR // 2
    ntm = cfg.ntm
    NTB = 2 * ntm
    XW = KD * NT
    YW = 2 * KR * NT

    def din(name, shape):
        return nc.dram_tensor(name, shape, F32, kind="ExternalInput").ap()

    xmain = din("xmain", [ntm, 128, XW])
    outT = nc.dram_tensor("outT", [ntm, 128, XW], F32, kind="ExternalOutput").ap()
    wg = [din("wg%d" % i, [cfg.NSF, 128, KD * 256]) for i in (1, 2)]
    wu = [din("wu%d" % i, [cfg.NSF, 128, KD * 256]) for i in (1, 2)]
    wd = [din("wd%d" % i, [KD, 128, KF * 128]) for i in (1, 2)]
    winx = din("winx", [1, 128, KD * 288])
    winr = din("winr", [KR, 128, KD * 128])
    winkv = din("winkv", [KR, 128, KD * 256])
    winfi = din("winfi", [KR, 128, KD * 256])
    winqg = din("winqg", [KR, 128, KD * 256])
    wout = din("wout", [cfg.NSO, 128, KD * 256])
    X1 = nc.dram_tensor("X1s", [ntm * 128, XW], F32)
    HBin = [nc.dram_tensor("HBin%d" % i, [128, XW], BF16) for i in range(ntm)]
    HBall = [nc.dram_tensor("HBall%d" % i, [256, XW], BF16) for i in range(ntm)]
    YBin = [nc.dram_tensor("YBin%d" % i, [128, YW], BF16) for i in range(NTB)]
    YBall = [nc.dram_tensor("YBall%d" % i, [256, YW], BF16) for i in range(NTB)]
    NSM = 4 * KD + 13 * KR + 6
    smalls_d = din("smalls", [128, NSM])
    w2_d = din("w2", [64, C])
    a2_d = din("a2", [64, C])
    g2_d = din("g2", [160, C])
    dbg = None
    if debug:
        dbg = nc.dram_tensor("dbg", [16, 128, NT], F32, kind="ExternalOutput").ap()

    for dry in (True, False):
        with ExitStack() as st:
            P = Prog(nc, st, dry)
            if dry:
                ws_prev = None

            pfx = "dry_" if dry else ""

            def sb(name, shape, dt=F32):
                return st.enter_context(nc.sbuf_tensor(pfx + name, shape, dt))

            XTt = sb("XT", [128, XW])
            HTt = sb("HT", [128, XW], BF16)
            YTt = sb("YT", [128, YW], BF16)
            ARW = max(KF * NT // 2, 16 * NT + 2 * (8 * NT // 2 + NT) + 16 * NT // 2)
            ARt = sb("ARENA", [128, ARW])
            WRt = [sb("WR%d" % i, [128, SLOT], BF16) for i in range(NSLOT)]
            XT = [B(XTt[:, k * NT:(k + 1) * NT], "XT%d" % k) for k in range(KD)]
            HT = [B(HTt[:, k * NT:(k + 1) * NT], "HT%d" % k) for k in range(KD)]
            YT = [B(YTt[:, k * NT:(k + 1) * NT], "YT%d" % k) for k in range(2 * KR)]
            HX = [B(XTt[:, 0:XW // 2].bitcast(BF16)[:, k * NT:(k + 1) * NT], "HX%d" % k) for k in range(KD)]
            HTB = [HT, HX]
            HTcur = [HT]
            ATv = ARt[:].bitcast(BF16)
            AT = [B(ATv[:, f * NT:(f + 1) * NT], "AT%d" % f) for f in range(KF)]
            WR = [B(WRt[i][:], "WR%d" % i) for i in range(NSLOT)]
            ws = WStream(P, WR)
            if not dry:
                ws.reqs = saved_reqs
            arena_off = [0]
            mix_keys = []

            def carve(name, words, dt=F32):
                o = arena_off[0]
                arena_off[0] += words
                assert arena_off[0] <= ARW, "arena overflow"
                ap = ARt[:, o:o + words]
                if dt == BF16:
                    ap = ap.bitcast(BF16)
                mix_keys.append(name)
                return B(ap, name)

            def f32t(name):
                return carve(name, NT)

            def b16t(name):
                return carve(name, NT // 2, BF16)

            SM = B(sb("SM", [128, NSM])[:], "SM")
            ONESB = B(sb("ONESB", [128, 128], BF16)[:], "ONESB")
            ONES32 = B(sb("ONES32", [128, 128])[:], "ONES32")
            BONES32 = B(sb("BONES32", [128, 128])[:], "BONES32")
            BONESB = B(sb("BONESB", [128, 128], BF16)[:], "BONESB")
            BONV = B(sb("BONV", [128, 128])[:], "BONV")
            CEN32 = B(sb("CEN32", [128, 128])[:], "CEN32")
            IDB = B(sb("IDB", [128, 128], BF16)[:], "IDB")
            IDF = B(sb("IDF", [128, 128])[:], "IDF")
            MU_S = B(sb("MU_S", [128, NT], BF16)[:], "MU_S")
            MU_I = B(sb("MU_I", [128, NT], BF16)[:], "MU_I")
            ML_S = B(sb("ML_S", [128, NT], BF16)[:], "ML_S")
            IDS = B(sb("IDS", [128, NT])[:], "IDS")
            IDSB = B(sb("IDSB", [128, NT], BF16)[:], "IDSB")
            RMASK = B(sb("RMASK", [128, NT])[:], "RMASK")
            TMPC = B(sb("TMPC", [128, NT])[:], "TMPC")
            W2B = B(sb("W2B", [64, C], BF16)[:], "W2B")
            A2B = B(sb("A2B", [64, C], BF16)[:], "A2B")
            G2A = B(sb("G2A", [128, C], BF16)[:], "G2A")
            G2B = B(sb("G2B", [32, C], BF16)[:], "G2B")
            NCC = 3 * KR + 4
            CARRY = B(sb("CARRY", [128, NCC])[:], "CARRY")
            DER = B(sb("DER", [128, 4 * KR])[:], "DER")
            OM = B(sb("OM", [128, NCC])[:], "OM")
            STR = [B(sb("STR%d" % h, [128, 64], BF16)[:], "STR%d" % h) for h in range(KR)]
            SG32 = [B(sb("SG32_%d" % h, [128, 128])[:], "SG32_%d" % h) for h in range(KR)]
            SGB = [B(sb("SGB_%d" % h, [128, 128], BF16)[:], "SGB_%d" % h) for h in range(KR)]
            TW = B(sb("TW", [64, NT], BF16)[:], "TW")
            XA = B(sb("XA", [64, NT], BF16)[:], "XA")
            SGA = B(sb("SGA", [128, NT], BF16)[:], "SGA")
            SGBb = B(sb("SGBb", [32, NT], BF16)[:], "SGBb")
            PF = [B(sb("PF%d" % i, [128, NT + 1])[:], "PF%d" % i) for i in range(2)]
            SQ = [B(sb("SQ%d" % i, [128, NT], BF16)[:], "SQ%d" % i) for i in range(2)]
            RS = B(sb("RS", [128, NT])[:], "RS")
            PSF = [B(st.enter_context(nc.psum_tensor(pfx + "psf%d" % i, [128, NT], F32))[:], "psf%d" % i) for i in range(6)]
            PSB = [B(st.enter_context(nc.psum_tensor(pfx + "psb%d" % i, [128, 2 * NT], BF16))[:], "psb%d" % i) for i in range(2)]
            rr = {"f": 0, "b": 0, "pf": 0, "sq": 0}

            def psf():
                rr["f"] = (rr["f"] + 1) % len(PSF)
                return PSF[rr["f"]]

            def psb():
                rr["b"] = (rr["b"] + 1) % len(PSB)
                return PSB[rr["b"]]

            def mm(out, lhsT, rhs, start=True, stop=True, inc=None):
                if inc is None:
                    inc = stop
                P.op("pe", lambda e: e.matmul(out.ap, lhsT.ap, rhs.ap, start=start, stop=stop),
                     reads=[lhsT.key, rhs.key], writes=[out.key], inc=inc)

            def tr(out, in_, ident, inc=True):
                P.op("pe", lambda e: e.transpose(out.ap, in_.ap, ident.ap), reads=[in_.key, ident.key], writes=[out.key], inc=inc)

            def act(out, in_, func, scale=None, bias=None):
                kw = {}
                rd = [in_.key]
                if scale is not None:
                    if isinstance(scale, B):
                        kw["scale"] = scale.ap
                        rd.append(scale.key)
                    else:
                        kw["scale"] = float(scale)
                if bias is not None:
                    if isinstance(bias, B):
                        kw["bias"] = bias.ap
                        rd.append(bias.key)
                    else:
                        kw["bias"] = float(bias)
                P.op("act", lambda e: e.activation(out=out.ap, in_=in_.ap, func=func, **kw), reads=rd, writes=[out.key])

            def tt(out, in0, in1, op, eng="dve"):
                P.op(eng, lambda e: e.tensor_tensor(out=out.ap, in0=in0.ap, in1=in1.ap, op=op),
                     reads=[in0.key, in1.key], writes=[out.key])

            def tsc(out, in0, s1, op0, s2=None, op1=None, eng="dve"):
                rd = [in0.key]
                a1 = s1.ap if isinstance(s1, B) else float(s1)
                if isinstance(s1, B):
                    rd.append(s1.key)
                if s2 is None:
                    P.op(eng, lambda e: e.tensor_scalar(out=out.ap, in0=in0.ap, scalar1=a1, scalar2=None, op0=op0),
                         reads=rd, writes=[out.key])
                else:
                    a2 = s2.ap if isinstance(s2, B) else float(s2)
                    if isinstance(s2, B):
                        rd.append(s2.key)
                    P.op(eng, lambda e: e.tensor_scalar(out=out.ap, in0=in0.ap, scalar1=a1, scalar2=a2, op0=op0, op1=op1),
                         reads=rd, writes=[out.key])

            def stt(out, in0, s, in1, op0, op1):
                rd = [in0.key, in1.key]
                a = s.ap if isinstance(s, B) else float(s)
                if isinstance(s, B):
                    rd.append(s.key)
                P.op("dve", lambda e: e.scalar_tensor_tensor(out=out.ap, in0=in0.ap, scalar=a, in1=in1.ap, op0=op0, op1=op1),
                     reads=rd, writes=[out.key])

            def cp(out, in_, eng="dve"):
                if eng == "act":
                    act(out, in_, AF.Identity)
                    return
                P.op(eng, lambda e: e.tensor_copy(out=out.ap, in_=in_.ap), reads=[in_.key], writes=[out.key])

            def recip(out, in_):
                P.op("dve", lambda e: e.reciprocal(out=out.ap, in_=in_.ap), reads=[in_.key], writes=[out.key])

            def memset(buf, val, eng="dve"):
                P.op(eng, lambda e: e.memset(buf.ap, val), writes=[buf.key])

            def scan(out, d0, d1):
                P.op("dve", lambda e: e.tensor_tensor_scan(out=out.ap, data0=d0.ap, data1=d1.ap, initial=0.0,
                                                           op0=ALU.mult, op1=ALU.add),
                     reads=[d0.key, d1.key], writes=[out.key])

            def asel(buf, rows, pattern, cmp, fill, base, cm):
                P.op("pool", lambda e: e.affine_select(out=buf.ap[rows], in_=buf.ap[rows], pattern=pattern, compare_op=cmp,
                                                       fill=fill, base=base, channel_multiplier=cm),
                     reads=[buf.key], writes=[buf.key])

            dbg_n = [0]

            def dump(buf):
                if debug and dbg_n[0] < 16:
                    if buf.ap.dtype != F32:
                        cp(TMPC.v(TMPC.ap[0:buf.ap.shape[0], 0:buf.ap.shape[1]]), buf)
                        src = TMPC.v(TMPC.ap[0:buf.ap.shape[0], 0:buf.ap.shape[1]])
                    else:
                        src = buf
                    t = P.dma("sp", dbg[dbg_n[0], 0:src.ap.shape[0], 0:src.ap.shape[1]], src.ap, reads=[src.key])
                    out_toks.append(t)
                    dbg_n[0] += 1

            out_toks = []

            P.dma("sp", SM.ap, smalls_d[:, :], writes=["SM"])
            P.dma("pool", W2B.ap, w2_d[:, :], writes=["W2B"])
            P.dma("pool", A2B.ap, a2_d[:, :], writes=["A2B"])
            P.dma("pool", G2A.ap, g2_d[0:128, :], writes=["G2A"])
            P.dma("pool", G2B.ap, g2_d[128:160, :], writes=["G2B"])
            memset(ONESB, 1.0)
            memset(ONES32, 1.0)
            memset(BONES32, 0.0)
            memset(BONES32.v(BONES32.ap[0:64, 0:64]), 1.0)
            memset(BONES32.v(BONES32.ap[64:128, 64:128]), 1.0)
            cp(BONESB, BONES32)
            memset(IDF, 0.0, "pool")
            asel(IDF, slice(0, 128), [[-1, 128]], ALU.not_equal, 1.0, 0, 1)
            cp(IDB, IDF)
            tsc(BONV, BONES32, 1.0 / 64, ALU.mult)
            tt(CEN32, IDF, BONV, ALU.subtract)
            memset(TMPC, 1.0, "pool")
            TMPM = RS
            for buf, pat, base, cm, cmpop in (
                (MU_S, [[0, NCH], [1, CH]], -1, -1, ALU.is_ge),
                (MU_I, [[0, NCH], [1, CH]], 0, -1, ALU.is_ge),
                (ML_S, [[0, NCH], [-1, CH]], -1, 1, ALU.is_ge),
                (IDS, [[0, NCH], [-1, CH]], 0, 1, ALU.is_equal),
            ):
                tmpm = TMPM
                memset(tmpm, 1.0, "pool")
                for h2 in range(2):
                    asel(tmpm, slice(64 * h2, 64 * h2 + 64), pat, cmpop, 0.0, base, cm)
                cp(buf, tmpm)
            cp(IDSB, IDS)
            memset(RMASK, 1.0)
            memset(RMASK.v(RMASK.ap.rearrange("p (c t) -> p c t", t=CH)[:, :, 0:1]), 0.0)
            memset(CARRY, 0.0)
            for h in range(KR):
                memset(STR[h], 0.0)
                memset(SG32[h], 0.0)
                memset(SGB[h], 0.0)
            o = [0]

            def col(n):
                s = o[0]
                o[0] += n
                return s

            c_g1, c_gm, c_g2, c_gf = col(KD), col(KD), col(KD), col(KD)
            c_mur, c_muk, c_muv = col(KR), col(KR), col(KR)
            c_w0, c_a0, c_kk, c_ka, c_rk, c_lnw, c_lnb = (col(KR) for _ in range(7))
            c_l0, c_l1, c_hn = col(KR), col(KR), col(KR)
            c_mxw, c_mxa, c_mga, c_mgb = col(1), col(1), col(1), col(1)
            c_s, c_1ms = col(1), col(1)
            assert o[0] == NSM

            def smc(c, rows=128):
                return SM.v(SM.ap[0:rows, c:c + 1])

            def derc(c, rows=128):
                return DER.v(DER.ap[0:rows, c:c + 1])

            tsc(OM.v(OM.ap[:, 0:3 * KR]), SM.v(SM.ap[:, c_mur:c_mur + 3 * KR]), -1.0, ALU.mult, 1.0, ALU.add)
            tsc(OM.v(OM.ap[:, 3 * KR:3 * KR + 4]), SM.v(SM.ap[:, c_mxw:c_mxw + 4]), -1.0, ALU.mult, 1.0, ALU.add)
            tsc(DER.v(DER.ap[:, 0:KR]), SM.v(SM.ap[:, c_ka:c_ka + KR]), -1.0, ALU.mult, 1.0, ALU.add)
            tt(DER.v(DER.ap[:, 3 * KR:4 * KR]), SM.v(SM.ap[:, c_l0:c_l0 + KR]), SM.v(SM.ap[:, c_l1:c_l1 + KR]), ALU.subtract)
            act(DER.v(DER.ap[:, KR:2 * KR]), DER.v(DER.ap[:, 3 * KR:4 * KR]), AF.Sigmoid)
            tsc(DER.v(DER.ap[:, 2 * KR:3 * KR]), DER.v(DER.ap[:, KR:2 * KR]), -1.0, ALU.mult, 1.0, ALU.add)

            PSS = B(PSB[0].ap.bitcast(F32)[:, 0:NT], PSB[0].key)

            XSRC = [XT]

            def sumsq_chunk(k):
                s_ = SQ[rr["sq"] % 2]
                rr["sq"] += 1
                act(s_, XSRC[0][k], AF.Square)
                mm(PSS, ONESB, s_, start=(k == 0), stop=(k == KD - 1), inc=True)

            def norm_apply(cg, out_list=None, inplace=False, on_done=None):
                act(RS, PSS, AF.Ln, scale=1.0 / D, bias=smc_eps)
                act(RS, RS, AF.Exp, scale=-0.5)
                for k in range(KD):
                    dst = XSRC[0][k] if inplace else out_list[k]
                    stt(dst, XSRC[0][k], smc(cg + k), RS, ALU.mult, ALU.mult)
                    if on_done:
                        on_done(k)

            def rmsnorm(cg, out_list=None, inplace=False):
                for k in range(KD):
                    sumsq_chunk(k)
                norm_apply(cg, out_list, inplace)

            def ffn(i, on_chunk=None, dst=None):
                for s in range(cfg.NSF):
                    sg = ws.get(wg[i][s], KD * 256)
                    su = ws.get(wu[i][s], KD * 256)
                    for j in range(2):
                        f = s * 2 + j
                        pg, pu = psf(), psf()
                        for k in range(KD):
                            mm(pg, sg.v(sg.ap[:, k * 256 + j * 128:k * 256 + (j + 1) * 128]), HT[k], start=(k == 0), stop=(k == KD - 1))
                        for k in range(KD):
                            mm(pu, su.v(su.ap[:, k * 256 + j * 128:k * 256 + (j + 1) * 128]), HT[k], start=(k == 0), stop=(k == KD - 1))
                        act(TMPC, pg, AF.Silu)
                        tt(AT[f], TMPC, pu, ALU.mult)
                for dch in range(KD):
                    sd = ws.get(wd[i][dch], KF * 128)
                    po = psf()
                    for f in range(KF):
                        mm(po, sd.v(sd.ap[:, f * 128:(f + 1) * 128]), AT[f], start=(f == 0), stop=(f == KF - 1))
                    if on_chunk and dch > 0:
                        on_chunk(dch - 1)
                    stt((dst or XT)[dch], po, 0.5, XT[dch], ALU.mult, ALU.add)
                if on_chunk:
                    on_chunk(KD - 1)

            def shift_lerp(ps, rows, mu, cidx, out):
                pf = PF[rr["pf"] % 2]
                rr["pf"] += 1
                pfr = pf.v(pf.ap[0:rows, :])
                cp(pfr.v(pf.ap[0:rows, 0:1]), CARRY.v(CARRY.ap[0:rows, cidx:cidx + 1]))
                act(pfr.v(pf.ap[0:rows, 1:NT + 1]), ps.v(ps.ap[0:rows, :]), AF.Identity)
                tmp = TMPC.v(TMPC.ap[0:rows, :])
                act(tmp, ps.v(ps.ap[0:rows, :]), AF.Identity, scale=OM.v(OM.ap[0:rows, cidx:cidx + 1]))
                cp(CARRY.v(CARRY.ap[0:rows, cidx:cidx + 1]), pfr.v(pf.ap[0:rows, NT:NT + 1]), "act")
                stt(out, pfr.v(pf.ap[0:rows, 0:NT]), mu, tmp, ALU.mult, ALU.add)

            def proj(slab, coff, ncols, width):
                ps = psf()
                for k in range(KD):
                    mm(ps.v(ps.ap[0:ncols, :]), slab.v(slab.ap[:, k * width + coff:k * width + coff + ncols]), HTcur[0][k],
                       start=(k == 0), stop=(k == KD - 1))
                return ps

            def c3(b):
                return b.ap.rearrange("p (c t) -> p c t", t=CH)

            def blk(buf, h2, c, w=CH):
                return buf.v(buf.ap[64 * h2:64 * h2 + 64, c * w:(c + 1) * w])

            LASTF = lambda h2, c: (h2 == 1 and c == NCH - 1)

            def rwkv_prep(hp, need_out, T, need_r, O):
                Rf, Kf, Vf = T["Rf"], T["Kf"], T["Vf"]
                if need_r:
                    slab = ws.get(winr[hp], KD * 128)
                    ps = proj(slab, 0, 128, 128)
                    shift_lerp(ps, 128, smc(c_mur + hp), hp, Rf)
                    yield
                slab = ws.get(winkv[hp], KD * 256)
                ps = proj(slab, 0, 128, 256)
                shift_lerp(ps, 128, smc(c_muk + hp), KR + hp, Kf)
                yield
                ps = proj(slab, 128, 128, 256)
                shift_lerp(ps, 128, smc(c_muv + hp), 2 * KR + hp, Vf)
                yield
                chs = slice(hp * 128, (hp + 1) * 128)
                LW, Af, GG = T["LW"], T["Af"], O["GG"]
                ps = psf()
                mm(ps, W2B.v(W2B.ap[:, chs]), TW)
                act(LW, ps, AF.Sigmoid, bias=smc(c_w0 + hp))
                tsc(LW, LW, WSCALE, ALU.mult)
                yield
                ps = psf()
                mm(ps, A2B.v(A2B.ap[:, chs]), XA)
                act(Af, ps, AF.Sigmoid, bias=smc(c_a0 + hp))
                yield
                if need_out:
                    ps = psf()
                    mm(ps, G2A.v(G2A.ap[:, chs]), SGA, start=True, stop=False)
                    mm(ps, G2B.v(G2B.ap[:, chs]), SGBb, start=False, stop=True)
                    act(GG, ps, AF.Identity)
                    yield
                KK, T1, T2 = T["KK"], T["T1"], T["T2"]
                tsc(KK, Kf, smc(c_kk + hp), ALU.mult)
                sq = SQ[rr["sq"] % 2]
                rr["sq"] += 1
                act(sq, KK, AF.Square)
                ps = psf()
                mm(ps, BONESB, sq)
                yield
                act(T1, ps, AF.Sqrt)
                tsc(T1, T1, 1e-12, ALU.max)
                yield
                recip(T1, T1)
                tt(KK, KK, T1, ALU.mult)
                yield
                KM, BE = T["KM"], T["BE"]
                tsc(T1, Af, smc(c_ka + hp), ALU.mult, derc(hp), ALU.add)
                tt(KM, Kf, T1, ALU.mult)
                yield
                tt(BE, KK, Af, ALU.mult)
                G, EG, EGM, EGX, EP = T["G"], T["EG"], T["EGM"], T["EGX"], T["EP"]
                scan(G, RMASK, LW)
                yield
                act(EG, G, AF.Exp)
                act(EGM, G, AF.Exp, scale=-1.0)
                tt(T2, G, LW, ALU.subtract)
                yield
                act(EGX, T2, AF.Exp)
                cp(O["EGC"], EG.v(c3(EG)[:, :, CH - 1:CH]))
                egc = O["EGC"].v(O["EGC"].ap.to_broadcast([128, NCH, CH]))
                tt(EP.v(c3(EP)), EGM.v(c3(EGM)), egc, ALU.mult)
                yield
                stt(O["AL"], KK, -1.0, EGX, ALU.mult, ALU.mult)
                tt(O["BM"], BE, EGM, ALU.mult)
                yield
                tt(O["KMm"], KM, EGM, ALU.mult)
                tt(O["BP"], BE, EP, ALU.mult)
                yield
                tt(O["KP"], KM, EP, ALU.mult)
                cp(O["VB"], Vf, "act")
                yield
                if need_out:
                    tt(O["RT"], Rf, EG, ALU.mult)
                    stt(T2, Rf, smc(c_rk + hp), KM, ALU.mult, ALU.mult)
                    pb_ = psf()
                    mm(pb_, BONES32, T2)
                    yield
                    tt(O["BON"], pb_, Vf, ALU.mult)
                    yield

            def rwkv_mm(hp, need_out, T, O):
                AL, BM, KMm, BP, KP, VB, RT = (O[n] for n in ("AL", "BM", "KMm", "BP", "KP", "VB", "RT"))

                def transposed(src, dst):
                    pb = psb()
                    for c in range(NCH):
                        for h2 in range(2):
                            tr(blk(pb, h2, c), blk(src, h2, c), IDB.v(IDB.ap[64 * h2:64 * h2 + 64, 64 * h2:64 * h2 + 64]), inc=LASTF(h2, c))
                    cp(dst, pb.v(pb.ap[:, 0:NT]), "act")

                Zb, Z32 = T["Zb"], T["Z32"]
                Zb3 = Zb.ap.rearrange("p (c j) -> p c j", j=2 * CH)
                Z323 = Z32.ap.rearrange("p (c j) -> p c j", j=2 * CH)

                def zw(h2, c):
                    return Zb.v(Zb.ap[64 * h2:64 * h2 + 64, c * 2 * CH:c * 2 * CH + CH])

                def zu(h2, c):
                    return Zb.v(Zb.ap[64 * h2:64 * h2 + 64, c * 2 * CH + CH:(c + 1) * 2 * CH])

                ALt, BPt, KPt, Vt = T["ALt"], T["BPt"], T["KPt"], T["Vt"]
                transposed(AL, ALt)
                yield
                transposed(VB, Vt)
                yield

                def prod(lhs, rhs, mask, dst, eng="dve"):
                    ps = psf()
                    for c in range(NCH):
                        for h2 in range(2):
                            mm(blk(ps, h2, c), blk(lhs, h2, c), blk(rhs, h2, c), inc=LASTF(h2, c))
                    if mask is None:
                        if eng == "act":
                            act(dst, ps, AF.Identity)
                        else:
                            cp(dst, ps)
                    else:
                        tt(dst, ps, mask, ALU.mult)

                PT, PN = [T["PTa"], T["PTb"]], [T["PNa"], T["PNb"]]
                AKT, ARBT, ARKT = T["AKT"], T["ARBT"], T["ARKT"]
                prod(KMm, AL, MU_S, AKT)
                yield
                prod(BM, AL, MU_S, PT[0])
                yield
                prod(AL, BM, ML_S, PN[0])
                yield
                ps = psf()
                for c in range(NCH):
                    for h2 in range(2):
                        mm(blk(ps, h2, c), blk(AKT, h2, c), blk(Vt, h2, c), inc=LASTF(h2, c))
                cp(Zb.v(Zb3[:, :, 0:CH]), ALt.v(c3(ALt)), "act")
                act(Zb.v(Zb3[:, :, CH:2 * CH]), ps.v(c3(ps)), AF.Identity)
                PTI = T["AKT"]
                tt(PTI, PT[0], IDSB, ALU.add)
                yield
                transposed(BP, BPt)
                yield
                transposed(KP, KPt)
                yield
                if need_out:
                    prod(BM, RT, MU_I, ARBT)
                    yield
                    prod(KMm, RT, MU_I, ARKT)
                    yield
                cur = 0
                for lvl in range(6):
                    psa, psb_ = psf(), psf()
                    for c in range(NCH):
                        pz = psa if c < NCH // 2 else psb_
                        cc_ = c % (NCH // 2)
                        for h2 in range(2):
                            mm(pz.v(pz.ap[64 * h2:64 * h2 + 64, cc_ * 2 * CH:(cc_ + 1) * 2 * CH]), blk(PTI, h2, c),
                               Zb.v(Zb.ap[64 * h2:64 * h2 + 64, c * 2 * CH:(c + 1) * 2 * CH]),
                               inc=(h2 == 1 and (c == NCH // 2 - 1 or c == NCH - 1)))
                    nxt = 1 - cur
                    if lvl < 5:
                        psq = psf()
                        for c in range(NCH):
                            for h2 in range(2):
                                mm(blk(psq, h2, c), blk(PN[cur], h2, c), blk(PT[cur], h2, c), inc=LASTF(h2, c))
                        if lvl < 4:
                            psn = psf()
                            for c in range(NCH):
                                for h2 in range(2):
                                    mm(blk(psn, h2, c), blk(PT[cur], h2, c), blk(PN[cur], h2, c), inc=LASTF(h2, c))
                    act(Zb.v(Zb.ap[:, 0:NT]), psa, AF.Identity)
                    if lvl < 5:
                        tt(PTI, psq, IDSB, ALU.add)
                    act(Zb.v(Zb.ap[:, NT:2 * NT]), psb_, AF.Identity)
                    yield
                    if lvl < 4:
                        act(PT[nxt], psq, AF.Identity)
                        act(PN[nxt], psn, AF.Identity)
                        cur = nxt
                        yield
                PMT, QT, RH, DG = T["PMT"], T["QT"], T["RH"], T["DG"]
                egc = O["EGC"].v(O["EGC"].ap.to_broadcast([128, NCH, CH]))
                tt(DG.v(c3(DG)), IDS.v(c3(IDS)), egc, ALU.mult)
                ps = psf()
                for c in range(NCH):
                    for h2 in range(2):
                        mm(blk(ps, h2, c), zw(h2, c), blk(BPt, h2, c), inc=LASTF(h2, c))
                ps2 = psf()
                for c in range(NCH):
                    for h2 in range(2):
                        mm(blk(ps2, h2, c), blk(BPt, h2, c), zu(h2, c), start=True, stop=False)
                        mm(blk(ps2, h2, c), blk(KPt, h2, c), blk(Vt, h2, c), start=False, stop=True, inc=LASTF(h2, c))
                tt(PMT, ps, DG, ALU.add)
                QTb = B(QT.ap.bitcast(BF16)[:, 0:NT], QT.key)
                act(QTb, ps2, AF.Identity)
                yield
                STA = T["STA"]
                cp(STA.v(STA.ap[:, 0:CH]), STR[hp])
                for c in range(NCH):
                    ps = psf()
                    for h2 in range(2):
                        mm(ps.v(ps.ap[64 * h2:64 * h2 + 64, 0:CH]), blk(PMT, h2, c), blk(STA, h2, c), start=True, stop=False)
                        mm(ps.v(ps.ap[64 * h2:64 * h2 + 64, 0:CH]), IDB.v(IDB.ap[64 * h2:64 * h2 + 64, 64 * h2:64 * h2 + 64]), blk(QTb, h2, c),
                           start=False, stop=True, inc=(h2 == 1))
                    if c == 0 and need_out:
                        ps3 = psf()
                        for cc in range(NCH):
                            for h2 in range(2):
                                mm(blk(ps3, h2, cc), zw(h2, cc), blk(ARBT, h2, cc), inc=LASTF(h2, cc))
                    dst = STA.v(STA.ap[:, (c + 1) * CH:(c + 2) * CH]) if c < NCH - 1 else STR[hp]
                    act(dst, ps.v(ps.ap[:, 0:CH]), AF.Identity)
                    if c == 0 and need_out:
                        tt(RH, ps3, RT, ALU.add)
                    yield
                if not need_out:
                    return
                psy = psf()
                for c in range(NCH):
                    for h2 in range(2):
                        mm(blk(psy, h2, c), blk(STA, h2, c), blk(RH, h2, c), start=True, stop=False)
                        mm(blk(psy, h2, c), zu(h2, c), blk(ARBT, h2, c), start=False, stop=False)
                        mm(blk(psy, h2, c), blk(Vt, h2, c), blk(ARKT, h2, c), start=False, stop=True, inc=LASTF(h2, c))
                Yf, YQ, Yc, VR = Z32.v(Z32.ap[:, 0:NT]), Z32.v(Z32.ap[:, NT:2 * NT]), T["QT"], T["DG"]
                act(Yf, psy, AF.Identity)
                yield
                pc = psf()
                mm(pc, CEN32, Yf)
                act(Yc, pc, AF.Identity)
                act(YQ, pc, AF.Square)
                yield
                pq = psf()
                mm(pq, BONV, YQ)
                act(VR, pq, AF.Ln, bias=smc_gn)
                yield
                act(VR, VR, AF.Exp, scale=-0.5)
                tt(Yc, Yc, VR, ALU.mult)
                yield
                tsc(Yf, Yc, smc(c_lnw + hp), ALU.mult, smc(c_lnb + hp), ALU.add)
                yield
                tt(Yf, Yf, O["BON"], ALU.add)
                tt(YT[hp], Yf, O["GG"], ALU.mult)
                yield

            def interleave(ga, gb):
                gens = [g for g in (ga, ga, gb) if g is not None]
                while gens:
                    for g in list(gens):
                        if g not in gens:
                            continue
                        try:
                            next(g)
                        except StopIteration:
                            gens[:] = [x for x in gens if x is not g]

            def rwkv_all(need_out, T, need_r):
                for r in range(KR + 1):
                    gp = rwkv_prep(r, need_out, T, need_r, OSET[r % 2]) if r < KR else None
                    gm = rwkv_mm(r - 1, need_out, T, OSET[(r - 1) % 2]) if r > 0 else None
                    interleave(gm, gp)

            def hgrn_prep(hh, T, O):
                Qf, FG, LF, KG = T["Rf"], T["Kf"], T["LW"], T["KM"]
                slab = ws.get(winqg[hh], KD * 256)
                ps = proj(slab, 0, 128, 256)
                act(Qf, ps, AF.Silu)
                yield
                ps = proj(slab, 128, 128, 256)
                act(O["GS"], ps, AF.Silu)
                yield
                slab = ws.get(winfi[hh], KD * 256)
                ps = proj(slab, 0, 128, 256)
                act(FG, ps, AF.Sigmoid)
                tsc(FG, FG, derc(2 * KR + hh), ALU.mult, derc(KR + hh), ALU.add)
                yield
                act(LF, FG, AF.Ln)
                tsc(KG, FG, -1.0, ALU.mult, 1.0, ALU.add)
                ps = proj(slab, 128, 128, 256)
                act(O["VB"], ps, AF.Identity)
                yield
                G, EG, EGM, EP = T["G"], T["EG"], T["EGM"], T["EP"]
                scan(G, RMASK, LF)
                yield
                act(EG, G, AF.Exp)
                act(EGM, G, AF.Exp, scale=-1.0)
                yield
                cp(O["EGC"], EG.v(c3(EG)[:, :, CH - 1:CH]))
                egc = O["EGC"].v(O["EGC"].ap.to_broadcast([128, NCH, CH]))
                tt(EP.v(c3(EP)), EGM.v(c3(EGM)), egc, ALU.mult)
                tt(O["KMm"], KG, EGM, ALU.mult)
                yield
                tt(O["KP"], KG, EP, ALU.mult)
                tt(O["QTl"], Qf, EG, ALU.mult)
                yield

            def hgrn_mm(hh, T, O):
                QTl, KMm, KP, VB, GS = O["QTl"], O["KMm"], O["KP"], O["VB"], O["GS"]
                KPt, Vt = T["KPth"], T["Vth"]
                for src, dst in ((KP, KPt), (VB, Vt)):
                    pb = psb()
                    for c in range(NCH):
                        tr(pb.v(pb.ap[0:64, c * 128:(c + 1) * 128]), src.v(src.ap[:, c * CH:(c + 1) * CH]), IDB, inc=(c == NCH - 1))
                    cp(dst.v(dst.ap[0:64, :]), pb.v(pb.ap[0:64, :]), "act")
                    yield
                ATm = T["ATm"]
                ps = psf()
                for c in range(NCH):
                    mm(ps.v(ps.ap[0:64, c * CH:(c + 1) * CH]), KMm.v(KMm.ap[:, c * CH:(c + 1) * CH]), QTl.v(QTl.ap[:, c * CH:(c + 1) * CH]), inc=(c == NCH - 1))
                tt(ATm.v(ATm.ap[0:64, :]), ps.v(ps.ap[0:64, :]), MU_I.v(MU_I.ap[0:64, :]), ALU.mult)
                yield
                SALL = T["SALL"]
                cp(SALL.v(SALL.ap[:, 0:128]), SGB[hh], "act")
                for c in range(NCH):
                    ps = psf()
                    mm(ps.v(ps.ap[:, 0:128]), KPt.v(KPt.ap[0:64, c * 128:(c + 1) * 128]), Vt.v(Vt.ap[0:64, c * 128:(c + 1) * 128]))
                    dst = SALL.v(SALL.ap[:, (c + 1) * 128:(c + 2) * 128]) if c < NCH - 1 else SGB[hh]
                    stt(dst, SALL.v(SALL.ap[:, c * 128:(c + 1) * 128]), O["EGC"].v(O["EGC"].ap[:, c, :]), ps.v(ps.ap[:, 0:128]), ALU.mult, ALU.add)
                    yield
                pso = psf()
                for c in range(NCH):
                    oc = pso.v(pso.ap[:, c * CH:(c + 1) * CH])
                    mm(oc, SALL.v(SALL.ap[:, c * 128:(c + 1) * 128]), QTl.v(QTl.ap[:, c * CH:(c + 1) * CH]), start=True, stop=False)
                    mm(oc, Vt.v(Vt.ap[0:64, c * 128:(c + 1) * 128]), ATm.v(ATm.ap[0:64, c * CH:(c + 1) * CH]), start=False, stop=True, inc=(c == NCH - 1))
                Z32 = T["Z32"]
                OQ, T1 = Z32.v(Z32.ap[:, 0:NT]), Z32.v(Z32.ap[:, NT:2 * NT])
                act(OQ, pso, AF.Square)
                yield
                pn = psf()
                mm(pn, ONES32, OQ)
                act(T1, pn, AF.Ln, scale=1.0 / 128, bias=smc_eps)
                yield
                act(T1, T1, AF.Exp, scale=-0.5)
                stt(T1, pso, smc(c_hn + hh), T1, ALU.mult, ALU.mult)
                yield
                tt(YT[KR + hh], T1, GS, ALU.mult)
                yield

            def xslab_prep(t):
                HTcur[0] = HTB[t % 2]
                slab = ws.get(winx[0], KD * 288)
                ps = proj(slab, 0, 64, 288)
                t64 = T["T1"].v(T["T1"].ap[0:64, :])
                shift_lerp(ps, 64, smc(c_mxw, 64), 3 * KR, t64)
                act(TW, t64, AF.Tanh)
                yield
                ps = proj(slab, 64, 64, 288)
                shift_lerp(ps, 64, smc(c_mxa, 64), 3 * KR + 1, t64)
                cp(XA, t64, "act")
                yield
                ps = proj(slab, 128, 128, 288)
                shift_lerp(ps, 128, smc(c_mga), 3 * KR + 2, T["T1"])
                act(SGA, T["T1"], AF.Sigmoid)
                yield
                ps = proj(slab, 256, 32, 288)
                t32 = T["T1"].v(T["T1"].ap[0:32, :])
                shift_lerp(ps, 32, smc(c_mgb, 32), 3 * KR + 3, t32)
                act(SGBb, t32, AF.Sigmoid)
                yield

            def chain_gens(*gs):
                for g in gs:
                    for _ in g:
                        yield

            def phase_b(load_h, finish_tile):
                units = []
                for t in range(NTB):
                    units += [("r", t, h) for h in range(KR)] + [("h", t, h) for h in range(KR)]
                nu = len(units)

                def mk_prep(i):
                    k, t, h = units[i]
                    if k == "r":
                        g = rwkv_prep(h, True, T, True, OSET[i % 2])
                        if h == 0:
                            if t + 1 < NTB:
                                load_h(t + 1)
                            g = chain_gens(xslab_prep(t), g)
                        return g
                    return hgrn_prep(h, T, OHS[i % 2])

                def mk_mm(i):
                    k, t, h = units[i]
                    return rwkv_mm(h, True, T, OSET[i % 2]) if k == "r" else hgrn_mm(h, T, OHS[i % 2])

                for r in range(nu + 1):
                    gp = mk_prep(r) if r < nu else None
                    gm = mk_mm(r - 1) if r > 0 else None
                    interleave(gm, gp)
                    if r > 0 and units[r - 1][0] == "h" and units[r - 1][2] == KR - 1:
                        finish_tile(units[r - 1][1])

            EPSC = B(sb("EPSC", [128, 2])[:], "EPSC")
            memset(EPSC.v(EPSC.ap[:, 0:1]), RMS_EPS)
            memset(EPSC.v(EPSC.ap[:, 1:2]), GN_EPS)
            smc_eps = EPSC.v(EPSC.ap[:, 0:1])
            smc_gn = EPSC.v(EPSC.ap[:, 1:2])

            T = {}
            for n in ("Rf", "Kf", "Vf", "LW", "Af", "KK", "T1", "T2", "KM", "G", "EG", "EGM", "Z32a", "Z32b", "QT", "DG"):
                T[n] = f32t(n)
            ZBOFF = [0]
            for n in ("ALt", "BPt", "KPt", "Vt", "Zba", "Zbb", "AKT", "PMT", "RH", "STA"):
                if n == "Zba":
                    ZBOFF[0] = arena_off[0]
                T[n] = b16t(n)
            for pair, nm in ((("PTa", "PTb"), "KPth"), (("PNa", "PNb"), "Vth"), (("ARBT", "ARKT"), "SALL")):
                o0 = arena_off[0]
                T[pair[0]] = b16t(pair[0])
                T[pair[1]] = b16t(pair[1])
                T[pair[1]] = B(T[pair[1]].ap, pair[0])
                T[nm] = B(ARt[:, o0:o0 + NT].bitcast(BF16), pair[0])
            OSET = []
            for si in range(2):
                Od = {}
                for n in ("AL", "BM", "KMm", "BP", "KP", "VB", "RT", "GG"):
                    Od[n] = b16t("%s_%d" % (n, si))
                Od["BON"] = f32t("BON_%d" % si)
                Od["EGC"] = B(sb("EGC%d" % si, [128, NCH, 1])[:], "EGC%d" % si)
                OSET.append(Od)
            oz = [i for i, n in enumerate(mix_keys) if n == "Z32a"][0]
            T["Z32"] = B(ARt[:, oz * NT:(oz + 2) * NT], "Z32a")
            zb0 = T["Zba"].ap
            T["Zb"] = B(ARt[:, ZBOFF[0]:ZBOFF[0] + NT].bitcast(BF16), "Zba")
            OHS = []
            ohk = []
            for si in range(2):
                Od = {}
                for j, n in enumerate(("QTl", "KMm", "KP", "VB", "GS")):
                    if XW // 2 >= 10 * (NT // 2):
                        o_ = XW // 2 + (si * 5 + j) * (NT // 2)
                        Od[n] = B(XTt[:, o_:o_ + NT // 2].bitcast(BF16), "OH%s_%d" % (n, si))
                    else:
                        Od[n] = B(sb("OH%s_%d" % (n, si), [128, NT], BF16)[:], "OH%s_%d" % (n, si))
                    ohk.append(Od[n].key)
                Od["EGC"] = B(sb("EGCH%d" % si, [128, NCH, 1])[:], "EGCH%d" % si)
                OHS.append(Od)
            T["BE"], T["EGX"], T["EP"] = T["Af"], T["T2"], T["G"]
            T["GG"] = OSET[0]["GG"]
            T["VBh"], T["QTl"], T["KMmh"], T["KPh"], T["ATm"] = OSET[0]["VB"], OSET[0]["RT"], OSET[0]["KMm"], OSET[0]["KP"], T["AKT"]
            at_keys = [a.key for a in AT]

            half = XW // 2
            xk0 = [x.key for x in XT[0:KD // 2]]
            xk1 = [x.key for x in XT[KD // 2:KD]]
            htk = [h.key for h in HT]
            hxk = [h.key for h in HX]
            ytk = [y.key for y in YT]
            groups = [[0, 1], [2, 3], [4, 5], [6, 7]]
            ATW_ = KF * NT // 2
            n_top = min(KD, (ARW - ATW_) // NT)
            assert (KD - n_top) * NT <= YW // 2
            X2 = [B(ARt[:, ATW_ + k * NT:ATW_ + (k + 1) * NT], "X2_%d" % k) for k in range(n_top)] + \
                 [B(YTt[:, :].bitcast(F32)[:, (k - n_top) * NT:(k - n_top + 1) * NT], "X2_%d" % k) for k in range(n_top, KD)]
            x2k = [x.key for x in X2]
            nq = 4 if KD % 4 == 0 else 1
            qw = XW // nq
            def load_x(tj):
                for q in range(nq):
                    P.dma("sp", XTt[:, q * qw:(q + 1) * qw], xmain[tj, :, q * qw:(q + 1) * qw],
                          writes=[x.key for x in XT[q * (KD // nq):(q + 1) * (KD // nq)]])

            load_x(0)
            for ti in range(ntm):
                XSRC[0] = XT
                rmsnorm(c_g1, HT)
                XSRC[0] = X2
                ffn(0, sumsq_chunk, dst=X2)
                if ti + 1 < ntm:
                    load_x(ti + 1)
                P.dma("sp", X1[ti * 128:(ti + 1) * 128, 0:n_top * NT], ARt[:, ATW_:ATW_ + n_top * NT], reads=x2k[0:n_top], writes=["X1a%d" % ti])
                if n_top < KD:
                    P.dma("sp", X1[ti * 128:(ti + 1) * 128, n_top * NT:XW], YTt[:, :].bitcast(F32)[:, 0:(KD - n_top) * NT], reads=x2k[n_top:], writes=["X1b%d" % ti])
                norm_apply(c_gm, AT[0:KD])
                P.dma("sp", HBin[ti][:, :], ATv[:, 0:XW], reads=[a.key for a in AT[0:KD]], writes=["HBin%d" % ti])
                P.collective(HBin[ti], HBall[ti], groups, reads=["HBin%d" % ti], writes=["HBall%d" % ti])
            XSRC[0] = XT
            P.transfer(at_keys + x2k, mix_keys + ytk)
            P.transfer(xk0 + xk1, hxk + ohk)

            def load_h(t):
                buf = HTB[t % 2]
                dst = HTt[:, :] if t % 2 == 0 else XTt[:, 0:half].bitcast(BF16)
                ci, ro = (t, 0) if t < ntm else (t - ntm, 128)
                P.dma("sp", dst, HBall[ci][ro:ro + 128, :], reads=["HBall%d" % ci], writes=[b.key for b in buf])

            if cfg.stage >= 2:
                load_h(0)

                def finish_tile(t):
                    P.dma("sp", YBin[t][:, :], YTt[:, :], reads=ytk, writes=["YBin%d" % t])
                    P.collective(YBin[t], YBall[t], groups, reads=["YBin%d" % t], writes=["YBall%d" % t])

                phase_b(load_h, finish_tile)
            P.transfer(hxk + ohk, xk0 + xk1)
            HTcur[0] = HT
            ATW = KF * NT // 2
            if ARW - ATW >= 3 * (YW // 2):
                yl_aps = [ARt[:, ATW + i * (YW // 2):ATW + (i + 1) * (YW // 2)].bitcast(BF16) for i in range(3)] + [YTt[:, :]]
                yl_pref = True
            else:
                yl_aps = [ARt[:, i * (YW // 2):(i + 1) * (YW // 2)].bitcast(BF16) for i in range(4)]
                yl_pref = False
            YL = [[B(yl_aps[2 * r + h], "YL%d%d" % (r, h)) for h in range(2)] for r in range(2)]
            ylk = [YL[r][h].key for r in range(2) for h in range(2)]
            for ti in range(ntm):
                for q in range(nq):
                    P.dma("sp", XTt[:, q * qw:(q + 1) * qw], X1[ti * 128:(ti + 1) * 128, q * qw:(q + 1) * qw],
                          reads=["X1a%d" % ti, "X1b%d" % ti], writes=[x.key for x in XT[q * (KD // nq):(q + 1) * (KD // nq)]])
                def load_y(tj):
                    for r in range(2):
                        for h in range(2):
                            tq = h * ntm + tj
                            P.dma("sp", YL[r][h].ap, YBall[tq][r * 128:(r + 1) * 128, :], reads=["YBall%d" % tq], writes=[YL[r][h].key])

                if ti == 0:
                    P.transfer(at_keys + mix_keys + ytk, ylk)
                    load_y(0)
                elif not yl_pref:
                    P.transfer(at_keys + mix_keys, ylk)
                    load_y(ti)
                for r in range(2):
                    for j in range(2 * KR):
                        kglob = (r * KR + j) if j < KR else (2 * KR + r * KR + (j - KR))
                        ya = YL[r][0].v(YL[r][0].ap[:, j * NT:(j + 1) * NT])
                        yb = YL[r][1].v(YL[r][1].ap[:, j * NT:(j + 1) * NT])
                        tsc(SQ[0], ya, smc(c_1ms), ALU.mult)
                        stt(HT[kglob], yb, smc(c_s), SQ[0], ALU.mult, ALU.add)
                if yl_pref and ti + 1 < ntm:
                    load_y(ti + 1)
                if cfg.stage >= 5:
                    for sl in range(cfg.NSO):
                        slab = ws.get(wout[sl], KD * 256)
                        for j in range(2):
                            dch = sl * 2 + j
                            po = psf()
                            for k in range(KD):
                                mm(po, slab.v(slab.ap[:, k * 256 + j * 128:k * 256 + (j + 1) * 128]), HT[k], start=(k == 0), stop=(k == KD - 1))
                            if dch > 0:
                                sumsq_chunk(dch - 1)
                            tt(XT[dch], XT[dch], po, ALU.add)
                    sumsq_chunk(KD - 1)
                if not yl_pref:
                    P.transfer(ylk, at_keys)
                norm_apply(c_g2, HT)
                ffn(1, sumsq_chunk)
                def store_q(k, ti=ti):
                    cq = KD // nq
                    if k % cq == cq - 1:
                        q = k // cq
                        out_toks.append(P.dma("sp", outT[ti, :, q * qw:(q + 1) * qw], XTt[:, q * qw:(q + 1) * qw],
                                              reads=[x.key for x in XT[q * cq:(q + 1) * cq]]))

                norm_apply(c_gf, inplace=True, on_done=store_q)
            P.wait_all("sp", out_toks)
            if dry:
                saved_reqs = ws.reqs
                n_dry = P.ninstr
            else:
                print("instructions:", P.ninstr, "weight slabs:", len(ws.reqs), "engine counts:", P.cnt)
    return nc


def _cols(v, n):
    return np.ascontiguousarray(np.asarray(v, np.float32).reshape(n, 128).T)


def _slab(W, c0, width):
    K = W.shape[0]
    blk = W[:, c0:c0 + width].reshape(K // 128, 128, width).transpose(1, 0, 2)
    return np.ascontiguousarray(blk).reshape(128, (K // 128) * width)


def _slabs_cols(W, cols_list):
    out = []
    K = W.shape[0]
    for ranges in cols_list:
        parts = [W[:, c0:c0 + w] for c0, w in ranges]
        Wc = np.concatenate(parts, axis=1)
        out.append(_slab(Wc, 0, Wc.shape[1]))
    return np.stack(out)


def prepare_weights(cfg, inp):
    D, F, KD, KF, C, KR = cfg.D, cfg.F, cfg.KD, cfg.KF, cfg.C, cfg.KR
    CL, KRL = C // 2, KR // 2
    m = {}
    for i, tag in ((1, "ffn1"), (2, "ffn2")):
        Wg = np.asarray(inp[tag + "_w_gate"][0], np.float32)
        Wu = np.asarray(inp[tag + "_w_up"][0], np.float32)
        Wd = np.asarray(inp[tag + "_w_down"][0], np.float32)
        m["wg%d" % i] = np.stack([_slab(Wg, s * 256, 256) for s in range(cfg.NSF)])
        m["wu%d" % i] = np.stack([_slab(Wu, s * 256, 256) for s in range(cfg.NSF)])
        m["wd%d" % i] = np.stack([_slab(Wd, d * 128, 128) for d in range(KD)])
    Win = np.asarray(inp["w_in"][0], np.float32)
    m["winx"] = _slabs_cols(Win, [[(3 * C, 288)]])
    Wo = np.asarray(inp["w_out"][0], np.float32)
    m["wout"] = np.stack([_slab(Wo, s * 256, 256) for s in range(cfg.NSO)])
    R0 = 3 * C + 288
    mu = np.asarray(inp["rwkv_mu"][0], np.float32)
    w2 = np.asarray(inp["rwkv_w2"][0], np.float32)
    a2 = np.asarray(inp["rwkv_a2"][0], np.float32)
    g2 = np.asarray(inp["rwkv_g2"][0], np.float32)
    per = []
    for s in range(2):
        p = {}
        prs = range(s * KRL, (s + 1) * KRL)
        p["winr"] = _slabs_cols(Win, [[(hp * 128, 128)] for hp in prs])
        p["winkv"] = _slabs_cols(Win, [[(C + hp * 128, 128), (2 * C + hp * 128, 128)] for hp in prs])
        p["winqg"] = _slabs_cols(Win, [[(R0 + hh * 128, 128), (R0 + 3 * C + hh * 128, 128)] for hh in prs])
        p["winfi"] = _slabs_cols(Win, [[(R0 + C + hh * 128, 128), (R0 + 2 * C + hh * 128, 128)] for hh in prs])
        ch = slice(s * CL, (s + 1) * CL)

        def lc(v):
            return _cols(np.asarray(v, np.float32).reshape(-1)[ch], KRL)

        cols = [_cols(inp["ffn1_norm"][0], KD), _cols(inp["mix_norm"][0], KD), _cols(inp["ffn2_norm"][0], KD),
                _cols(inp["final_norm"], KD),
                lc(mu[0:C]), lc(mu[C:2 * C]), lc(mu[2 * C:3 * C]),
                lc(inp["rwkv_w0"][0]), lc(inp["rwkv_a0"][0]), lc(inp["rwkv_k_k"][0]),
                lc(inp["rwkv_k_a"][0]), lc(inp["rwkv_r_k"][0]),
                lc(inp["rwkv_ln_w"][0]), lc(inp["rwkv_ln_b"][0]),
                lc(inp["hgrn_lb_logits"][0]), lc(inp["hgrn_lb_logits"][1]), lc(inp["hgrn_norm"][0])]
        extra = np.zeros((128, 6), np.float32)
        extra[0:64, 0] = mu[3 * C:3 * C + 64]
        extra[0:64, 1] = mu[3 * C + 64:3 * C + 128]
        extra[0:128, 2] = mu[3 * C + 128:3 * C + 256]
        extra[0:32, 3] = mu[3 * C + 256:3 * C + 288]
        extra[:, 4] = float(s)
        extra[:, 5] = float(1 - s)
        p["smalls"] = np.ascontiguousarray(np.concatenate(cols + [extra], axis=1))
        p["w2"] = np.ascontiguousarray(w2[:, ch])
        p["a2"] = np.ascontiguousarray(a2[:, ch])
        p["g2"] = np.ascontiguousarray(g2[:, ch])
        per.append(p)
    return m, per


def _tiles(xseg, KD):
    Tn = xseg.shape[0]
    a = xseg.reshape(Tn // NT, NT, KD, 128).transpose(0, 3, 2, 1)
    return np.ascontiguousarray(a).reshape(Tn // NT, 128, KD * NT)


def run(cfg, inp, debug=False):
    x = np.asarray(inp["x"], np.float32)
    Bn, Tn, D = x.shape
    assert Bn * 2 == 8 and Tn // 2 == cfg.ntm * NT
    wm, per = prepare_weights(cfg, inp)
    nc = build_program(cfg, debug=debug)
    in_maps = []
    hl = Tn // 2
    for c in range(8):
        b, s = c // 2, c % 2
        d = dict(wm)
        d.update(per[s])
        d["xmain"] = _tiles(x[b, s * hl:(s + 1) * hl], cfg.KD)
        in_maps.append(d)
    res = run_bass_kernel_spmd(nc, in_maps, core_ids=list(range(8)))
    out = np.empty((Bn, Tn, D), np.float32)
    for c in range(8):
        b, s = c // 2, c % 2
        o = res.results[c]["outT"].reshape(cfg.ntm, 128, cfg.KD, NT).transpose(0, 3, 2, 1).reshape(hl, D)
        out[b, s * hl:(s + 1) * hl] = o
    return out


def kernel(**inputs):
    cfg = Cfg(D=2048, F=5632, ntp=8, ntm=8)
    return run(cfg, inputs)
```
